# Optimizing a Trainium2 kernel written in Bass

```python
import math
import jax, jax.numpy as jnp
from jax import lax
import numpy as np

D_MODEL = 2048
BATCH = 8
SEQ = 2048
DEPTH = 4
DEC_BATCH = 16
DEC_SEQ = 64
PAST_LEN = 4096

CHUNK = 64
MIX_WIDTH = D_MODEL
CONV_CH = MIX_WIDTH // 2
CONV_GROUPS = 8
CONV_DW_WIDTH = 31
HEAD_DIM = 128
GDN_WIDTH = MIX_WIDTH - CONV_CH
GDN_HEADS = GDN_WIDTH // HEAD_DIM
SHORT_CONV = 4
FFN_DIM = 11 * D_MODEL // 4
FFN_CONV = 3
EPS = 1e-6
IN_COLS = 2 * CONV_CH + 4 * GDN_WIDTH + 2 * GDN_HEADS

kernel_name = "hymba_conformer_gdn_convffn_stream"


def rmsnorm(x, g):
    xf = x.astype(jnp.float32)
    y = xf * lax.rsqrt(jnp.mean(xf * xf, axis=-1, keepdims=True) + EPS)
    return (y * g.astype(jnp.float32)).astype(x.dtype)


def l2norm(x):
    return x * lax.rsqrt(jnp.sum(x * x, axis=-1, keepdims=True) + EPS)


def causal_dwconv(x, buf, w, b):
    K = w.shape[0]
    xp = jnp.concatenate([buf.astype(x.dtype), x], axis=1)
    y = lax.conv_general_dilated(xp, w[:, None, :].astype(x.dtype), window_strides=(1,),
                                 padding='VALID', dimension_numbers=('NWC', 'WIO', 'NWC'),
                                 feature_group_count=x.shape[-1])
    if b is not None:
        y = y + b.astype(y.dtype)
    return y, xp[:, -(K - 1):]


def conformer_conv(h_val, h_gate, buf, w_dw, b_dw, gn_g, gn_b):
    u = h_val * jax.nn.sigmoid(h_gate)
    u, new_buf = causal_dwconv(u, buf, w_dw, b_dw)
    B, T, C = u.shape
    uf = u.astype(jnp.float32).reshape(B, T, CONV_GROUPS, C // CONV_GROUPS)
    mu = jnp.mean(uf, axis=-1, keepdims=True)
    var = jnp.mean(jnp.square(uf - mu), axis=-1, keepdims=True)
    un = ((uf - mu) * lax.rsqrt(var + EPS)).reshape(B, T, C) * gn_g.astype(jnp.float32) + gn_b.astype(jnp.float32)
    return jax.nn.silu(un).astype(h_val.dtype), new_buf


def gated_delta_chunked(q, k, v, g, beta, s0):
    L = q.shape[-2]
    G = jnp.cumsum(g, axis=-1)
    incl = jnp.tril(jnp.ones((L, L), dtype=bool))
    strict = jnp.tril(jnp.ones((L, L), dtype=bool), -1)
    decay = jnp.exp(jnp.where(incl, G[..., :, None] - G[..., None, :], -jnp.inf))
    kb = k * beta[..., None]
    m = jnp.where(strict, jnp.einsum('bhnid,bhnjd->bhnij', kb, k) * decay, 0.0)
    eye = jnp.eye(L, dtype=jnp.float32)
    tmat = lax.linalg.triangular_solve(m + eye, jnp.broadcast_to(eye, m.shape), left_side=True,
                                       lower=True, unit_diagonal=True)
    u = jnp.einsum('bhnij,bhnjd->bhnid', tmat, v * beta[..., None])
    w = jnp.einsum('bhnij,bhnjd->bhnid', tmat, kb * jnp.exp(G)[..., None])
    qk = jnp.einsum('bhnid,bhnjd->bhnij', q, k) * decay
    q_dec = q * jnp.exp(G)[..., None]
    k_dec = k * jnp.exp(G[..., -1:] - G)[..., None]
    g_last = jnp.exp(G[..., -1])

    def step(s, xs):
        u_c, w_c, qk_c, qd_c, kd_c, gl_c = xs
        v_new = u_c - jnp.einsum('bhld,bhde->bhle', w_c, s)
        o = jnp.einsum('bhld,bhde->bhle', qd_c, s) + jnp.einsum('bhij,bhje->bhie', qk_c, v_new)
        s = s * gl_c[..., None, None] + jnp.einsum('bhld,bhle->bhde', kd_c, v_new)
        return s, o

    xs = tuple(jnp.moveaxis(t, 2, 0) for t in (u, w, qk, q_dec, k_dec, g_last))
    s_final, o = lax.scan(step, s0, xs)
    return jnp.moveaxis(o, 0, 2), s_final


def gdn_mixer(qkv, z, b_raw, a_raw, buf, s0, w_sc, a_log, dt_bias, onorm_g, chunk):
    qkv, new_buf = causal_dwconv(qkv, buf, w_sc, None)
    qkv = jax.nn.silu(qkv.astype(jnp.float32))
    B, T, _ = qkv.shape
    n = T // chunk
    q, k, v = jnp.split(qkv, 3, axis=-1)

    def heads(t):
        return t.reshape(B, n, chunk, GDN_HEADS, HEAD_DIM).transpose(0, 3, 1, 2, 4)

    def hchunks(t):
        return t.reshape(B, n, chunk, GDN_HEADS).transpose(0, 3, 1, 2)

    q = l2norm(heads(q)) * (HEAD_DIM ** -0.5)
    k = l2norm(heads(k))
    v = heads(v)
    beta = jax.nn.sigmoid(b_raw.astype(jnp.float32))
    g = -jnp.exp(a_log.astype(jnp.float32)) * jax.nn.softplus(a_raw.astype(jnp.float32) + dt_bias.astype(jnp.float32))
    o, s = gated_delta_chunked(q, k, v, hchunks(g), hchunks(beta), s0.astype(jnp.float32))
    o = o.transpose(0, 2, 3, 1, 4).reshape(B, T, GDN_HEADS, HEAD_DIM)
    o = o * lax.rsqrt(jnp.mean(o * o, axis=-1, keepdims=True) + EPS) * onorm_g.astype(jnp.float32)
    o = o.reshape(B, T, GDN_WIDTH) * jax.nn.silu(z.astype(jnp.float32))
    return o.astype(z.dtype), new_buf, s


def trunk_layer(x, conv_buf, qkv_buf, s0, ffn_buf, chunk,
                g_pre_mix, w_in, w_dw, b_dw, gn_g, gn_b, w_sc, a_log, dt_bias, onorm_g,
                w_out, g_post_mix, g_pre_ffn, w_up, w_ffn_dw, b_ffn_dw, w_down, g_post_ffn):
    h = rmsnorm(x, g_pre_mix)
    p = h @ w_in
    c0 = CONV_CH
    c1 = 2 * CONV_CH
    c2 = c1 + 3 * GDN_WIDTH
    c3 = c2 + GDN_WIDTH
    c4 = c3 + GDN_HEADS
    a_out, conv_buf = conformer_conv(p[..., :c0], p[..., c0:c1], conv_buf, w_dw, b_dw, gn_g, gn_b)
    b_out, qkv_buf, s = gdn_mixer(p[..., c1:c2], p[..., c2:c3], p[..., c3:c4], p[..., c4:],
                                  qkv_buf, s0, w_sc, a_log, dt_bias, onorm_g, chunk)
    x = x + rmsnorm(jnp.concatenate([a_out, b_out], axis=-1) @ w_out, g_post_mix)
    h = rmsnorm(x, g_pre_ffn)
    gu, ffn_buf = causal_dwconv(h @ w_up, ffn_buf, w_ffn_dw, b_ffn_dw)
    gate, up = jnp.split(gu, 2, axis=-1)
    x = x + rmsnorm((jax.nn.silu(gate) * up) @ w_down, g_post_ffn)
    return x, conv_buf, qkv_buf, s, ffn_buf


def setup_inputs(seed: int = 0) -> dict:
    key = jax.random.key(seed)
    ks = jax.random.split(key, 32)
    f32 = jnp.float32

    def nrm(k, shape, s):
        return jax.random.normal(k, shape, f32) * s

    L = DEPTH
    a_val = jax.random.uniform(ks[10], (L, GDN_HEADS), f32, 1.0, 16.0)
    dtv = jnp.exp(jax.random.uniform(ks[11], (L, GDN_HEADS), f32, math.log(1e-3), math.log(0.1)))
    dt_bias = dtv + jnp.log(-jnp.expm1(-dtv))
    return {
        "x_prompt": nrm(ks[0], (BATCH, SEQ, D_MODEL), 1.0),
        "x_sample": nrm(ks[1], (DEC_BATCH, DEC_SEQ, D_MODEL), 1.0),
        "state_conv": nrm(ks[2], (L, DEC_BATCH, CONV_DW_WIDTH - 1, CONV_CH), 0.5),
        "state_qkv_conv": nrm(ks[3], (L, DEC_BATCH, SHORT_CONV - 1, 3 * GDN_WIDTH), 1.0),
        "state_gdn": nrm(ks[4], (L, DEC_BATCH, GDN_HEADS, HEAD_DIM, HEAD_DIM), 0.1),
        "state_ffn_conv": nrm(ks[5], (L, DEC_BATCH, FFN_CONV - 1, 2 * FFN_DIM), 1.0),
        "g_pre_mix": 1.0 + nrm(ks[6], (L, D_MODEL), 0.05),
        "w_in": nrm(ks[7], (L, D_MODEL, IN_COLS), D_MODEL ** -0.5),
        "w_dw": nrm(ks[8], (L, CONV_DW_WIDTH, CONV_CH), CONV_DW_WIDTH ** -0.5),
        "b_dw": nrm(ks[9], (L, CONV_CH), 0.02),
        "gn_g": 1.0 + nrm(ks[12], (L, CONV_CH), 0.05),
        "gn_b": nrm(ks[13], (L, CONV_CH), 0.02),
        "w_sc": nrm(ks[14], (L, SHORT_CONV, 3 * GDN_WIDTH), SHORT_CONV ** -0.5),
        "a_log": jnp.log(a_val),
        "dt_bias": dt_bias,
        "onorm_g": 1.0 + nrm(ks[15], (L, HEAD_DIM), 0.05),
        "w_out": nrm(ks[16], (L, MIX_WIDTH, D_MODEL), MIX_WIDTH ** -0.5),
        "g_post_mix": 1.0 + nrm(ks[17], (L, D_MODEL), 0.05),
        "g_pre_ffn": 1.0 + nrm(ks[18], (L, D_MODEL), 0.05),
        "w_up": nrm(ks[19], (L, D_MODEL, 2 * FFN_DIM), D_MODEL ** -0.5),
        "w_ffn_dw": nrm(ks[20], (L, FFN_CONV, 2 * FFN_DIM), FFN_CONV ** -0.5),
        "b_ffn_dw": nrm(ks[21], (L, 2 * FFN_DIM), 0.02),
        "w_down": nrm(ks[22], (L, FFN_DIM, D_MODEL), FFN_DIM ** -0.5),
        "g_post_ffn": 1.0 + nrm(ks[23], (L, D_MODEL), 0.05),
    }


def reference(x_prompt, x_sample, state_conv, state_qkv_conv, state_gdn, state_ffn_conv,
              g_pre_mix, w_in, w_dw, b_dw, gn_g, gn_b, w_sc, a_log, dt_bias, onorm_g,
              w_out, g_post_mix, g_pre_ffn, w_up, w_ffn_dw, b_ffn_dw, w_down, g_post_ffn):
    xp = x_prompt
    xs = x_sample
    dt = x_prompt.dtype
    cp, qp, sp, fp = [], [], [], []
    cs, qs, ss, fs = [], [], [], []
    for l in range(DEPTH):
        params = (g_pre_mix[l], w_in[l], w_dw[l], b_dw[l], gn_g[l], gn_b[l], w_sc[l], a_log[l],
                  dt_bias[l], onorm_g[l], w_out[l], g_post_mix[l], g_pre_ffn[l], w_up[l],
                  w_ffn_dw[l], b_ffn_dw[l], w_down[l], g_post_ffn[l])
        xp, c_b, q_b, s_b, f_b = trunk_layer(
            xp,
            jnp.zeros((xp.shape[0], CONV_DW_WIDTH - 1, CONV_CH), dt),
            jnp.zeros((xp.shape[0], SHORT_CONV - 1, 3 * GDN_WIDTH), dt),
            jnp.zeros((xp.shape[0], GDN_HEADS, HEAD_DIM, HEAD_DIM), jnp.float32),
            jnp.zeros((xp.shape[0], FFN_CONV - 1, 2 * FFN_DIM), dt),
            CHUNK, *params)
        cp.append(c_b); qp.append(q_b); sp.append(s_b); fp.append(f_b)
        xs, c_b, q_b, s_b, f_b = trunk_layer(
            xs, state_conv[l], state_qkv_conv[l], state_gdn[l], state_ffn_conv[l],
            xs.shape[1], *params)
        cs.append(c_b); qs.append(q_b); ss.append(s_b); fs.append(f_b)
    return (xp, xs, jnp.stack(cp), jnp.stack(qp), jnp.stack(sp), jnp.stack(fp),
            jnp.stack(cs), jnp.stack(qs), jnp.stack(ss), jnp.stack(fs))
```

```python
import numpy as np
import concourse.bass as bass
import concourse.mybir as mybir
from concourse.bass_utils import run_bass_kernel_spmd

F32 = mybir.dt.float32
BF16 = mybir.dt.bfloat16
ALU = mybir.AluOpType
AF = mybir.ActivationFunctionType
EPS = 1e-6
ENGS = ["pe", "act", "dve", "pool", "sp"]
KW = 31
SCW = 4
FCW = 3
CK = 64


class Cfg:
    def __init__(s, D=2048, CC=8, H=8, FFN=5632, L=4, SEQ=2048, DSEQ=64, PT=(512, 512, 512, 512), NSLOT=3, MERGE=False):
        s.MERGE = MERGE
        s.D, s.CC, s.H, s.FFN, s.L, s.SEQ, s.DSEQ, s.PT, s.NSLOT = D, CC, H, FFN, L, SEQ, DSEQ, tuple(PT), NSLOT
        assert sum(PT) == SEQ and all(p % 64 == 0 for p in PT)
        s.KC = D // 128
        s.FCH = FFN // 128
        s.FH = s.FCH // 2
        s.KM = CC + H
        s.NIN = 2 * CC * 128 + 4 * H * 128 + 2 * H
        s.TW = max(max(PT), (PT[-1] if MERGE else 0) + 2 * DSEQ)
        assert s.TW <= 512
        s.HC = H * CK
        s.HD = H * 128
        o = 0
        s.P = {}
        for name, n in [("g1", s.KC), ("g2", s.KC), ("g3", s.KC), ("g4", s.KC),
                        ("wdw", CC * KW), ("bdw", CC), ("gng", CC), ("gnb", CC),
                        ("wsc", 3 * H * SCW), ("on", 1),
                        ("wf", 2 * s.FCH * FCW), ("bf", 2 * s.FCH),
                        ("alog", H), ("dtb", H)]:
            s.P[name] = (o, n)
            o += n
        s.NPL = o
        s.SLOTW = max(s.KC * 256, s.KM * 256, s.FH * 128)
        s.NLIN = CC + 3 * H // 2 + H // 2
        s.NLOUT = s.KC // 2
        s.NLUP = s.FCH
        s.NLDN = 2 * s.KC


class Sched:
    def __init__(self):
        self.items = {e: [] for e in ENGS}
        self.cnt = {}
        self.known = {e: {} for e in ENGS}
        self.res = {}
        self.region_base = {}
        self.epoch = 0
        self.n_ops = 0

    def new_epoch(self):
        self.epoch += 1

    def _get(self, name):
        r = self.res.get(name)
        if r is None:
            base = {}
            if isinstance(name, tuple) and name[0] in self.region_base:
                base = dict(self.region_base[name[0]])
            r = {"w": None, "r": base}
            self.res[name] = r
        return r

    def fence(self, region):
        base = dict(self.region_base.get(region, {}))
        for name in list(self.res):
            if isinstance(name, tuple) and name[0] == region:
                r = self.res.pop(name)
                if r["w"] is not None:
                    k, v = r["w"]
                    base[k] = max(base.get(k, 0), v)
                for k, v in r["r"].items():
                    base[k] = max(base.get(k, 0), v)
        self.region_base[region] = base

    def op(self, eng, fn, R=(), W=(), dma=None):
        deps = {}

        def add(k, v):
            if v > deps.get(k, 0):
                deps[k] = v

        for n in R:
            r = self._get(n)
            if r["w"] is not None:
                add(*r["w"])
        for n in W:
            r = self._get(n)
            if r["w"] is not None:
                add(*r["w"])
            for k, v in r["r"].items():
                add(k, v)
        waits = []
        kn = self.known[eng]
        for k, v in deps.items():
            if dma is None and eng == "pe" and k[0] == "pe":
                continue
            if kn.get(k, 0) >= v:
                continue
            kn[k] = v
            waits.append((k, v))
        if dma is None:
            key = (eng, self.epoch)
            self.cnt[key] = self.cnt.get(key, 0) + 1
            tok = (key, self.cnt[key])
        else:
            key = ("dma", dma)
            self.cnt[key] = self.cnt.get(key, 0) + 16
            tok = (key, self.cnt[key])
        self.items[eng].append((waits, fn, key, dma is not None))
        for n in R:
            r = self._get(n)
            k, v = tok
            if v > r["r"].get(k, 0):
                r["r"][k] = v
        for n in W:
            r = self._get(n)
            r["w"] = tok
            r["r"] = {}
        self.n_ops += 1
        return tok

    def wait_all(self, eng, keys):
        waits = []
        for k in keys:
            v = self.cnt.get(k, 0)
            if v:
                waits.append((k, v))
        self.items[eng].append((waits, None, None, False))

    def emit(self, nc):
        sems = {}
        for k in self.cnt:
            sems[k] = nc.alloc_semaphore("s_%s_%s" % (k[0], str(k[1])))
        blk_engs = {"pe": "tensor", "act": "scalar", "dve": "vector", "pool": "gpsimd", "sp": "sync"}
        with nc.Block() as block:
            for e in ENGS:
                items = self.items[e]

                def body(engine, items=items):
                    for waits, fn, key, is_dma in items:
                        for k, v in waits:
                            engine.wait_ge(sems[k], v)
                        if fn is None:
                            continue
                        ins = fn(engine)
                        ins.then_inc(sems[key], 16 if is_dma else 1)

                getattr(block, blk_engs[e])(body)


class Seg:
    def __init__(s, seq, col0, n, first, last, tok0, sidx):
        s.seq, s.col0, s.n, s.first, s.last, s.tok0, s.sidx = seq, col0, n, first, last, tok0, sidx


def build(cfg):
    c = cfg
    nc = bass.Bass("TRN2", target_bir_lowering=False)
    S = Sched()
    D, KC, CC, H, L, TW, FCH, FH, KM, HC, HD = c.D, c.KC, c.CC, c.H, c.L, c.TW, c.FCH, c.FH, c.KM, c.HC, c.HD
    NPT = len(c.PT)

    def din(name, shape):
        return nc.dram_tensor(name, list(shape), F32, kind="ExternalInput").ap()

    def dout(name, shape):
        return nc.dram_tensor(name, list(shape), F32, kind="ExternalOutput").ap()

    xp_d = din("xp", [128, KC, c.SEQ])
    xs_d = din("xs", [128, KC, 2 * c.DSEQ])
    sc_d = din("st_conv", [L, 2, 128, CC * 30])
    sq_d = din("st_qkv", [L, 2, 128, 3 * H * 3])
    sg_d = din("st_gdn", [L, 2, 128, H * 128])
    sf_d = din("st_ffn", [L, 2, 128, 2 * FCH * 2])
    prm_d = din("prm", [L, 128, c.NPL])
    cst_d = din("cst", [128, 708])
    win_d = din("win", [L, c.NLIN, 128, KC * 256])
    wba_d = din("wba", [L, 128, KC * 2 * H])
    wout_d = din("wout", [L, c.NLOUT, 128, KM * 256])
    wup_d = din("wup", [L, c.NLUP, 128, KC * 256])
    wdn_d = din("wdn", [L, c.NLDN, 128, FH * 128])

    def dscr(name, shape):
        return nc.dram_tensor(name, list(shape), BF16, kind="Internal").ap()

    scr = {"win": dscr("win_s", [L, c.NLIN, 128, KC * 256]), "wout": dscr("wout_s", [L, c.NLOUT, 128, KM * 256]),
           "wup": dscr("wup_s", [L, c.NLUP, 128, KC * 256]), "wdn": dscr("wdn_s", [L, c.NLDN, 128, FH * 128])}
    wsrc = {"win": win_d, "wout": wout_d, "wup": wup_d, "wdn": wdn_d}
    yp_d = dout("yp", [128, KC, c.SEQ])
    ys_d = dout("ys", [128, KC, 2 * c.DSEQ])
    oc_d = dout("o_conv", [L, 3, 128, CC * 30])
    oq_d = dout("o_qkv", [L, 3, 128, 3 * H * 3])
    og_d = dout("o_gdn", [L, 3, 128, H * 128])
    of_d = dout("o_ffn", [L, 3, 128, 2 * FCH * 2])

    import contextlib
    es = contextlib.ExitStack()

    def sb(name, shape, dt=F32):
        return es.enter_context(nc.sbuf_tensor("sb_" + name, list(shape), dt))

    RA_N = max(10 * HC + 4 * HD, KC * TW, FH * TW // 2 + 2 * (6 + TW) + 4 * TW,
               (90 + TW) + (10 + TW) + (KW + SCW) * 128 + 8 * TW) + 64
    RB_N = max((3 * H + H) * TW // 2, KC * TW)
    x_t = sb("x", [128, KC, TW])
    xh_t = sb("xh", [128, KC, TW], BF16)
    mA_t = sb("mA", [128, CC, TW], BF16)
    RA = sb("RA", [128, RA_N])
    RB = sb("RB", [128, RB_N])
    slots = [sb("slot%d" % i, [128, c.SLOTW], BF16) for i in range(c.NSLOT)]
    wba_t = sb("wba", [128, KC * 2 * H], BF16)
    prm_t = sb("prm", [128, c.NPL])
    cst_t = sb("cst", [128, 708])
    idb_t = sb("idb", [128, 128], BF16)
    stc_P = sb("stc_P", [128, L, CC, 30])
    stq_P = sb("stq_P", [128, L, 3 * H, 3])
    stf_P = sb("stf_P", [128, L, 2 * FCH, 2])
    S_P = sb("S_P", [128, L, H, 128])
    stc_S = sb("stc_S", [128, 2, CC, 30])
    stq_S = sb("stq_S", [128, 2, 3 * H, 3])
    stf_S = sb("stf_S", [128, 2, 2 * FCH, 2])
    S_S = sb("S_S", [128, H, 128])
    Sbf = sb("Sbf", [128, H, 128], BF16)
    rstd_t = sb("rstd", [128, TW])
    sqn_t = sb("sqn", [128, 2, TW])
    NCH = TW // CK
    l1_t = sb("l1", [64, 8, NCH, H])
    l1c_t = sb("l1c", [128, 8, H])
    ps = es.enter_context(nc.psum_tensor("ps", [128, 8 * 512], F32))

    def bank(b, npart=128, n=512, off=0):
        return ps[0:npart, b * 512 + off: b * 512 + off + n]

    identF = cst_t[:, 0:128]
    ones1 = cst_t[:, 128:256]
    onesD = cst_t[:, 256:384]
    ones128 = cst_t[:, 384:512]
    Umask = cst_t[0:64, 512:576]
    negmask = cst_t[0:64, 576:640]
    strict01 = cst_t[0:64, 640:704]
    ident64 = cst_t[0:64, 0:64]

    def P_(name, a=None, b=None):
        o, n = c.P[name]
        if a is None:
            return prm_t[:, o:o + n]
        return prm_t[:, o + a:o + (b if b is not None else a + 1)]

    def m_chunk(j):
        if j < CC:
            return mA_t[:, j, :]
        return xh_t[:, KC - H + (j - CC), :]

    def RAv(off, n, dt=F32):
        if dt == F32:
            return RA[:, off:off + n]
        return RA[:, off:off + n].bitcast(BF16)

    def RBv(off, n, dt=F32):
        if dt == F32:
            return RB[:, off:off + n]
        return RB[:, off:off + n].bitcast(BF16)

    qkv_v = RBv(0, 3 * H * TW // 2, BF16).rearrange("p (c t) -> p c t", t=TW)
    zs_v = RBv(3 * H * TW // 2, H * TW // 2, BF16).rearrange("p (c t) -> p c t", t=TW)
    y2_v = RBv(0, KC * TW).rearrange("p (c t) -> p c t", t=TW)
    y_v = RAv(0, KC * TW).rearrange("p (c t) -> p c t", t=TW)
    a_v = RAv(0, FH * TW // 2, BF16).rearrange("p (c t) -> p c t", t=TW)
    o = 0
    ubuf_v = RAv(o, 90 + TW, BF16).rearrange("p (a t) -> p a t", a=2); o += 90 + TW
    cbuf_v = RAv(o, 10 + TW, BF16).rearrange("p (a t) -> p a t", a=2); o += 10 + TW
    dg31_v = RAv(o, KW * 128, BF16).rearrange("p (a j c) -> p a j c", a=2, j=KW); o += KW * 128
    dg4_v = RAv(o, SCW * 128, BF16).rearrange("p (a j c) -> p a j c", a=2, j=SCW); o += SCW * 128
    acc_v = RAv(o, 2 * TW).rearrange("p (a t) -> p a t", a=2); o += 2 * TW
    sg_v = RAv(o, 2 * TW).rearrange("p (a t) -> p a t", a=2); o += 2 * TW
    mu_v = RAv(o, TW); o += TW
    var_v = RAv(o, TW); o += TW
    cen_v = RAv(o, TW); o += TW
    sqA_v = RAv(o, TW); o += TW
    assert o <= RA_N, (o, RA_N)
    o = FH * TW // 2
    fbuf_v = RAv(o, 2 * (6 + TW)).rearrange("p (a t) -> p a t", a=2); o += 2 * (6 + TW)
    accf_v = RAv(o, 2 * TW).rearrange("p (a t) -> p a t", a=2); o += 2 * TW
    sgate_v = RAv(o, 2 * TW).rearrange("p (a t) -> p a t", a=2); o += 2 * TW

    def gslot(i, npart=128):
        return RA[0:npart, i * HC:(i + 1) * HC].rearrange("p (h t) -> p h t", h=H)

    def gslot_bf(i):
        return RA[:, i * HC:(i + 1) * HC].bitcast(BF16).rearrange("p (a h t) -> p a h t", a=2, h=H)

    def gbig(i):
        o_ = 10 * HC + i * HD
        return RA[0:64, o_:o_ + HD // 2].bitcast(BF16).rearrange("p (h d) -> p h d", h=H)

    def gslot_h(i, npart=128):
        return RA[0:npart, i * HC:i * HC + HC // 2].bitcast(BF16).rearrange("p (h t) -> p h t", h=H)

    tiles = []
    tok = 0
    for t, pw in enumerate(c.PT):
        segs = [Seg("P", 0, pw, t == 0, t == NPT - 1, tok, 0)]
        wt = pw
        if t == NPT - 1 and c.MERGE:
            segs.append(Seg("A", pw, c.DSEQ, True, True, 0, 1))
            segs.append(Seg("B", pw + c.DSEQ, c.DSEQ, True, True, 0, 2))
            wt = pw + 2 * c.DSEQ
        tiles.append(dict(w=wt, segs=segs))
        tok += pw
    if not c.MERGE:
        tiles.append(dict(w=2 * c.DSEQ, segs=[Seg("A", 0, c.DSEQ, True, True, 0, 1),
                                              Seg("B", c.DSEQ, c.DSEQ, True, True, 0, 2)]))

    QKV = ("RB", "qkv")
    ZS = ("RB", "zs")
    ring = {"i": 0}

    cur = {"ti": 0}

    def wload(kind, l, idx, ncols):
        i = ring["i"] % c.NSLOT
        ring["i"] += 1
        slot = slots[i]
        if cur["ti"] == 0:
            src_ap = wsrc[kind][l, idx]
            S.op("pool", lambda e, slot=slot, src_ap=src_ap, ncols=ncols:
                 e.dma_start(out=slot[:, 0:ncols], in_=src_ap),
                 W=[("slot", i)], dma="w%d" % i)
            dst_ap = scr[kind][l, idx]
            S.op("sp", lambda e, slot=slot, dst_ap=dst_ap, ncols=ncols:
                 e.dma_start(out=dst_ap, in_=slot[:, 0:ncols]),
                 R=[("slot", i)], W=[("scr", kind, l, idx)], dma="ws%d" % i)
        else:
            src_ap = scr[kind][l, idx]
            S.op("sp", lambda e, slot=slot, src_ap=src_ap, ncols=ncols:
                 e.dma_start(out=slot[:, 0:ncols], in_=src_ap),
                 R=[("scr", kind, l, idx)], W=[("slot", i)], dma="w%d" % i)
        return i, slot

    mmrot = {"i": 0}

    def next_mm_bank():
        b = 1 + (mmrot["i"] % 3)
        mmrot["i"] += 1
        return b

    def mm_group(b, slot_i, slot, nk, ncols_per_k, col_off, rhs_fn, w, rhs_res, start=True, stop=True, m=128):
        def fn(e):
            ins = None
            for k in range(nk):
                ins = e.matmul(bank(b, m, w), lhsT=slot[:, k * ncols_per_k + col_off:k * ncols_per_k + col_off + m],
                               rhs=rhs_fn(k), start=(start and k == 0), stop=(stop and k == nk - 1))
            return ins
        S.op("pe", fn, R=[("slot", slot_i)] + list(rhs_res), W=[("bank", b)])

    def st_conv(seg, l):
        return (stc_P[:, l], ("stc", "P", l)) if seg.seq == "P" else (stc_S[:, seg.sidx - 1], ("stc", seg.seq))

    def st_qkv(seg, l):
        return (stq_P[:, l], ("stq", "P", l)) if seg.seq == "P" else (stq_S[:, seg.sidx - 1], ("stq", seg.seq))

    def st_ffn(seg, l):
        return (stf_P[:, l], ("stf", "P", l)) if seg.seq == "P" else (stf_S[:, seg.sidx - 1], ("stf", seg.seq))

    def st_S(seg, l):
        return (S_P[:, l], ("S", "P", l)) if seg.seq == "P" else (S_S[:, :, :], ("S", "S"))

    def rsqrt_eps(out_ap, in_ap, R, W):
        npart = out_ap.shape[0]
        S.op("act", lambda e: e.activation(out=out_ap, in_=in_ap, func=AF.Ln, bias=cst_t[0:npart, 704:705]), R=list(R) + ["cst"], W=W)
        S.op("act", lambda e: e.activation(out=out_ap, in_=out_ap, func=AF.Exp, scale=-0.5), R=[], W=W)

    S.op("sp", lambda e: e.dma_start(out=cst_t[:, :], in_=cst_d), W=["cst"], dma="cst")
    S.op("dve", lambda e: e.tensor_copy(out=idb_t[:, :], in_=identF), R=["cst"], W=["idb"])
    S.op("dve", lambda e: e.memset(stc_P[:, :, :, :].rearrange("p a b c -> p (a b c)"), 0.0), W=[("stc", "P", l) for l in range(L)])
    S.op("dve", lambda e: e.memset(stq_P[:, :, :, :].rearrange("p a b c -> p (a b c)"), 0.0), W=[("stq", "P", l) for l in range(L)])
    S.op("dve", lambda e: e.memset(stf_P[:, :, :, :].rearrange("p a b c -> p (a b c)"), 0.0), W=[("stf", "P", l) for l in range(L)])
    S.op("dve", lambda e: e.memset(S_P[:, :, :, :].rearrange("p a b c -> p (a b c)"), 0.0), W=[("S", "P", l) for l in range(L)])

    def rmsnorm_to_xh(w, gname):
        for k in range(KC):
            par = k % 2
            S.op("act", lambda e, k=k, par=par: e.activation(out=sqn_t[:, par, 0:w], in_=x_t[:, k, 0:w], func=AF.Square),
                 R=[("x", k)], W=[("sqn", par)])
            S.op("pe", lambda e, k=k, par=par: e.matmul(bank(0, 128, w), lhsT=onesD, rhs=sqn_t[:, par, 0:w],
                                                          start=(k == 0), stop=(k == KC - 1)),
                 R=[("sqn", par), "cst"], W=[("bank", 0)])
        rsqrt_eps(rstd_t[:, 0:w], bank(0, 128, w), [("bank", 0)], ["rstd"])
        for k in range(KC):
            S.op("dve", lambda e, k=k: e.scalar_tensor_tensor(out=xh_t[:, k, 0:w], in0=x_t[:, k, 0:w],
                                                                scalar=P_(gname, k), in1=rstd_t[:, 0:w],
                                                                op0=ALU.mult, op1=ALU.mult),
                 R=[("x", k), "rstd", "prm"], W=["xh"])

    def residual_epilogue(w, yv, yres, gname):
        rsqrt_eps(rstd_t[:, 0:w], bank(0, 128, w), [("bank", 0)], ["rstd"])
        for k in range(KC):
            S.op("dve", lambda e, k=k: e.scalar_tensor_tensor(out=yv[:, k, 0:w], in0=yv[:, k, 0:w],
                                                                scalar=P_(gname, k), in1=rstd_t[:, 0:w],
                                                                op0=ALU.mult, op1=ALU.mult),
                 R=["rstd", "prm"], W=[(yres, k)])
            S.op("dve", lambda e, k=k: e.tensor_tensor(out=x_t[:, k, 0:w], in0=x_t[:, k, 0:w], in1=yv[:, k, 0:w],
                                                         op=ALU.add),
                 R=[(yres, k)], W=[("x", k)])

    def out_block_epilogue(b, w, yv, yres, n, first, last):
        par = n % 2
        S.op("act", lambda e: e.activation(out=yv[:, n, 0:w], in_=bank(b, 128, w), func=AF.Copy),
             R=[("bank", b)], W=[(yres, n)])
        S.op("act", lambda e: e.activation(out=sqn_t[:, par, 0:w], in_=bank(b, 128, w), func=AF.Square),
             R=[("bank", b)], W=[("sqn", par)])
        return lambda: S.op("pe", lambda e: e.matmul(bank(0, 128, w), lhsT=onesD, rhs=sqn_t[:, par, 0:w], start=first, stop=last),
                            R=[("sqn", par), "cst"], W=[("bank", 0)])

    def gdn_chunk(l, seg, ci, c0, w):
        Sv, Sres = st_S(seg, l)
        G = ("RA",)

        def g(n):
            return ("RA", n)

        qv = qkv_v[:, 0:H, c0:c0 + CK]
        kv = qkv_v[:, H:2 * H, c0:c0 + CK]
        vv = qkv_v[:, 2 * H:3 * H, c0:c0 + CK]
        s0, s2, s3, s4 = [gslot(i) for i in (0, 2, 3, 4)]
        s5 = gslot_h(5)
        s4h = gslot_h(4)
        S.op("act", lambda e: e.activation(out=Sbf[:, :, :], in_=Sv, func=AF.Copy), R=[Sres], W=["Sbf"])
        qn = qv
        kn = kv
        kbg, kd, vb, vnew = gbig(0), gbig(1), gbig(2), gbig(3)
        bkf = lambda b, npart=128: bank(b, npart, HC).rearrange("p (h t) -> p h t", h=H)
        gL = l1_t[:, 1, ci, :]
        bL = l1_t[:, 2, ci, :]
        G_sb = l1c_t[0:64, 0, :]
        eG = l1c_t[0:64, 1, :]
        bEG = l1c_t[0:64, 2, :]
        kdsc = l1c_t[0:64, 3, :]
        gl128 = l1c_t[:, 4, :]
        dGl = l1c_t[0:64, 5, :]

        b1bf = bank(1, 64, HD // 2).bitcast(BF16).rearrange("p (h d) -> p h d", h=H)
        b2bf = bank(2, 64, HD // 2).bitcast(BF16).rearrange("p (h d) -> p h d", h=H)

        def tr_fn(dst, src):
            def fn(e):
                ins = None
                for h in range(H):
                    ins = e.transpose(dst[:, h, :], src[:, h, :], idb_t[:, :])
                return ins
            return fn
        S.op("pe", tr_fn(b1bf, kn), R=[("RB", "qn", ci), "idb"], W=[("bank", 1)])
        S.op("pe", tr_fn(b2bf, vv), R=[QKV, "idb"], W=[("bank", 2)])
        b7 = bank(7, 128, 2 * H)

        def l1mm(e):
            e.matmul(b7[0:64, 0:H], lhsT=Umask, rhs=gL, start=True, stop=True)
            return e.matmul(b7[:, H:2 * H], lhsT=ones1[0:64, :], rhs=gL, start=True, stop=True)
        S.op("pe", l1mm, R=["l1", "cst"], W=[("bank", 7)])
        S.op("act", lambda e: e.activation(out=G_sb, in_=b7[0:64, 0:H], func=AF.Copy), R=[("bank", 7)], W=["l1c0"])
        S.op("act", lambda e: e.activation(out=eG, in_=b7[0:64, 0:H], func=AF.Exp), R=[("bank", 7)], W=["l1c1"])
        S.op("act", lambda e: e.activation(out=gl128, in_=b7[:, H:2 * H], func=AF.Exp), R=[("bank", 7)], W=["l1c4"])
        S.op("dve", lambda e: e.tensor_tensor(out=dGl, in0=b7[0:64, H:2 * H], in1=G_sb, op=ALU.subtract),
             R=[("bank", 7), "l1c0"], W=["l1c5"])
        S.op("act", lambda e: e.activation(out=kdsc, in_=dGl, func=AF.Exp), R=["l1c5"], W=["l1c3"])
        S.op("dve", lambda e: e.tensor_tensor(out=bEG, in0=bL, in1=eG, op=ALU.mult), R=["l1", "l1c1"], W=["l1c2"])
        s2_64, s3_64 = gslot(2, 64), gslot(3, 64)
        S.op("dve", lambda e: e.tensor_tensor(out=s2_64, in0=Umask.unsqueeze(1).to_broadcast([64, H, CK]),
                                              in1=gL.unsqueeze(2).to_broadcast([64, H, CK]), op=ALU.mult),
             R=["l1", "cst"], W=[g("s2")])
        S.op("pe", lambda e: e.matmul(bank(5, 128, HC), lhsT=ones1[0:64, :], rhs=s2_64.rearrange("p h t -> p (h t)"),
                                      start=True, stop=True), R=[g("s2"), "cst"], W=[("bank", 5)])
        S.op("dve", lambda e: e.tensor_tensor(out=s3_64, in0=ident64.unsqueeze(1).to_broadcast([64, H, CK]),
                                              in1=bL.unsqueeze(2).to_broadcast([64, H, CK]), op=ALU.mult),
             R=["l1", "cst"], W=[g("s3")])
        S.op("pe", lambda e: e.matmul(bank(6, 64, HC), lhsT=ones1[0:64, 0:64], rhs=s3_64.rearrange("p h t -> p (h t)"),
                                      start=True, stop=True), R=[g("s3"), "cst"], W=[("bank", 6)])
        S.op("dve", lambda e: e.tensor_tensor(out=s2_64, in0=bkf(5, 64), in1=G_sb.unsqueeze(2).to_broadcast([64, H, CK]),
                                              op=ALU.subtract), R=[("bank", 5), "l1c0"], W=[g("s2")])
        S.op("dve", lambda e: e.tensor_tensor(out=s2_64, in0=s2_64, in1=negmask.unsqueeze(1).to_broadcast([64, H, CK]),
                                              op=ALU.add), R=["cst"], W=[g("s2")])
        S.op("act", lambda e: e.activation(out=s3_64, in_=s2_64, func=AF.Exp), R=[g("s2")], W=[g("s3")])
        S.op("act", lambda e: e.activation(out=s4, in_=bkf(5), func=AF.Exp), R=[("bank", 5)], W=[g("s4")])
        S.op("dve", lambda e: e.tensor_tensor(out=s5, in0=qn, in1=s4, op=ALU.mult), R=[("RB", "qn", ci), g("s4")], W=[g("s5")])
        def kkfn(e):
            ins = None
            for h in range(H):
                ins = e.matmul(bank(3, 64, CK, h * CK), lhsT=kn[:, h, :], rhs=kn[:, h, :], start=True, stop=True)
            return ins

        def qkfn(e):
            ins = None
            for h in range(H):
                ins = e.matmul(bank(4, 64, CK, h * CK), lhsT=kn[:, h, :], rhs=qn[:, h, :], start=True, stop=True)
            return ins
        S.op("pe", kkfn, R=[("RB", "qn", ci)], W=[("bank", 3)])
        S.op("pe", qkfn, R=[("RB", "qn", ci)], W=[("bank", 4)])
        s6_64, s7_64, s8_64, s9_64 = gslot_h(6, 64), gslot_h(7, 64), gslot_h(8, 64), gslot_h(9, 64)
        b6bf = bank(6, 64, HC // 2).bitcast(BF16).rearrange("p (h t) -> p h t", h=H)
        S.op("dve", lambda e: e.tensor_tensor(out=s6_64, in0=bkf(4, 64), in1=s3_64, op=ALU.mult),
             R=[("bank", 4), g("s3")], W=[g("s6")])
        S.op("dve", lambda e: e.tensor_tensor(out=s2_64, in0=s3_64, in1=strict01.unsqueeze(1).to_broadcast([64, H, CK]),
                                              op=ALU.mult), R=[g("s3"), "cst"], W=[g("s2")])
        S.op("dve", lambda e: e.tensor_tensor(out=s2_64, in0=s2_64, in1=bkf(6, 64), op=ALU.mult),
             R=[("bank", 6)], W=[g("s2")])
        S.op("dve", lambda e: e.scalar_tensor_tensor(out=s7_64, in0=bkf(3, 64), scalar=-1.0, in1=s2_64,
                                                     op0=ALU.mult, op1=ALU.mult),
             R=[("bank", 3), g("s2")], W=[g("s7")])
        def ptfn(e):
            ins = None
            for h in range(H):
                ins = e.transpose(b6bf[:, h, :], s7_64[:, h, :], idb_t[0:64, 0:64])
            return ins
        S.op("pe", ptfn, R=[g("s7"), "idb"], W=[("bank", 6)])
        S.op("act", lambda e: e.activation(out=s8_64, in_=b6bf, func=AF.Copy), R=[("bank", 6)], W=[g("s8")])
        S.op("dve", lambda e: e.tensor_tensor(out=s9_64, in0=s7_64, in1=ident64.unsqueeze(1).to_broadcast([64, H, CK]),
                                              op=ALU.add), R=[g("s7"), "cst"], W=[g("s9")])
        for kk in range(1, 6):
            def sqfn(e, kk=kk):
                ins = None
                for h in range(H):
                    if kk < 5:
                        e.matmul(bank(3, 64, CK, h * CK), lhsT=s8_64[:, h, :], rhs=s7_64[:, h, :], start=True, stop=True)
                    ins = e.matmul(bank(4, 64, CK, h * CK), lhsT=s7_64[:, h, :], rhs=s8_64[:, h, :], start=True, stop=True)
                return ins
            S.op("pe", sqfn, R=[g("s7"), g("s8")], W=[("bank", 3), ("bank", 4)])
            if kk < 5:
                S.op("act", lambda e: e.activation(out=s7_64, in_=bkf(3, 64), func=AF.Copy), R=[("bank", 3)], W=[g("s7")])
            S.op("dve", lambda e: e.tensor_copy(out=s8_64, in_=bkf(4, 64)), R=[("bank", 4)], W=[g("s8")])

            def xfn(e):
                ins = None
                for h in range(H):
                    ins = e.matmul(bank(5, 64, CK, h * CK), lhsT=s8_64[:, h, :], rhs=s9_64[:, h, :], start=True, stop=True)
                return ins
            S.op("pe", xfn, R=[g("s8"), g("s9")], W=[("bank", 5)])
            S.op("dve", lambda e: e.tensor_tensor(out=s9_64, in0=s9_64, in1=bkf(5, 64), op=ALU.add),
                 R=[("bank", 5)], W=[g("s9")])
        S.op("dve", lambda e: e.tensor_tensor(out=kbg, in0=b1bf, in1=bEG.unsqueeze(2).to_broadcast([64, H, 128]), op=ALU.mult),
             R=[("bank", 1), "l1c2"], W=[g("kbg")])
        S.op("dve", lambda e: e.tensor_tensor(out=kd, in0=b1bf, in1=kdsc.unsqueeze(2).to_broadcast([64, H, 128]), op=ALU.mult),
             R=[("bank", 1), "l1c3"], W=[g("kd")])
        S.op("dve", lambda e: e.tensor_tensor(out=vb, in0=b2bf, in1=bL.unsqueeze(2).to_broadcast([64, H, 128]), op=ALU.mult),
             R=[("bank", 2), "l1"], W=[g("vb")])
        def wtfn(e):
            ins = None
            for h in range(H):
                ins = e.matmul(bank(1, 128, CK, h * CK), lhsT=kbg[:, h, :], rhs=s9_64[:, h, :], start=True, stop=True)
            return ins
        S.op("pe", wtfn, R=[g("kbg"), g("s9")], W=[("bank", 1)])
        S.op("act", lambda e: e.activation(out=s4h, in_=bkf(1), func=AF.Copy, scale=-1.0), R=[("bank", 1)], W=[g("s4")])
        vps = ps[0:64, 6 * 512:6 * 512 + HD].rearrange("p (h d) -> p h d", h=H)
        sps = ps[:, 6 * 512:6 * 512 + HD].rearrange("p (h d) -> p h d", h=H)

        def vnfn(e):
            ins = None
            for h in range(H):
                e.matmul(vps[:, h, :], lhsT=s9_64[:, h, :], rhs=vb[:, h, :], start=True, stop=False)
                ins = e.matmul(vps[:, h, :], lhsT=s4h[:, h, :], rhs=Sbf[:, h, :], start=False, stop=True)
            return ins
        S.op("pe", vnfn, R=[g("s9"), g("vb"), g("s4"), "Sbf"], W=[("bank", 6), ("bank", 7)])
        S.op("act", lambda e: e.activation(out=vnew, in_=vps, func=AF.Copy), R=[("bank", 6), ("bank", 7)], W=[g("vnew")])

        def otfn(e):
            ins = None
            for h in range(H):
                e.matmul(bank(2, 128, CK, h * CK), lhsT=Sbf[:, h, :], rhs=s5[:, h, :], start=True, stop=False)
                ins = e.matmul(bank(2, 128, CK, h * CK), lhsT=vnew[:, h, :], rhs=s6_64[:, h, :], start=False, stop=True)
            return ins
        S.op("pe", otfn, R=["Sbf", g("s5"), g("vnew"), g("s6")], W=[("bank", 2)])

        def supfn(e):
            ins = None
            for h in range(H):
                ins = e.matmul(sps[:, h, :], lhsT=kd[:, h, :], rhs=vnew[:, h, :], start=True, stop=True)
            return ins
        S.op("pe", supfn, R=[g("kd"), g("vnew")], W=[("bank", 6), ("bank", 7)])
        S.op("dve", lambda e: e.tensor_tensor(out=Sv, in0=Sv, in1=gl128.unsqueeze(2).to_broadcast([128, H, 128]), op=ALU.mult),
             R=["l1c4"], W=[Sres])
        S.op("dve", lambda e: e.tensor_tensor(out=Sv, in0=Sv, in1=sps, op=ALU.add),
             R=[("bank", 6), ("bank", 7)], W=[Sres])
        S.op("act", lambda e: e.activation(out=s3, in_=bkf(2), func=AF.Copy), R=[("bank", 2)], W=[g("s3")])
        S.op("act", lambda e: e.activation(out=s0, in_=bkf(2), func=AF.Square), R=[("bank", 2)], W=[g("s0")])
        S.op("pe", lambda e: e.matmul(bank(0, 128, HC), lhsT=ones128, rhs=s0.rearrange("p h t -> p (h t)"),
                                      start=True, stop=True), R=[g("s0"), "cst"], W=[("bank", 0)])
        rsqrt_eps(s2, bkf(0), [("bank", 0)], [g("s2")])
        S.op("dve", lambda e: e.tensor_tensor(out=s3, in0=s3, in1=s2, op=ALU.mult), R=[g("s2")], W=[g("s3")])
        mg = xh_t[:, KC - H:KC, c0:c0 + CK]
        S.op("dve", lambda e: e.scalar_tensor_tensor(out=mg, in0=s3, scalar=P_("on", 0), in1=zs_v[:, :, c0:c0 + CK],
                                                     op0=ALU.mult, op1=ALU.mult),
             R=[g("s3"), "prm", ZS], W=["xh"])

    def l2norm_tile(w):
        for ci in range(w // CK):
            c0 = ci * CK
            pr = ci % 2
            sA = gslot(0) if pr == 0 else gslot(3)
            sB = gslot(2) if pr == 0 else gslot(4)
            bk = 0 if pr == 0 else 3
            nA, nB = (("RA", "s0"), ("RA", "s2")) if pr == 0 else (("RA", "s3"), ("RA", "s4"))
            for which, scl in ((0, 128.0 ** -0.5), (1, 1.0)):
                src = qkv_v[:, which * H:(which + 1) * H, c0:c0 + CK]
                S.op("act", lambda e, src=src, sA=sA: e.activation(out=sA, in_=src, func=AF.Square), R=[QKV], W=[nA])
                S.op("pe", lambda e, sA=sA, bk=bk: e.matmul(bank(bk, 128, HC), lhsT=ones1, rhs=sA.rearrange("p h t -> p (h t)"),
                                                           start=True, stop=True), R=[nA, "cst"], W=[("bank", bk)])
                rsqrt_eps(sB, bank(bk, 128, HC).rearrange("p (h t) -> p h t", h=H), [("bank", bk)], [nB])
                S.op("dve", lambda e, src=src, sB=sB, scl=scl: e.scalar_tensor_tensor(
                    out=src, in0=src, scalar=scl, in1=sB, op0=ALU.mult, op1=ALU.mult), R=[nB], W=[("RB", "qn", ci)])

    def layer(l, tile):
        w = tile["w"]
        segs = tile["segs"]
        grp = (len(segs) == 2 and all(sg_.seq != "P" for sg_ in segs) and segs[0].n == segs[1].n
               and segs[1].col0 == segs[0].col0 + segs[0].n)
        gn = segs[0].n
        gc0 = segs[0].col0
        S.new_epoch()
        S.op("sp", lambda e: e.dma_start(out=prm_t[:, :], in_=prm_d[l]), W=["prm"], dma="prm")
        S.op("pool", lambda e: e.dma_start(out=wba_t[:, :], in_=wba_d[l]), W=["wba"], dma="wba")
        for seg in segs:
            if seg.seq != "P":
                b = seg.sidx - 1
                S.op("sp", lambda e, b=b: e.dma_start(out=stc_S[:, b].rearrange("p c j -> p (c j)"), in_=sc_d[l, b]),
                     W=[("stc", seg.seq)], dma="stc" + seg.seq)
                S.op("sp", lambda e, b=b: e.dma_start(out=stq_S[:, b].rearrange("p c j -> p (c j)"), in_=sq_d[l, b]),
                     W=[("stq", seg.seq)], dma="stq" + seg.seq)
                S.op("sp", lambda e, b=b: e.dma_start(out=stf_S[:, b].rearrange("p c j -> p (c j)"), in_=sf_d[l, b]),
                     W=[("stf", seg.seq)], dma="stf" + seg.seq)
        S.fence("RA")
        S.fence("RB")
        rmsnorm_to_xh(w, "g1")
        pend = {"cc": None, "p2": None}

        def conv_ln(cc, par):
            av = acc_v[:, par, 0:w]
            S.op("act", lambda e, av=av: e.activation(out=sqA_v[:, 0:w], in_=av, func=AF.Square),
                 R=[("RA", "acc", par)], W=[("RA", "sqA")])
            S.op("pe", lambda e, av=av: e.matmul(bank(0, 128, w), lhsT=ones128, rhs=av, start=True, stop=True),
                 R=[("RA", "acc", par), "cst"], W=[("bank", 0)])
            S.op("pe", lambda e: e.matmul(bank(4, 128, w), lhsT=ones128, rhs=sqA_v[:, 0:w], start=True, stop=True),
                 R=[("RA", "sqA"), "cst"], W=[("bank", 4)])
            S.op("act", lambda e: e.activation(out=mu_v[:, 0:w], in_=bank(0, 128, w), func=AF.Copy),
                 R=[("bank", 0)], W=[("RA", "mu")])
            S.op("dve", lambda e: e.tensor_tensor(out=var_v[:, 0:w], in0=mu_v[:, 0:w], in1=mu_v[:, 0:w], op=ALU.mult),
                 R=[("RA", "mu")], W=[("RA", "var")])
            S.op("dve", lambda e: e.tensor_tensor(out=var_v[:, 0:w], in0=bank(4, 128, w), in1=var_v[:, 0:w], op=ALU.subtract),
                 R=[("bank", 4)], W=[("RA", "var")])
            rsqrt_eps(var_v[:, 0:w], var_v[:, 0:w], [], [("RA", "var")])
            S.op("dve", lambda e, av=av: e.tensor_tensor(out=cen_v[:, 0:w], in0=av, in1=mu_v[:, 0:w], op=ALU.subtract),
                 R=[("RA", "acc", par), ("RA", "mu")], W=[("RA", "cen")])
            S.op("dve", lambda e: e.tensor_tensor(out=cen_v[:, 0:w], in0=cen_v[:, 0:w], in1=var_v[:, 0:w], op=ALU.mult),
                 R=[("RA", "var")], W=[("RA", "cen")])
            S.op("act", lambda e, cc=cc: e.activation(out=mA_t[:, cc, 0:w], in_=cen_v[:, 0:w], func=AF.Silu,
                                                      bias=P_("gnb", cc), scale=P_("gng", cc)),
                 R=[("RA", "cen"), "prm"], W=["mA"])

        li = 0
        for pair in range(CC // 2):
            si, slot = wload("win", l, li, KC * 256); li += 1
            for j in range(2):
                b = next_mm_bank()
                mm_group(b, si, slot, KC, 256, j * 128, lambda k: xh_t[:, k, 0:w], w, ["xh"])
                S.op("act", lambda e, b=b, j=j: e.activation(out=sg_v[:, j, 0:w], in_=bank(b, 128, w), func=AF.Sigmoid),
                     R=[("bank", b)], W=[("RA", "sg", j)])
            si, slot = wload("win", l, li, KC * 256); li += 1
            for j in range(2):
                cc = pair * 2 + j
                b = next_mm_bank()
                mm_group(b, si, slot, KC, 256, j * 128, lambda k: xh_t[:, k, 0:w], w, ["xh"])
                par = cc % 2
                cb_ = 6 + par
                S.op("dve", lambda e, cc=cc, par=par: e.tensor_tensor(
                    out=dg31_v[:, par], in0=idb_t[:, :].unsqueeze(1).to_broadcast([128, KW, 128]),
                    in1=P_("wdw", cc * KW, cc * KW + KW).unsqueeze(2).to_broadcast([128, KW, 128]), op=ALU.mult),
                    R=["idb", "prm"], W=[("RA", "dg31", par)])
                if grp:
                    UB = ubuf_v[:, par, 0:2 * (30 + gn)].rearrange("p (s t) -> p s t", t=30 + gn)
                    STc = stc_S[:, :, cc, :]
                    BKc = bank(b, 128, 2 * gn, gc0).rearrange("p (s t) -> p s t", t=gn)
                    SGc = sg_v[:, j, gc0:gc0 + 2 * gn].rearrange("p (s t) -> p s t", t=gn)
                    sres = [("stc", "A"), ("stc", "B")]
                    S.op("act", lambda e, UB=UB, STc=STc: e.activation(out=UB[:, :, 0:30], in_=STc, func=AF.Copy),
                         R=sres, W=[("RA", "ubuf", par)])
                    S.op("dve", lambda e, UB=UB, BKc=BKc, SGc=SGc: e.tensor_tensor(out=UB[:, :, 30:30 + gn], in0=BKc, in1=SGc, op=ALU.mult),
                         R=[("bank", b), ("RA", "sg", j)], W=[("RA", "ubuf", par)])
                    S.op("dve", lambda e, STc=STc, BKc=BKc, SGc=SGc: e.tensor_tensor(
                        out=STc, in0=BKc[:, :, gn - 30:gn], in1=SGc[:, :, gn - 30:gn], op=ALU.mult),
                        R=[("bank", b), ("RA", "sg", j), ("RA", "ubuf", par)], W=sres)
                    cfns = []
                    off = 0
                    for seg in segs:
                        n = seg.n
                        ub = ubuf_v[:, par, off:off + 30 + n]

                        def convfn(e, ub=ub, par=par, cb_=cb_, seg=seg, n=n):
                            ins = None
                            for tp in range(KW):
                                ins = e.matmul(bank(cb_, 128, n, seg.col0), lhsT=dg31_v[:, par, tp, :], rhs=ub[:, tp:tp + n],
                                               start=(tp == 0), stop=(tp == KW - 1))
                            return ins
                        cfns.append(convfn)
                        off += 30 + n
                else:
                    off = 0
                    cfns = []
                    for seg in segs:
                        stv, stres = st_conv(seg, l)
                        n = seg.n
                        assert n >= 30
                        ub = ubuf_v[:, par, off:off + 30 + n]
                        S.op("act", lambda e, ub=ub, stv=stv, cc=cc: e.activation(out=ub[:, 0:30], in_=stv[:, cc, :], func=AF.Copy),
                             R=[stres], W=[("RA", "ubuf", par)])
                        S.op("dve", lambda e, ub=ub, b=b, j=j, seg=seg, n=n: e.tensor_tensor(
                            out=ub[:, 30:30 + n], in0=bank(b, 128, n, seg.col0), in1=sg_v[:, j, seg.col0:seg.col0 + n], op=ALU.mult),
                            R=[("bank", b), ("RA", "sg", j)], W=[("RA", "ubuf", par)])
                        S.op("dve", lambda e, stv=stv, cc=cc, b=b, j=j, seg=seg, n=n: e.tensor_tensor(
                            out=stv[:, cc, :], in0=bank(b, 128, 30, seg.col0 + n - 30),
                            in1=sg_v[:, j, seg.col0 + n - 30:seg.col0 + n], op=ALU.mult),
                            R=[("bank", b), ("RA", "sg", j), ("RA", "ubuf", par)], W=[stres])

                        def convfn(e, ub=ub, par=par, cb_=cb_, seg=seg, n=n):
                            ins = None
                            for tp in range(KW):
                                ins = e.matmul(bank(cb_, 128, n, seg.col0), lhsT=dg31_v[:, par, tp, :], rhs=ub[:, tp:tp + n],
                                               start=(tp == 0), stop=(tp == KW - 1))
                            return ins
                        cfns.append(convfn)
                        off += 30 + n

                def part2(cfns=cfns, par=par, cb_=cb_, cc=cc):
                    for f in cfns:
                        S.op("pe", f, R=[("RA", "ubuf", par), ("RA", "dg31", par)], W=[("bank", cb_)])
                    S.op("act", lambda e: e.activation(out=acc_v[:, par, 0:w], in_=bank(cb_, 128, w),
                                                       func=AF.Identity, bias=P_("bdw", cc)),
                         R=[("bank", cb_), "prm"], W=[("RA", "acc", par)])
                if pend["p2"] is not None:
                    pend["p2"][0]()
                    if pend["cc"] is not None:
                        conv_ln(*pend["cc"])
                    pend["cc"] = pend["p2"][1]
                pend["p2"] = (part2, (cc, par))
        pend["p2"][0]()
        if pend["cc"] is not None:
            conv_ln(*pend["cc"])
        conv_ln(*pend["p2"][1])
        pq = {"f": None}
        for ld in range(3 * H // 2):
            si, slot = wload("win", l, li, KC * 256); li += 1
            for j in range(2):
                i = ld * 2 + j
                b = next_mm_bank()
                mm_group(b, si, slot, KC, 256, j * 128, lambda k: xh_t[:, k, 0:w], w, ["xh"])
                par = i % 2
                cb_ = 6 + par
                S.op("dve", lambda e, i=i, par=par: e.tensor_tensor(
                    out=dg4_v[:, par], in0=idb_t[:, :].unsqueeze(1).to_broadcast([128, SCW, 128]),
                    in1=P_("wsc", i * SCW, i * SCW + SCW).unsqueeze(2).to_broadcast([128, SCW, 128]), op=ALU.mult),
                    R=["idb", "prm"], W=[("RA", "dg4", par)])
                if grp:
                    CBg = cbuf_v[:, par, 0:2 * (3 + gn)].rearrange("p (s t) -> p s t", t=3 + gn)
                    STq = stq_S[:, :, i, :]
                    BKq = bank(b, 128, 2 * gn, gc0).rearrange("p (s t) -> p s t", t=gn)
                    sres = [("stq", "A"), ("stq", "B")]
                    S.op("act", lambda e, CBg=CBg, STq=STq: e.activation(out=CBg[:, :, 0:3], in_=STq, func=AF.Copy),
                         R=sres, W=[("RA", "cbuf", par)])
                    S.op("act", lambda e, CBg=CBg, BKq=BKq: e.activation(out=CBg[:, :, 3:3 + gn], in_=BKq, func=AF.Copy),
                         R=[("bank", b)], W=[("RA", "cbuf", par)])
                    S.op("act", lambda e, STq=STq, BKq=BKq: e.activation(out=STq, in_=BKq[:, :, gn - 3:gn], func=AF.Copy),
                         R=[("bank", b), ("RA", "cbuf", par)], W=sres)
                    c4s = []
                    off = 0
                    for seg in segs:
                        n = seg.n
                        cb = cbuf_v[:, par, off:off + 3 + n]

                        def c4fn(e, cb=cb, par=par, cb_=cb_, seg=seg, n=n):
                            ins = None
                            for tp in range(SCW):
                                ins = e.matmul(bank(cb_, 128, n, seg.col0), lhsT=dg4_v[:, par, tp, :], rhs=cb[:, tp:tp + n],
                                               start=(tp == 0), stop=(tp == SCW - 1))
                            return ins
                        c4s.append(c4fn)
                        off += 3 + n
                else:
                    off = 0
                    c4s = []
                    for seg in segs:
                        stv, stres = st_qkv(seg, l)
                        n = seg.n
                        cb = cbuf_v[:, par, off:off + 3 + n]
                        S.op("act", lambda e, cb=cb, stv=stv, i=i: e.activation(out=cb[:, 0:3], in_=stv[:, i, :], func=AF.Copy),
                             R=[stres], W=[("RA", "cbuf", par)])
                        S.op("act", lambda e, cb=cb, b=b, seg=seg, n=n: e.activation(out=cb[:, 3:3 + n], in_=bank(b, 128, n, seg.col0),
                                                                                 func=AF.Copy),
                             R=[("bank", b)], W=[("RA", "cbuf", par)])
                        S.op("act", lambda e, stv=stv, i=i, b=b, seg=seg, n=n: e.activation(
                            out=stv[:, i, :], in_=bank(b, 128, 3, seg.col0 + n - 3), func=AF.Copy),
                            R=[("bank", b), ("RA", "cbuf", par)], W=[stres])

                        def c4fn(e, cb=cb, par=par, cb_=cb_, seg=seg, n=n):
                            ins = None
                            for tp in range(SCW):
                                ins = e.matmul(bank(cb_, 128, n, seg.col0), lhsT=dg4_v[:, par, tp, :], rhs=cb[:, tp:tp + n],
                                               start=(tp == 0), stop=(tp == SCW - 1))
                            return ins
                        c4s.append(c4fn)
                        off += 3 + n

                def q2(c4s=c4s, par=par, cb_=cb_, i=i):
                    for f in c4s:
                        S.op("pe", f, R=[("RA", "cbuf", par), ("RA", "dg4", par)], W=[("bank", cb_)])
                    S.op("act", lambda e: e.activation(out=qkv_v[:, i, 0:w], in_=bank(cb_, 128, w), func=AF.Silu),
                         R=[("bank", cb_)], W=[QKV])
                if pq["f"] is not None:
                    pq["f"]()
                pq["f"] = q2
        pq["f"]()
        for ld in range(H // 2):
            si, slot = wload("win", l, li, KC * 256); li += 1
            for j in range(2):
                h = ld * 2 + j
                b = next_mm_bank()
                mm_group(b, si, slot, KC, 256, j * 128, lambda k: xh_t[:, k, 0:w], w, ["xh"])
                S.op("act", lambda e, b=b, h=h: e.activation(out=zs_v[:, h, 0:w], in_=bank(b, 128, w), func=AF.Silu),
                     R=[("bank", b)], W=[ZS])
        nch = w // CK
        b5 = bank(5, 64, nch * 2 * H).rearrange("p (c h) -> p c h", c=nch)

        def bafn(e):
            ins = None
            for ci in range(nch):
                for k in range(KC):
                    ins = e.matmul(b5[:, ci, :], lhsT=xh_t[:, k, ci * CK:(ci + 1) * CK], rhs=wba_t[:, k * 2 * H:(k + 1) * 2 * H],
                                   start=(k == 0), stop=(k == KC - 1))
            return ins
        S.op("pe", bafn, R=["xh", "wba"], W=[("bank", 5)])
        L1 = lambda i: l1_t[:, i, 0:nch, :]
        S.op("act", lambda e: e.activation(out=L1(2), in_=b5[:, :, 0:H], func=AF.Sigmoid), R=[("bank", 5)], W=["l1"])
        dtb = P_("dtb")[0:64, :].unsqueeze(1).to_broadcast([64, nch, H])
        alg = P_("alog")[0:64, :]
        S.op("dve", lambda e: e.tensor_tensor(out=L1(0), in0=b5[:, :, H:2 * H], in1=dtb, op=ALU.add),
             R=[("bank", 5), "prm"], W=["l1"])
        S.op("act", lambda e: e.activation(out=L1(3), in_=L1(0), func=AF.Abs), R=[], W=["l1"])
        S.op("act", lambda e: e.activation(out=L1(3), in_=L1(3), func=AF.Exp, scale=-1.0), R=[], W=["l1"])
        S.op("act", lambda e: e.activation(out=L1(3), in_=L1(3), func=AF.Ln, bias=cst_t[0:64, 705:706]), R=[], W=["l1"])
        S.op("dve", lambda e: e.scalar_tensor_tensor(out=L1(0), in0=L1(0), scalar=0.0, in1=L1(3), op0=ALU.max, op1=ALU.add),
             R=[], W=["l1"])
        S.op("act", lambda e: e.activation(out=l1_t[:, 4, 0, :], in_=alg, func=AF.Exp), R=["prm"], W=["l1"])
        S.op("dve", lambda e: e.scalar_tensor_tensor(out=L1(1), in0=L1(0), scalar=-1.0,
                                                     in1=l1_t[:, 4, 0, :].unsqueeze(1).to_broadcast([64, nch, H]),
                                                     op0=ALU.mult, op1=ALU.mult), R=[], W=["l1"])
        S.fence("RA")
        l2norm_tile(w)
        for seg in segs:
            if seg.seq != "P":
                b = seg.sidx - 1
                S.op("sp", lambda e, b=b: e.dma_start(out=S_S[:, :, :].rearrange("p h d -> p (h d)"), in_=sg_d[l, b]),
                     W=[("S", "S")], dma="SS")
            for cj in range(seg.n // CK):
                c0 = seg.col0 + cj * CK
                gdn_chunk(l, seg, c0 // CK, c0, w)
            if seg.seq != "P":
                S.op("sp", lambda e, seg=seg: e.dma_start(out=og_d[l, seg.sidx], in_=S_S[:, :, :].rearrange("p h d -> p (h d)")),
                     R=[("S", "S")], dma="out_SS")
            elif seg.last:
                S.op("sp", lambda e: e.dma_start(out=og_d[l, 0], in_=S_P[:, l].rearrange("p h d -> p (h d)")),
                     R=[("S", "P", l)], dma="out")
        for seg in segs:
            if seg.last:
                stv, stres = st_conv(seg, l)
                S.op("sp", lambda e, stv=stv, seg=seg: e.dma_start(out=oc_d[l, seg.sidx], in_=stv.rearrange("p c j -> p (c j)")),
                     R=[stres], dma="out_stc" + seg.seq)
                stv, stres = st_qkv(seg, l)
                S.op("sp", lambda e, stv=stv, seg=seg: e.dma_start(out=oq_d[l, seg.sidx], in_=stv.rearrange("p c j -> p (c j)")),
                     R=[stres], dma="out_stq" + seg.seq)
        S.fence("RA")
        pend2 = {"f": None}
        for ld in range(c.NLOUT):
            si, slot = wload("wout", l, ld, KM * 256)
            for j in range(2):
                n = ld * 2 + j
                b = next_mm_bank()
                mm_group(b, si, slot, KM, 256, j * 128, lambda k: m_chunk(k)[:, 0:w], w, ["mA", "xh"])
                nxt = out_block_epilogue(b, w, y_v, ("RA", "y"), n, n == 0, n == KC - 1)
                if pend2["f"] is not None:
                    pend2["f"]()
                pend2["f"] = nxt
        pend2["f"]()
        pend2["f"] = None
        residual_epilogue(w, y_v, ("RA", "y"), "g2")
        rmsnorm_to_xh(w, "g3")
        S.fence("RA")
        S.fence("RB")
        for half in range(2):
            for pr in range(FH // 2):
                for which in range(2):
                    ld = (half * (FH // 2) + pr) * 2 + which
                    si, slot = wload("wup", l, ld, KC * 256)
                    for j in range(2):
                        hc = half * FH + pr * 2 + j
                        ch = hc + which * FCH
                        b = next_mm_bank()
                        mm_group(b, si, slot, KC, 256, j * 128, lambda k: xh_t[:, k, 0:w], w, ["xh"])
                        par = j
                        if grp:
                            Fg = fbuf_v[:, par, 0:2 * (2 + gn)].rearrange("p (s t) -> p s t", t=2 + gn)
                            STf = stf_S[:, :, ch, :]
                            BKf = bank(b, 128, 2 * gn, gc0).rearrange("p (s t) -> p s t", t=gn)
                            AVg = accf_v[:, par, gc0:gc0 + 2 * gn].rearrange("p (s t) -> p s t", t=gn)
                            sres = [("stf", "A"), ("stf", "B")]
                            S.op("act", lambda e, Fg=Fg, STf=STf: e.activation(out=Fg[:, :, 0:2], in_=STf, func=AF.Copy),
                                 R=sres, W=[("RA", "fbuf", par)])
                            S.op("act", lambda e, Fg=Fg, BKf=BKf: e.activation(out=Fg[:, :, 2:2 + gn], in_=BKf, func=AF.Copy),
                                 R=[("bank", b)], W=[("RA", "fbuf", par)])
                            S.op("dve", lambda e, Fg=Fg, AVg=AVg, ch=ch: e.tensor_scalar(
                                out=AVg, in0=Fg[:, :, 0:gn], scalar1=P_("wf", ch * FCW), scalar2=P_("bf", ch),
                                op0=ALU.mult, op1=ALU.add), R=[("RA", "fbuf", par), "prm"], W=[("RA", "accf", par)])
                            for tp in range(1, FCW):
                                S.op("dve", lambda e, Fg=Fg, AVg=AVg, ch=ch, tp=tp: e.scalar_tensor_tensor(
                                    out=AVg, in0=Fg[:, :, tp:tp + gn], scalar=P_("wf", ch * FCW + tp), in1=AVg,
                                    op0=ALU.mult, op1=ALU.add), R=[("RA", "fbuf", par), "prm"], W=[("RA", "accf", par)])
                            S.op("act", lambda e, Fg=Fg, STf=STf: e.activation(out=STf, in_=Fg[:, :, gn:gn + 2], func=AF.Copy),
                                 R=[("RA", "fbuf", par)], W=sres)
                        else:
                            off = 0
                            for seg in segs:
                                stv, stres = st_ffn(seg, l)
                                n = seg.n
                                fb = fbuf_v[:, par, off:off + 2 + n]
                                S.op("act", lambda e, fb=fb, stv=stv, ch=ch: e.activation(out=fb[:, 0:2], in_=stv[:, ch, :], func=AF.Copy),
                                     R=[stres], W=[("RA", "fbuf", par)])
                                S.op("act", lambda e, fb=fb, b=b, seg=seg, n=n: e.activation(out=fb[:, 2:2 + n], in_=bank(b, 128, n, seg.col0),
                                                                                         func=AF.Copy),
                                     R=[("bank", b)], W=[("RA", "fbuf", par)])
                                av = accf_v[:, par, seg.col0:seg.col0 + n]
                                S.op("dve", lambda e, fb=fb, av=av, ch=ch, n=n: e.tensor_scalar(
                                    out=av, in0=fb[:, 0:n], scalar1=P_("wf", ch * FCW), scalar2=P_("bf", ch),
                                    op0=ALU.mult, op1=ALU.add), R=[("RA", "fbuf", par), "prm"], W=[("RA", "accf", par)])
                                for tp in range(1, FCW):
                                    S.op("dve", lambda e, fb=fb, av=av, ch=ch, n=n, tp=tp: e.scalar_tensor_tensor(
                                        out=av, in0=fb[:, tp:tp + n], scalar=P_("wf", ch * FCW + tp), in1=av,
                                        op0=ALU.mult, op1=ALU.add), R=[("RA", "fbuf", par), "prm"], W=[("RA", "accf", par)])
                                S.op("act", lambda e, fb=fb, stv=stv, ch=ch, n=n: e.activation(out=stv[:, ch, :], in_=fb[:, n:n + 2], func=AF.Copy),
                                     R=[("RA", "fbuf", par)], W=[stres])
                                off += 2 + n
                        if which == 0:
                            S.op("act", lambda e, j=j, par=par: e.activation(out=sgate_v[:, j, 0:w], in_=accf_v[:, par, 0:w], func=AF.Silu),
                                 R=[("RA", "accf", par)], W=[("RA", "sgate", j)])
                        else:
                            S.op("dve", lambda e, j=j, par=par, ai=hc - half * FH: e.tensor_tensor(
                                out=a_v[:, ai, 0:w], in0=accf_v[:, par, 0:w], in1=sgate_v[:, j, 0:w], op=ALU.mult),
                                R=[("RA", "accf", par), ("RA", "sgate", j)], W=[("RA", "a")])
            for n in range(KC):
                si, slot = wload("wdn", l, half * KC + n, FH * 128)
                b = next_mm_bank()
                mm_group(b, si, slot, FH, 128, 0, lambda k: a_v[:, k, 0:w], w, [("RA", "a")])
                if half == 0:
                    S.op("act", lambda e, b=b, n=n: e.activation(out=y2_v[:, n, 0:w], in_=bank(b, 128, w), func=AF.Copy),
                         R=[("bank", b)], W=[("RB", "y2", n)])
                else:
                    S.op("dve", lambda e, b=b, n=n: e.tensor_tensor(out=y2_v[:, n, 0:w], in0=y2_v[:, n, 0:w], in1=bank(b, 128, w),
                                                                  op=ALU.add), R=[("bank", b)], W=[("RB", "y2", n)])
                    par = n % 2
                    S.op("act", lambda e, n=n, par=par: e.activation(out=sqn_t[:, par, 0:w], in_=y2_v[:, n, 0:w], func=AF.Square),
                         R=[("RB", "y2", n)], W=[("sqn", par)])
                    nxt = (lambda n=n, par=par: S.op("pe", lambda e: e.matmul(bank(0, 128, w), lhsT=onesD, rhs=sqn_t[:, par, 0:w],
                                                                              start=(n == 0), stop=(n == KC - 1)),
                                                     R=[("sqn", par), "cst"], W=[("bank", 0)]))
                    if pend2["f"] is not None:
                        pend2["f"]()
                    pend2["f"] = nxt
        pend2["f"]()
        pend2["f"] = None
        residual_epilogue(w, y2_v, ("RB", "y2"), "g4")
        for seg in segs:
            if seg.last:
                stv, stres = st_ffn(seg, l)
                S.op("sp", lambda e, stv=stv, seg=seg: e.dma_start(out=of_d[l, seg.sidx], in_=stv.rearrange("p c j -> p (c j)")),
                     R=[stres], dma="out_stf" + seg.seq)

    for ti, tile in enumerate(tiles):
        cur["ti"] = ti
        w = tile["w"]
        seg0 = tile["segs"][0]
        xres = [("x", k) for k in range(KC)]
        has_p = seg0.seq == "P"
        if has_p:
            S.op("sp", lambda e, seg0=seg0: e.dma_start(out=x_t[:, :, 0:seg0.n], in_=xp_d[:, :, seg0.tok0:seg0.tok0 + seg0.n]),
                 W=xres, dma="xin")
        else:
            seg0 = Seg("P", 0, 0, False, False, 0, 0)
        has_s = tile["segs"][-1].seq != "P"
        if has_s:
            S.op("sp", lambda e, seg0=seg0, w=w: e.dma_start(out=x_t[:, :, seg0.n:w], in_=xs_d[:, :, 0:w - seg0.n]),
                 W=xres, dma="xin")
        for l in range(L):
            layer(l, tile)
        if has_p:
            S.op("sp", lambda e, seg0=seg0: e.dma_start(out=yp_d[:, :, seg0.tok0:seg0.tok0 + seg0.n], in_=x_t[:, :, 0:seg0.n]),
                 R=xres, dma="out_x")
        if has_s:
            S.op("sp", lambda e, seg0=seg0, w=w: e.dma_start(out=ys_d[:, :, 0:w - seg0.n], in_=x_t[:, :, seg0.n:w]),
                 R=xres, dma="out_x")
    S.wait_all("sp", [k for k in S.cnt if k[0] == "dma" and k[1].startswith("out")])
    S.emit(nc)
    es.close()
    return nc, S


def fm(a):
    sh = a.shape
    nt, nf = sh[-2], sh[-1]
    b = a.reshape(sh[:-2] + (nt, nf // 128, 128))
    nd = b.ndim
    perm = tuple(range(nd - 3)) + (nd - 1, nd - 2, nd - 3)
    return np.ascontiguousarray(b.transpose(perm))


def unfm(a):
    nd = a.ndim
    perm = tuple(range(nd - 3)) + (nd - 1, nd - 2, nd - 3)
    b = a.transpose(perm)
    return np.ascontiguousarray(b.reshape(b.shape[:-2] + (b.shape[-2] * b.shape[-1],)))


def wblk(W, cols):
    K = W.shape[0]
    sub = W[:, cols].reshape(K // 128, 128, len(cols)).transpose(1, 0, 2)
    return np.ascontiguousarray(sub.reshape(128, -1))


def make_consts():
    cst = np.zeros((128, 708), np.float32)
    cst[:, 0:128] = np.eye(128, dtype=np.float32)
    cst[:, 128:256] = 1.0
    cst[:, 384:512] = 1.0 / 128.0
    k = np.arange(64)[:, None]
    i = np.arange(64)[None, :]
    cst[0:64, 512:576] = (k <= i)
    cst[0:64, 576:640] = np.where(i >= k, 0.0, -1e30)
    cst[0:64, 640:704] = (i > k)
    cst[:, 704] = EPS
    cst[:, 705] = 1.0
    return cst


def prep_shared(c, inp):
    L, H, CC, KC, FCH, FH = c.L, c.H, c.CC, c.KC, c.FCH, c.FH
    cst = make_consts()
    cst[:, 256:384] = 1.0 / c.D
    CW = CC * 128
    GW = H * 128
    c1 = 2 * CW
    c2 = c1 + 3 * GW
    c3 = c2 + GW
    win = np.empty((L, c.NLIN, 128, KC * 256), np.float32)
    wba = np.empty((L, 128, KC * 2 * H), np.float32)
    wout = np.empty((L, c.NLOUT, 128, c.KM * 256), np.float32)
    wup = np.empty((L, c.NLUP, 128, KC * 256), np.float32)
    wdn = np.empty((L, c.NLDN, 128, FH * 128), np.float32)
    prm = np.zeros((L, 128, c.NPL), np.float32)
    ar = np.arange
    for l in range(L):
        Wi = inp["w_in"][l]
        li = 0
        for pair in range(CC // 2):
            win[l, li] = wblk(Wi, CW + pair * 256 + ar(256)); li += 1
            win[l, li] = wblk(Wi, pair * 256 + ar(256)); li += 1
        for ld in range(3 * H // 2):
            win[l, li] = wblk(Wi, c1 + ld * 256 + ar(256)); li += 1
        for ld in range(H // 2):
            win[l, li] = wblk(Wi, c2 + ld * 256 + ar(256)); li += 1
        wba[l] = wblk(Wi, c3 + ar(2 * H))
        Wo = inp["w_out"][l]
        for ld in range(c.NLOUT):
            wout[l, ld] = wblk(Wo, ld * 256 + ar(256))
        Wu = inp["w_up"][l]
        for half in range(2):
            for pr in range(FH // 2):
                for which in range(2):
                    ld = (half * (FH // 2) + pr) * 2 + which
                    col0 = which * c.FFN + (half * FH + pr * 2) * 128
                    wup[l, ld] = wblk(Wu, col0 + ar(256))
        Wd = inp["w_down"][l]
        for half in range(2):
            for n in range(KC):
                wdn[l, half * KC + n] = wblk(Wd[half * FH * 128:(half + 1) * FH * 128], n * 128 + ar(128))

        def put(name, arr):
            o, n = c.P[name]
            prm[l, :, o:o + n] = arr.reshape(128, n)

        def colvec(v):
            return v.reshape(-1, 128).T

        put("g1", colvec(inp["g_pre_mix"][l]))
        put("g2", colvec(inp["g_post_mix"][l]))
        put("g3", colvec(inp["g_pre_ffn"][l]))
        put("g4", colvec(inp["g_post_ffn"][l]))
        put("wdw", inp["w_dw"][l].reshape(KW, CC, 128).transpose(2, 1, 0))
        put("bdw", colvec(inp["b_dw"][l]))
        put("gng", colvec(inp["gn_g"][l]))
        put("gnb", colvec(inp["gn_b"][l]))
        put("wsc", inp["w_sc"][l].reshape(SCW, 3 * H, 128).transpose(2, 1, 0))
        put("on", inp["onorm_g"][l].reshape(128, 1))
        put("wf", inp["w_ffn_dw"][l].reshape(FCW, 2 * FCH, 128).transpose(2, 1, 0))
        put("bf", colvec(inp["b_ffn_dw"][l]))
        put("alog", np.broadcast_to(inp["a_log"][l][None, :], (128, H)))
        put("dtb", np.broadcast_to(inp["dt_bias"][l][None, :], (128, H)))
    return dict(prm=prm, cst=cst, win=win, wba=wba, wout=wout, wup=wup, wdn=wdn)


def prep_core(c, inp, core):
    L = c.L
    d = {}
    d["xp"] = fm(inp["x_prompt"][core])
    xs = inp["x_sample"][2 * core:2 * core + 2]
    d["xs"] = fm(xs.reshape(2 * c.DSEQ, c.D))
    sl = slice(2 * core, 2 * core + 2)
    d["st_conv"] = fm(inp["state_conv"][:, sl]).reshape(L, 2, 128, -1)
    d["st_qkv"] = fm(inp["state_qkv_conv"][:, sl]).reshape(L, 2, 128, -1)
    d["st_ffn"] = fm(inp["state_ffn_conv"][:, sl]).reshape(L, 2, 128, -1)
    sg = inp["state_gdn"][:, sl]
    d["st_gdn"] = np.ascontiguousarray(sg.transpose(0, 1, 3, 2, 4)).reshape(L, 2, 128, -1)
    return d


def run(c, inp, ncores):
    nc, S = build(c)
    shared = prep_shared(c, inp)
    in_maps = []
    for core in range(ncores):
        d = dict(shared)
        d.update(prep_core(c, inp, core))
        in_maps.append(d)
    res = run_bass_kernel_spmd(nc, in_maps, core_ids=list(range(ncores)))
    L, H, CC, FCH = c.L, c.H, c.CC, c.FCH
    B = ncores
    yp = np.empty((B, c.SEQ, c.D), np.float32)
    ys = np.empty((2 * B, c.DSEQ, c.D), np.float32)
    conv = np.empty((L, 3 * B, KW - 1, CC * 128), np.float32)
    qkv = np.empty((L, 3 * B, SCW - 1, 3 * H * 128), np.float32)
    gdn = np.empty((L, 3 * B, H, 128, 128), np.float32)
    ffn = np.empty((L, 3 * B, FCW - 1, 2 * FCH * 128), np.float32)
    for core in range(ncores):
        r = res.results[core]
        yp[core] = unfm(r["yp"])
        ys[2 * core:2 * core + 2] = unfm(r["ys"]).reshape(2, c.DSEQ, c.D)
        for s, bi in ((0, core), (1, B + 2 * core), (2, B + 2 * core + 1)):
            conv[:, bi] = unfm(r["o_conv"][:, s].reshape(L, 128, CC, KW - 1))
            qkv[:, bi] = unfm(r["o_qkv"][:, s].reshape(L, 128, 3 * H, SCW - 1))
            ffn[:, bi] = unfm(r["o_ffn"][:, s].reshape(L, 128, 2 * FCH, FCW - 1))
            gdn[:, bi] = r["o_gdn"][:, s].reshape(L, 128, H, 128).transpose(0, 2, 1, 3)
    return (yp, ys, conv[:, :B], qkv[:, :B], gdn[:, :B], ffn[:, :B],
            conv[:, B:], qkv[:, B:], gdn[:, B:], ffn[:, B:])


def kernel(**inputs):
    inp = {k: np.asarray(v) for k, v in inputs.items()}
    c = Cfg()
    return run(c, inp, 8)
```

```python
import numpy as np
import concourse.bass as bass
import concourse.mybir as mybir
from concourse.bass_utils import run_bass_kernel_spmd

F32 = mybir.dt.float32
BF16 = mybir.dt.bfloat16
ALU = mybir.AluOpType
AF = mybir.ActivationFunctionType
EPS = 1e-6
ENGS = ["pe", "act", "dve", "pool", "sp"]
KW = 31
SCW = 4
FCW = 3
CK = 64


class Cfg:
    def __init__(s, D=2048, CC=8, H=8, FFN=5632, L=4, SEQ=2048, DSEQ=64, PT=(512, 512, 512, 512), NSLOT=4, MERGE=False):
        s.MERGE = MERGE
        s.D, s.CC, s.H, s.FFN, s.L, s.SEQ, s.DSEQ, s.PT, s.NSLOT = D, CC, H, FFN, L, SEQ, DSEQ, tuple(PT), NSLOT
        assert sum(PT) == SEQ and all(p % 64 == 0 for p in PT)
        s.KC = D // 128
        s.FCH = FFN // 128
        s.FH = s.FCH // 2
        s.KM = CC + H
        s.NIN = 2 * CC * 128 + 4 * H * 128 + 2 * H
        s.TW = max(max(PT), (PT[-1] if MERGE else 0) + 2 * DSEQ)
        assert s.TW <= 512
        s.HC = H * CK
        s.HD = H * 128
        o = 0
        s.P = {}
        for name, n in [("g1", s.KC), ("g2", s.KC), ("g3", s.KC), ("g4", s.KC),
                        ("wdw", CC * KW), ("bdw", CC), ("gng", CC), ("gnb", CC),
                        ("wsc", 3 * H * SCW), ("on", 1),
                        ("wf", 2 * s.FCH * FCW), ("bf", 2 * s.FCH),
                        ("alog", H), ("dtb", H)]:
            s.P[name] = (o, n)
            o += n
        s.NPL = o
        s.SLOTW = max(s.KC * 256, s.KM * 256, s.FH * 128)
        s.NLIN = CC + 3 * H // 2 + H // 2
        s.NLOUT = s.KC // 2
        s.NLUP = s.FCH
        s.NLDN = 2 * s.KC


class Sched:
    def __init__(self):
        self.items = {e: [] for e in ENGS}
        self.cnt = {}
        self.known = {e: {} for e in ENGS}
        self.res = {}
        self.region_base = {}
        self.epoch = 0
        self.n_ops = 0

    def new_epoch(self):
        self.epoch += 1

    def _get(self, name):
        r = self.res.get(name)
        if r is None:
            base = {}
            if isinstance(name, tuple) and name[0] in self.region_base:
                base = dict(self.region_base[name[0]])
            r = {"w": None, "r": base}
            self.res[name] = r
        return r

    def fence(self, region):
        base = dict(self.region_base.get(region, {}))
        for name in list(self.res):
            if isinstance(name, tuple) and name[0] == region:
                r = self.res.pop(name)
                if r["w"] is not None:
                    k, v = r["w"]
                    base[k] = max(base.get(k, 0), v)
                for k, v in r["r"].items():
                    base[k] = max(base.get(k, 0), v)
        self.region_base[region] = base

    def op(self, eng, fn, R=(), W=(), dma=None):
        deps = {}

        def add(k, v):
            if v > deps.get(k, 0):
                deps[k] = v

        for n in R:
            r = self._get(n)
            if r["w"] is not None:
                add(*r["w"])
        for n in W:
            r = self._get(n)
            if r["w"] is not None:
                add(*r["w"])
            for k, v in r["r"].items():
                add(k, v)
        waits = []
        kn = self.known[eng]
        for k, v in deps.items():
            if dma is None and eng == "pe" and k[0] == "pe":
                continue
            if kn.get(k, 0) >= v:
                continue
            kn[k] = v
            waits.append((k, v))
        if dma is None:
            key = (eng, self.epoch)
            self.cnt[key] = self.cnt.get(key, 0) + 1
            tok = (key, self.cnt[key])
        else:
            key = ("dma", dma)
            self.cnt[key] = self.cnt.get(key, 0) + 16
            tok = (key, self.cnt[key])
        self.items[eng].append((waits, fn, key, dma is not None))
        for n in R:
            r = self._get(n)
            k, v = tok
            if v > r["r"].get(k, 0):
                r["r"][k] = v
        for n in W:
            r = self._get(n)
            r["w"] = tok
            r["r"] = {}
        self.n_ops += 1
        return tok

    def wait_all(self, eng, keys):
        waits = []
        for k in keys:
            v = self.cnt.get(k, 0)
            if v:
                waits.append((k, v))
        self.items[eng].append((waits, None, None, False))

    def emit(self, nc):
        sems = {}
        for k in self.cnt:
            sems[k] = nc.alloc_semaphore("s_%s_%s" % (k[0], str(k[1])))
        blk_engs = {"pe": "tensor", "act": "scalar", "dve": "vector", "pool": "gpsimd", "sp": "sync"}
        with nc.Block() as block:
            for e in ENGS:
                items = self.items[e]

                def body(engine, items=items):
                    for waits, fn, key, is_dma in items:
                        for k, v in waits:
                            engine.wait_ge(sems[k], v)
                        if fn is None:
                            continue
                        ins = fn(engine)
                        ins.then_inc(sems[key], 16 if is_dma else 1)

                getattr(block, blk_engs[e])(body)


class Seg:
    def __init__(s, seq, col0, n, first, last, tok0, sidx):
        s.seq, s.col0, s.n, s.first, s.last, s.tok0, s.sidx = seq, col0, n, first, last, tok0, sidx


def build(cfg):
    c = cfg
    nc = bass.Bass("TRN2", target_bir_lowering=False)
    S = Sched()
    D, KC, CC, H, L, TW, FCH, FH, KM, HC, HD = c.D, c.KC, c.CC, c.H, c.L, c.TW, c.FCH, c.FH, c.KM, c.HC, c.HD
    NPT = len(c.PT)

    def din(name, shape):
        return nc.dram_tensor(name, list(shape), F32, kind="ExternalInput").ap()

    def dout(name, shape):
        return nc.dram_tensor(name, list(shape), F32, kind="ExternalOutput").ap()

    xp_d = din("xp", [128, KC, c.SEQ])
    xs_d = din("xs", [128, KC, 2 * c.DSEQ])
    sc_d = din("st_conv", [L, 2, 128, CC * 30])
    sq_d = din("st_qkv", [L, 2, 128, 3 * H * 3])
    sg_d = din("st_gdn", [L, 2, 128, H * 128])
    sf_d = din("st_ffn", [L, 2, 128, 2 * FCH * 2])
    prm_d = din("prm", [L, 128, c.NPL])
    cst_d = din("cst", [128, 708])
    win_d = din("win", [L, c.NLIN, 128, KC * 256])
    wba_d = din("wba", [L, 128, KC * 2 * H])
    wout_d = din("wout", [L, c.NLOUT, 128, KM * 256])
    wup_d = din("wup", [L, c.NLUP, 128, KC * 256])
    wdn_d = din("wdn", [L, c.NLDN, 128, FH * 128])

    def dscr(name, shape):
        return nc.dram_tensor(name, list(shape), BF16, kind="Internal").ap()

    scr = {"win": dscr("win_s", [L, c.NLIN, 128, KC * 256]), "wout": dscr("wout_s", [L, c.NLOUT, 128, KM * 256]),
           "wup": dscr("wup_s", [L, c.NLUP, 128, KC * 256]), "wdn": dscr("wdn_s", [L, c.NLDN, 128, FH * 128])}
    wsrc = {"win": win_d, "wout": wout_d, "wup": wup_d, "wdn": wdn_d}
    yp_d = dout("yp", [128, KC, c.SEQ])
    ys_d = dout("ys", [128, KC, 2 * c.DSEQ])
    oc_d = dout("o_conv", [L, 3, 128, CC * 30])
    oq_d = dout("o_qkv", [L, 3, 128, 3 * H * 3])
    og_d = dout("o_gdn", [L, 3, 128, H * 128])
    of_d = dout("o_ffn", [L, 3, 128, 2 * FCH * 2])

    import contextlib
    es = contextlib.ExitStack()

    def sb(name, shape, dt=F32):
        return es.enter_context(nc.sbuf_tensor("sb_" + name, list(shape), dt))

    RA_N = max(10 * HC + 4 * HD, KC * TW, FH * TW // 2 + 2 * (6 + TW) + 4 * TW,
               (90 + TW) + (10 + TW) + (KW + SCW) * 128 + 8 * TW) + 64
    RB_N = max((3 * H + H) * TW // 2, KC * TW)
    x_t = sb("x", [128, KC, TW])
    xh_t = sb("xh", [128, KC, TW], BF16)
    mA_t = sb("mA", [128, CC, TW], BF16)
    RA = sb("RA", [128, RA_N])
    RB = sb("RB", [128, RB_N])
    slots = [sb("slot%d" % i, [128, c.SLOTW], BF16) for i in range(c.NSLOT)]
    wba_t = sb("wba", [128, KC * 2 * H], BF16)
    prm_t = sb("prm", [128, c.NPL])
    cst_t = sb("cst", [128, 708])
    idb_t = sb("idb", [128, 128], BF16)
    stc_P = sb("stc_P", [128, L, CC, 30])
    stq_P = sb("stq_P", [128, L, 3 * H, 3])
    stf_P = sb("stf_P", [128, L, 2 * FCH, 2])
    S_P = sb("S_P", [128, L, H, 128])
    stc_S = sb("stc_S", [128, 2, CC, 30])
    stq_S = sb("stq_S", [128, 2, 3 * H, 3])
    stf_S = sb("stf_S", [128, 2, 2 * FCH, 2])
    S_S = sb("S_S", [128, H, 128])
    Sbf = sb("Sbf", [128, H, 128], BF16)
    rstd_t = sb("rstd", [128, TW])
    sqn_t = sb("sqn", [128, 2, TW])
    NCH = TW // CK
    l1_t = sb("l1", [64, 8, NCH, H])
    l1c_t = sb("l1c", [128, 8, H])
    ps = es.enter_context(nc.psum_tensor("ps", [128, 8 * 512], F32))

    def bank(b, npart=128, n=512, off=0):
        return ps[0:npart, b * 512 + off: b * 512 + off + n]

    identF = cst_t[:, 0:128]
    ones1 = cst_t[:, 128:256]
    onesD = cst_t[:, 256:384]
    ones128 = cst_t[:, 384:512]
    Umask = cst_t[0:64, 512:576]
    negmask = cst_t[0:64, 576:640]
    strict01 = cst_t[0:64, 640:704]
    ident64 = cst_t[0:64, 0:64]

    def P_(name, a=None, b=None):
        o, n = c.P[name]
        if a is None:
            return prm_t[:, o:o + n]
        return prm_t[:, o + a:o + (b if b is not None else a + 1)]

    def m_chunk(j):
        if j < CC:
            return mA_t[:, j, :]
        return xh_t[:, KC - H + (j - CC), :]

    def RAv(off, n, dt=F32):
        if dt == F32:
            return RA[:, off:off + n]
        return RA[:, off:off + n].bitcast(BF16)

    def RBv(off, n, dt=F32):
        if dt == F32:
            return RB[:, off:off + n]
        return RB[:, off:off + n].bitcast(BF16)

    qkv_v = RBv(0, 3 * H * TW // 2, BF16).rearrange("p (c t) -> p c t", t=TW)
    zs_v = RBv(3 * H * TW // 2, H * TW // 2, BF16).rearrange("p (c t) -> p c t", t=TW)
    y2_v = RBv(0, KC * TW).rearrange("p (c t) -> p c t", t=TW)
    y_v = RAv(0, KC * TW).rearrange("p (c t) -> p c t", t=TW)
    a_v = RAv(0, FH * TW // 2, BF16).rearrange("p (c t) -> p c t", t=TW)
    o = 0
    ubuf_v = RAv(o, 90 + TW, BF16).rearrange("p (a t) -> p a t", a=2); o += 90 + TW
    cbuf_v = RAv(o, 10 + TW, BF16).rearrange("p (a t) -> p a t", a=2); o += 10 + TW
    dg31_v = RAv(o, KW * 128, BF16).rearrange("p (a j c) -> p a j c", a=2, j=KW); o += KW * 128
    dg4_v = RAv(o, SCW * 128, BF16).rearrange("p (a j c) -> p a j c", a=2, j=SCW); o += SCW * 128
    acc_v = RAv(o, 2 * TW).rearrange("p (a t) -> p a t", a=2); o += 2 * TW
    sg_v = RAv(o, 2 * TW).rearrange("p (a t) -> p a t", a=2); o += 2 * TW
    mu_v = RAv(o, TW); o += TW
    var_v = RAv(o, TW); o += TW
    cen_v = RAv(o, TW); o += TW
    sqA_v = RAv(o, TW); o += TW
    assert o <= RA_N, (o, RA_N)
    o = FH * TW // 2
    fbuf_v = RAv(o, 2 * (6 + TW)).rearrange("p (a t) -> p a t", a=2); o += 2 * (6 + TW)
    accf_v = RAv(o, 2 * TW).rearrange("p (a t) -> p a t", a=2); o += 2 * TW
    sgate_v = RAv(o, 2 * TW).rearrange("p (a t) -> p a t", a=2); o += 2 * TW

    def gslot(i, npart=128):
        return RA[0:npart, i * HC:(i + 1) * HC].rearrange("p (h t) -> p h t", h=H)

    def gslot_bf(i):
        return RA[:, i * HC:(i + 1) * HC].bitcast(BF16).rearrange("p (a h t) -> p a h t", a=2, h=H)

    def gbig(i):
        o_ = 10 * HC + i * HD
        return RA[0:64, o_:o_ + HD // 2].bitcast(BF16).rearrange("p (h d) -> p h d", h=H)

    def gslot_h(i, npart=128):
        return RA[0:npart, i * HC:i * HC + HC // 2].bitcast(BF16).rearrange("p (h t) -> p h t", h=H)

    tiles = []
    tok = 0
    for t, pw in enumerate(c.PT):
        segs = [Seg("P", 0, pw, t == 0, t == NPT - 1, tok, 0)]
        wt = pw
        if t == NPT - 1 and c.MERGE:
            segs.append(Seg("A", pw, c.DSEQ, True, True, 0, 1))
            segs.append(Seg("B", pw + c.DSEQ, c.DSEQ, True, True, 0, 2))
            wt = pw + 2 * c.DSEQ
        tiles.append(dict(w=wt, segs=segs))
        tok += pw
    if not c.MERGE:
        tiles.append(dict(w=2 * c.DSEQ, segs=[Seg("A", 0, c.DSEQ, True, True, 0, 1),
                                              Seg("B", c.DSEQ, c.DSEQ, True, True, 0, 2)]))

    QKV = ("RB", "qkv")
    ZS = ("RB", "zs")
    ring = {"i": 0}

    cur = {"ti": 0}

    def wload(kind, l, idx, ncols):
        i = ring["i"] % c.NSLOT
        ring["i"] += 1
        slot = slots[i]
        if cur["ti"] == 0:
            src_ap = wsrc[kind][l, idx]
            S.op("pool", lambda e, slot=slot, src_ap=src_ap, ncols=ncols:
                 e.dma_start(out=slot[:, 0:ncols], in_=src_ap),
                 W=[("slot", i)], dma="w%d" % i)
            dst_ap = scr[kind][l, idx]
            S.op("sp", lambda e, slot=slot, dst_ap=dst_ap, ncols=ncols:
                 e.dma_start(out=dst_ap, in_=slot[:, 0:ncols]),
                 R=[("slot", i)], W=[("scr", kind, l, idx)], dma="ws%d" % i)
        else:
            src_ap = scr[kind][l, idx]
            S.op("sp", lambda e, slot=slot, src_ap=src_ap, ncols=ncols:
                 e.dma_start(out=slot[:, 0:ncols], in_=src_ap),
                 R=[("scr", kind, l, idx)], W=[("slot", i)], dma="w%d" % i)
        return i, slot

    mmrot = {"i": 0}

    def next_mm_bank():
        b = 1 + (mmrot["i"] % 3)
        mmrot["i"] += 1
        return b

    def mm_group(b, slot_i, slot, nk, ncols_per_k, col_off, rhs_fn, w, rhs_res, start=True, stop=True, m=128):
        def fn(e):
            ins = None
            for k in range(nk):
                ins = e.matmul(bank(b, m, w), lhsT=slot[:, k * ncols_per_k + col_off:k * ncols_per_k + col_off + m],
                               rhs=rhs_fn(k), start=(start and k == 0), stop=(stop and k == nk - 1))
            return ins
        S.op("pe", fn, R=[("slot", slot_i)] + list(rhs_res), W=[("bank", b)])

    def st_conv(seg, l):
        return (stc_P[:, l], ("stc", "P", l)) if seg.seq == "P" else (stc_S[:, seg.sidx - 1], ("stc", seg.seq))

    def st_qkv(seg, l):
        return (stq_P[:, l], ("stq", "P", l)) if seg.seq == "P" else (stq_S[:, seg.sidx - 1], ("stq", seg.seq))

    def st_ffn(seg, l):
        return (stf_P[:, l], ("stf", "P", l)) if seg.seq == "P" else (stf_S[:, seg.sidx - 1], ("stf", seg.seq))

    def st_S(seg, l):
        return (S_P[:, l], ("S", "P", l)) if seg.seq == "P" else (S_S[:, :, :], ("S", "S"))

    def rsqrt_eps(out_ap, in_ap, R, W):
        npart = out_ap.shape[0]
        S.op("act", lambda e: e.activation(out=out_ap, in_=in_ap, func=AF.Ln, bias=cst_t[0:npart, 704:705]), R=list(R) + ["cst"], W=W)
        S.op("act", lambda e: e.activation(out=out_ap, in_=out_ap, func=AF.Exp, scale=-0.5), R=[], W=W)

    S.op("sp", lambda e: e.dma_start(out=cst_t[:, :], in_=cst_d), W=["cst"], dma="cst")
    S.op("dve", lambda e: e.tensor_copy(out=idb_t[:, :], in_=identF), R=["cst"], W=["idb"])
    S.op("dve", lambda e: e.memset(stc_P[:, :, :, :].rearrange("p a b c -> p (a b c)"), 0.0), W=[("stc", "P", l) for l in range(L)])
    S.op("dve", lambda e: e.memset(stq_P[:, :, :, :].rearrange("p a b c -> p (a b c)"), 0.0), W=[("stq", "P", l) for l in range(L)])
    S.op("dve", lambda e: e.memset(stf_P[:, :, :, :].rearrange("p a b c -> p (a b c)"), 0.0), W=[("stf", "P", l) for l in range(L)])
    S.op("dve", lambda e: e.memset(S_P[:, :, :, :].rearrange("p a b c -> p (a b c)"), 0.0), W=[("S", "P", l) for l in range(L)])

    def rmsnorm_to_xh(w, gname):
        for k in range(KC):
            par = k % 2
            S.op("act", lambda e, k=k, par=par: e.activation(out=sqn_t[:, par, 0:w], in_=x_t[:, k, 0:w], func=AF.Square),
                 R=[("x", k)], W=[("sqn", par)])
            S.op("pe", lambda e, k=k, par=par: e.matmul(bank(0, 128, w), lhsT=onesD, rhs=sqn_t[:, par, 0:w],
                                                          start=(k == 0), stop=(k == KC - 1)),
                 R=[("sqn", par), "cst"], W=[("bank", 0)])
        rsqrt_eps(rstd_t[:, 0:w], bank(0, 128, w), [("bank", 0)], ["rstd"])
        for k in range(KC):
            S.op("dve", lambda e, k=k: e.scalar_tensor_tensor(out=xh_t[:, k, 0:w], in0=x_t[:, k, 0:w],
                                                                scalar=P_(gname, k), in1=rstd_t[:, 0:w],
                                                                op0=ALU.mult, op1=ALU.mult),
                 R=[("x", k), "rstd", "prm"], W=["xh"])

    def residual_epilogue(w, yv, yres, gname):
        rsqrt_eps(rstd_t[:, 0:w], bank(0, 128, w), [("bank", 0)], ["rstd"])
        for k in range(KC):
            S.op("dve", lambda e, k=k: e.scalar_tensor_tensor(out=yv[:, k, 0:w], in0=yv[:, k, 0:w],
                                                                scalar=P_(gname, k), in1=rstd_t[:, 0:w],
                                                                op0=ALU.mult, op1=ALU.mult),
                 R=["rstd", "prm"], W=[(yres, k)])
            S.op("dve", lambda e, k=k: e.tensor_tensor(out=x_t[:, k, 0:w], in0=x_t[:, k, 0:w], in1=yv[:, k, 0:w],
                                                         op=ALU.add),
                 R=[(yres, k)], W=[("x", k)])

    def out_block_epilogue(b, w, yv, yres, n, first, last):
        par = n % 2
        S.op("act", lambda e: e.activation(out=yv[:, n, 0:w], in_=bank(b, 128, w), func=AF.Copy),
             R=[("bank", b)], W=[(yres, n)])
        S.op("act", lambda e: e.activation(out=sqn_t[:, par, 0:w], in_=bank(b, 128, w), func=AF.Square),
             R=[("bank", b)], W=[("sqn", par)])
        return lambda: S.op("pe", lambda e: e.matmul(bank(0, 128, w), lhsT=onesD, rhs=sqn_t[:, par, 0:w], start=first, stop=last),
                            R=[("sqn", par), "cst"], W=[("bank", 0)])

    def gdn_chunk(l, seg, ci, c0, w):
        Sv, Sres = st_S(seg, l)
        G = ("RA",)

        def g(n):
            return ("RA", n)

        qv = qkv_v[:, 0:H, c0:c0 + CK]
        kv = qkv_v[:, H:2 * H, c0:c0 + CK]
        vv = qkv_v[:, 2 * H:3 * H, c0:c0 + CK]
        s0, s2, s3, s4 = [gslot(i) for i in (0, 2, 3, 4)]
        s5 = gslot_h(5)
        s4h = gslot_h(4)
        S.op("act", lambda e: e.activation(out=Sbf[:, :, :], in_=Sv, func=AF.Copy), R=[Sres], W=["Sbf"])
        qn = qv
        kn = kv
        kbg, kd, vb, vnew = gbig(0), gbig(1), gbig(2), gbig(3)
        bkf = lambda b, npart=128: bank(b, npart, HC).rearrange("p (h t) -> p h t", h=H)
        gL = l1_t[:, 1, ci, :]
        bL = l1_t[:, 2, ci, :]
        G_sb = l1c_t[0:64, 0, :]
        eG = l1c_t[0:64, 1, :]
        bEG = l1c_t[0:64, 2, :]
        kdsc = l1c_t[0:64, 3, :]
        gl128 = l1c_t[:, 4, :]
        dGl = l1c_t[0:64, 5, :]

        b1bf = bank(1, 64, HD // 2).bitcast(BF16).rearrange("p (h d) -> p h d", h=H)
        b2bf = bank(2, 64, HD // 2).bitcast(BF16).rearrange("p (h d) -> p h d", h=H)

        def tr_fn(dst, src):
            def fn(e):
                ins = None
                for h in range(H):
                    ins = e.transpose(dst[:, h, :], src[:, h, :], idb_t[:, :])
                return ins
            return fn
        S.op("pe", tr_fn(b1bf, kn), R=[("RB", "qn", ci), "idb"], W=[("bank", 1)])
        S.op("pe", tr_fn(b2bf, vv), R=[QKV, "idb"], W=[("bank", 2)])
        b7 = bank(7, 128, 2 * H)

        def l1mm(e):
            e.matmul(b7[0:64, 0:H], lhsT=Umask, rhs=gL, start=True, stop=True)
            return e.matmul(b7[:, H:2 * H], lhsT=ones1[0:64, :], rhs=gL, start=True, stop=True)
        S.op("pe", l1mm, R=["l1", "cst"], W=[("bank", 7)])
        S.op("act", lambda e: e.activation(out=G_sb, in_=b7[0:64, 0:H], func=AF.Copy), R=[("bank", 7)], W=["l1c0"])
        S.op("act", lambda e: e.activation(out=eG, in_=b7[0:64, 0:H], func=AF.Exp), R=[("bank", 7)], W=["l1c1"])
        S.op("act", lambda e: e.activation(out=gl128, in_=b7[:, H:2 * H], func=AF.Exp), R=[("bank", 7)], W=["l1c4"])
        S.op("dve", lambda e: e.tensor_tensor(out=dGl, in0=b7[0:64, H:2 * H], in1=G_sb, op=ALU.subtract),
             R=[("bank", 7), "l1c0"], W=["l1c5"])
        S.op("act", lambda e: e.activation(out=kdsc, in_=dGl, func=AF.Exp), R=["l1c5"], W=["l1c3"])
        S.op("dve", lambda e: e.tensor_tensor(out=bEG, in0=bL, in1=eG, op=ALU.mult), R=["l1", "l1c1"], W=["l1c2"])
        s2_64, s3_64 = gslot(2, 64), gslot(3, 64)
        S.op("dve", lambda e: e.tensor_tensor(out=s2_64, in0=Umask.unsqueeze(1).to_broadcast([64, H, CK]),
                                              in1=gL.unsqueeze(2).to_broadcast([64, H, CK]), op=ALU.mult),
             R=["l1", "cst"], W=[g("s2")])
        S.op("pe", lambda e: e.matmul(bank(5, 128, HC), lhsT=ones1[0:64, :], rhs=s2_64.rearrange("p h t -> p (h t)"),
                                      start=True, stop=True), R=[g("s2"), "cst"], W=[("bank", 5)])
        S.op("dve", lambda e: e.tensor_tensor(out=s3_64, in0=ident64.unsqueeze(1).to_broadcast([64, H, CK]),
                                              in1=bL.unsqueeze(2).to_broadcast([64, H, CK]), op=ALU.mult),
             R=["l1", "cst"], W=[g("s3")])
        S.op("pe", lambda e: e.matmul(bank(6, 64, HC), lhsT=ones1[0:64, 0:64], rhs=s3_64.rearrange("p h t -> p (h t)"),
                                      start=True, stop=True), R=[g("s3"), "cst"], W=[("bank", 6)])
        S.op("dve", lambda e: e.tensor_tensor(out=s2_64, in0=bkf(5, 64), in1=G_sb.unsqueeze(2).to_broadcast([64, H, CK]),
                                              op=ALU.subtract), R=[("bank", 5), "l1c0"], W=[g("s2")])
        S.op("dve", lambda e: e.tensor_tensor(out=s2_64, in0=s2_64, in1=negmask.unsqueeze(1).to_broadcast([64, H, CK]),
                                              op=ALU.add), R=["cst"], W=[g("s2")])
        S.op("act", lambda e: e.activation(out=s3_64, in_=s2_64, func=AF.Exp), R=[g("s2")], W=[g("s3")])
        S.op("act", lambda e: e.activation(out=s4, in_=bkf(5), func=AF.Exp), R=[("bank", 5)], W=[g("s4")])
        S.op("dve", lambda e: e.tensor_tensor(out=s5, in0=qn, in1=s4, op=ALU.mult), R=[("RB", "qn", ci), g("s4")], W=[g("s5")])
        def kkfn(e):
            ins = None
            for h in range(H):
                ins = e.matmul(bank(3, 64, CK, h * CK), lhsT=kn[:, h, :], rhs=kn[:, h, :], start=True, stop=True)
            return ins

        def qkfn(e):
            ins = None
            for h in range(H):
                ins = e.matmul(bank(4, 64, CK, h * CK), lhsT=kn[:, h, :], rhs=qn[:, h, :], start=True, stop=True)
            return ins
        S.op("pe", kkfn, R=[("RB", "qn", ci)], W=[("bank", 3)])
        S.op("pe", qkfn, R=[("RB", "qn", ci)], W=[("bank", 4)])
        s6_64, s7_64, s8_64, s9_64 = gslot_h(6, 64), gslot_h(7, 64), gslot_h(8, 64), gslot_h(9, 64)
        b6bf = bank(6, 64, HC // 2).bitcast(BF16).rearrange("p (h t) -> p h t", h=H)
        S.op("dve", lambda e: e.tensor_tensor(out=s6_64, in0=bkf(4, 64), in1=s3_64, op=ALU.mult),
             R=[("bank", 4), g("s3")], W=[g("s6")])
        S.op("dve", lambda e: e.tensor_tensor(out=s2_64, in0=s3_64, in1=strict01.unsqueeze(1).to_broadcast([64, H, CK]),
                                              op=ALU.mult), R=[g("s3"), "cst"], W=[g("s2")])
        S.op("dve", lambda e: e.tensor_tensor(out=s2_64, in0=s2_64, in1=bkf(6, 64), op=ALU.mult),
             R=[("bank", 6)], W=[g("s2")])
        S.op("dve", lambda e: e.scalar_tensor_tensor(out=s7_64, in0=bkf(3, 64), scalar=-1.0, in1=s2_64,
                                                     op0=ALU.mult, op1=ALU.mult),
             R=[("bank", 3), g("s2")], W=[g("s7")])
        def ptfn(e):
            ins = None
            for h in range(H):
                ins = e.transpose(b6bf[:, h, :], s7_64[:, h, :], idb_t[0:64, 0:64])
            return ins
        S.op("pe", ptfn, R=[g("s7"), "idb"], W=[("bank", 6)])
        S.op("act", lambda e: e.activation(out=s8_64, in_=b6bf, func=AF.Copy), R=[("bank", 6)], W=[g("s8")])
        S.op("dve", lambda e: e.tensor_tensor(out=s9_64, in0=s7_64, in1=ident64.unsqueeze(1).to_broadcast([64, H, CK]),
                                              op=ALU.add), R=[g("s7"), "cst"], W=[g("s9")])
        for kk in range(1, 6):
            def sqfn(e, kk=kk):
                ins = None
                for h in range(H):
                    if kk < 5:
                        e.matmul(bank(3, 64, CK, h * CK), lhsT=s8_64[:, h, :], rhs=s7_64[:, h, :], start=True, stop=True)
                    ins = e.matmul(bank(4, 64, CK, h * CK), lhsT=s7_64[:, h, :], rhs=s8_64[:, h, :], start=True, stop=True)
                return ins
            S.op("pe", sqfn, R=[g("s7"), g("s8")], W=[("bank", 3), ("bank", 4)])
            if kk < 5:
                S.op("act", lambda e: e.activation(out=s7_64, in_=bkf(3, 64), func=AF.Copy), R=[("bank", 3)], W=[g("s7")])
            S.op("dve", lambda e: e.tensor_copy(out=s8_64, in_=bkf(4, 64)), R=[("bank", 4)], W=[g("s8")])

            def xfn(e):
                ins = None
                for h in range(H):
                    ins = e.matmul(bank(5, 64, CK, h * CK), lhsT=s8_64[:, h, :], rhs=s9_64[:, h, :], start=True, stop=True)
                return ins
            S.op("pe", xfn, R=[g("s8"), g("s9")], W=[("bank", 5)])
            S.op("dve", lambda e: e.tensor_tensor(out=s9_64, in0=s9_64, in1=bkf(5, 64), op=ALU.add),
                 R=[("bank", 5)], W=[g("s9")])
        S.op("dve", lambda e: e.tensor_tensor(out=kbg, in0=b1bf, in1=bEG.unsqueeze(2).to_broadcast([64, H, 128]), op=ALU.mult),
             R=[("bank", 1), "l1c2"], W=[g("kbg")])
        S.op("dve", lambda e: e.tensor_tensor(out=kd, in0=b1bf, in1=kdsc.unsqueeze(2).to_broadcast([64, H, 128]), op=ALU.mult),
             R=[("bank", 1), "l1c3"], W=[g("kd")])
        S.op("dve", lambda e: e.tensor_tensor(out=vb, in0=b2bf, in1=bL.unsqueeze(2).to_broadcast([64, H, 128]), op=ALU.mult),
             R=[("bank", 2), "l1"], W=[g("vb")])
        def wtfn(e):
            ins = None
            for h in range(H):
                ins = e.matmul(bank(1, 128, CK, h * CK), lhsT=kbg[:, h, :], rhs=s9_64[:, h, :], start=True, stop=True)
            return ins
        S.op("pe", wtfn, R=[g("kbg"), g("s9")], W=[("bank", 1)])
        S.op("act", lambda e: e.activation(out=s4h, in_=bkf(1), func=AF.Copy, scale=-1.0), R=[("bank", 1)], W=[g("s4")])
        vps = ps[0:64, 6 * 512:6 * 512 + HD].rearrange("p (h d) -> p h d", h=H)
        sps = ps[:, 6 * 512:6 * 512 + HD].rearrange("p (h d) -> p h d", h=H)

        def vnfn(e):
            ins = None
            for h in range(H):
                e.matmul(vps[:, h, :], lhsT=s9_64[:, h, :], rhs=vb[:, h, :], start=True, stop=False)
                ins = e.matmul(vps[:, h, :], lhsT=s4h[:, h, :], rhs=Sbf[:, h, :], start=False, stop=True)
            return ins
        S.op("pe", vnfn, R=[g("s9"), g("vb"), g("s4"), "Sbf"], W=[("bank", 6), ("bank", 7)])
        S.op("act", lambda e: e.activation(out=vnew, in_=vps, func=AF.Copy), R=[("bank", 6), ("bank", 7)], W=[g("vnew")])

        def otfn(e):
            ins = None
            for h in range(H):
                e.matmul(bank(2, 128, CK, h * CK), lhsT=Sbf[:, h, :], rhs=s5[:, h, :], start=True, stop=False)
                ins = e.matmul(bank(2, 128, CK, h * CK), lhsT=vnew[:, h, :], rhs=s6_64[:, h, :], start=False, stop=True)
            return ins
        S.op("pe", otfn, R=["Sbf", g("s5"), g("vnew"), g("s6")], W=[("bank", 2)])

        def supfn(e):
            ins = None
            for h in range(H):
                ins = e.matmul(sps[:, h, :], lhsT=kd[:, h, :], rhs=vnew[:, h, :], start=True, stop=True)
            return ins
        S.op("pe", supfn, R=[g("kd"), g("vnew")], W=[("bank", 6), ("bank", 7)])
        S.op("dve", lambda e: e.tensor_tensor(out=Sv, in0=Sv, in1=gl128.unsqueeze(2).to_broadcast([128, H, 128]), op=ALU.mult),
             R=["l1c4"], W=[Sres])
        S.op("dve", lambda e: e.tensor_tensor(out=Sv, in0=Sv, in1=sps, op=ALU.add),
             R=[("bank", 6), ("bank", 7)], W=[Sres])
        S.op("act", lambda e: e.activation(out=s3, in_=bkf(2), func=AF.Copy), R=[("bank", 2)], W=[g("s3")])
        S.op("act", lambda e: e.activation(out=s0, in_=bkf(2), func=AF.Square), R=[("bank", 2)], W=[g("s0")])
        S.op("pe", lambda e: e.matmul(bank(0, 128, HC), lhsT=ones128, rhs=s0.rearrange("p h t -> p (h t)"),
                                      start=True, stop=True), R=[g("s0"), "cst"], W=[("bank", 0)])
        rsqrt_eps(s2, bkf(0), [("bank", 0)], [g("s2")])
        S.op("dve", lambda e: e.tensor_tensor(out=s3, in0=s3, in1=s2, op=ALU.mult), R=[g("s2")], W=[g("s3")])
        mg = xh_t[:, KC - H:KC, c0:c0 + CK]
        S.op("dve", lambda e: e.scalar_tensor_tensor(out=mg, in0=s3, scalar=P_("on", 0), in1=zs_v[:, :, c0:c0 + CK],
                                                     op0=ALU.mult, op1=ALU.mult),
             R=[g("s3"), "prm", ZS], W=["xh"])

    def l2norm_tile(w):
        for ci in range(w // CK):
            c0 = ci * CK
            pr = ci % 2
            sA = gslot(0) if pr == 0 else gslot(3)
            sB = gslot(2) if pr == 0 else gslot(4)
            bk = 0 if pr == 0 else 3
            nA, nB = (("RA", "s0"), ("RA", "s2")) if pr == 0 else (("RA", "s3"), ("RA", "s4"))
            for which, scl in ((0, 128.0 ** -0.5), (1, 1.0)):
                src = qkv_v[:, which * H:(which + 1) * H, c0:c0 + CK]
                S.op("act", lambda e, src=src, sA=sA: e.activation(out=sA, in_=src, func=AF.Square), R=[QKV], W=[nA])
                S.op("pe", lambda e, sA=sA, bk=bk: e.matmul(bank(bk, 128, HC), lhsT=ones1, rhs=sA.rearrange("p h t -> p (h t)"),
                                                           start=True, stop=True), R=[nA, "cst"], W=[("bank", bk)])
                rsqrt_eps(sB, bank(bk, 128, HC).rearrange("p (h t) -> p h t", h=H), [("bank", bk)], [nB])
                S.op("dve", lambda e, src=src, sB=sB, scl=scl: e.scalar_tensor_tensor(
                    out=src, in0=src, scalar=scl, in1=sB, op0=ALU.mult, op1=ALU.mult), R=[nB], W=[("RB", "qn", ci)])

    def layer(l, tile):
        w = tile["w"]
        segs = tile["segs"]
        grp = (len(segs) == 2 and all(sg_.seq != "P" for sg_ in segs) and segs[0].n == segs[1].n
               and segs[1].col0 == segs[0].col0 + segs[0].n)
        gn = segs[0].n
        gc0 = segs[0].col0
        S.new_epoch()
        S.op("sp", lambda e: e.dma_start(out=prm_t[:, :], in_=prm_d[l]), W=["prm"], dma="prm")
        S.op("pool", lambda e: e.dma_start(out=wba_t[:, :], in_=wba_d[l]), W=["wba"], dma="wba")
        for seg in segs:
            if seg.seq != "P":
                b = seg.sidx - 1
                S.op("sp", lambda e, b=b: e.dma_start(out=stc_S[:, b].rearrange("p c j -> p (c j)"), in_=sc_d[l, b]),
                     W=[("stc", seg.seq)], dma="stc" + seg.seq)
                S.op("sp", lambda e, b=b: e.dma_start(out=stq_S[:, b].rearrange("p c j -> p (c j)"), in_=sq_d[l, b]),
                     W=[("stq", seg.seq)], dma="stq" + seg.seq)
                S.op("sp", lambda e, b=b: e.dma_start(out=stf_S[:, b].rearrange("p c j -> p (c j)"), in_=sf_d[l, b]),
                     W=[("stf", seg.seq)], dma="stf" + seg.seq)
        S.fence("RA")
        S.fence("RB")
        rmsnorm_to_xh(w, "g1")
        pend = {"cc": None, "p2": None}

        def conv_ln(cc, par):
            av = acc_v[:, par, 0:w]
            S.op("act", lambda e, av=av: e.activation(out=sqA_v[:, 0:w], in_=av, func=AF.Square),
                 R=[("RA", "acc", par)], W=[("RA", "sqA")])
            S.op("pe", lambda e, av=av: e.matmul(bank(0, 128, w), lhsT=ones128, rhs=av, start=True, stop=True),
                 R=[("RA", "acc", par), "cst"], W=[("bank", 0)])
            S.op("pe", lambda e: e.matmul(bank(4, 128, w), lhsT=ones128, rhs=sqA_v[:, 0:w], start=True, stop=True),
                 R=[("RA", "sqA"), "cst"], W=[("bank", 4)])
            S.op("act", lambda e: e.activation(out=mu_v[:, 0:w], in_=bank(0, 128, w), func=AF.Copy),
                 R=[("bank", 0)], W=[("RA", "mu")])
            S.op("dve", lambda e: e.tensor_tensor(out=var_v[:, 0:w], in0=mu_v[:, 0:w], in1=mu_v[:, 0:w], op=ALU.mult),
                 R=[("RA", "mu")], W=[("RA", "var")])
            S.op("dve", lambda e: e.tensor_tensor(out=var_v[:, 0:w], in0=bank(4, 128, w), in1=var_v[:, 0:w], op=ALU.subtract),
                 R=[("bank", 4)], W=[("RA", "var")])
            rsqrt_eps(var_v[:, 0:w], var_v[:, 0:w], [], [("RA", "var")])
            S.op("dve", lambda e, av=av: e.tensor_tensor(out=cen_v[:, 0:w], in0=av, in1=mu_v[:, 0:w], op=ALU.subtract),
                 R=[("RA", "acc", par), ("RA", "mu")], W=[("RA", "cen")])
            S.op("dve", lambda e: e.tensor_tensor(out=cen_v[:, 0:w], in0=cen_v[:, 0:w], in1=var_v[:, 0:w], op=ALU.mult),
                 R=[("RA", "var")], W=[("RA", "cen")])
            S.op("act", lambda e, cc=cc: e.activation(out=mA_t[:, cc, 0:w], in_=cen_v[:, 0:w], func=AF.Silu,
                                                      bias=P_("gnb", cc), scale=P_("gng", cc)),
                 R=[("RA", "cen"), "prm"], W=["mA"])

        li = 0
        for pair in range(CC // 2):
            si, slot = wload("win", l, li, KC * 256); li += 1
            for j in range(2):
                b = next_mm_bank()
                mm_group(b, si, slot, KC, 256, j * 128, lambda k: xh_t[:, k, 0:w], w, ["xh"])
                S.op("act", lambda e, b=b, j=j: e.activation(out=sg_v[:, j, 0:w], in_=bank(b, 128, w), func=AF.Sigmoid),
                     R=[("bank", b)], W=[("RA", "sg", j)])
            si, slot = wload("win", l, li, KC * 256); li += 1
            for j in range(2):
                cc = pair * 2 + j
                b = next_mm_bank()
                mm_group(b, si, slot, KC, 256, j * 128, lambda k: xh_t[:, k, 0:w], w, ["xh"])
                par = cc % 2
                cb_ = 6 + par
                S.op("dve", lambda e, cc=cc, par=par: e.tensor_tensor(
                    out=dg31_v[:, par], in0=idb_t[:, :].unsqueeze(1).to_broadcast([128, KW, 128]),
                    in1=P_("wdw", cc * KW, cc * KW + KW).unsqueeze(2).to_broadcast([128, KW, 128]), op=ALU.mult),
                    R=["idb", "prm"], W=[("RA", "dg31", par)])
                if grp:
                    UB = ubuf_v[:, par, 0:2 * (30 + gn)].rearrange("p (s t) -> p s t", t=30 + gn)
                    STc = stc_S[:, :, cc, :]
                    BKc = bank(b, 128, 2 * gn, gc0).rearrange("p (s t) -> p s t", t=gn)
                    SGc = sg_v[:, j, gc0:gc0 + 2 * gn].rearrange("p (s t) -> p s t", t=gn)
                    sres = [("stc", "A"), ("stc", "B")]
                    S.op("act", lambda e, UB=UB, STc=STc: e.activation(out=UB[:, :, 0:30], in_=STc, func=AF.Copy),
                         R=sres, W=[("RA", "ubuf", par)])
                    S.op("dve", lambda e, UB=UB, BKc=BKc, SGc=SGc: e.tensor_tensor(out=UB[:, :, 30:30 + gn], in0=BKc, in1=SGc, op=ALU.mult),
                         R=[("bank", b), ("RA", "sg", j)], W=[("RA", "ubuf", par)])
                    S.op("dve", lambda e, STc=STc, BKc=BKc, SGc=SGc: e.tensor_tensor(
                        out=STc, in0=BKc[:, :, gn - 30:gn], in1=SGc[:, :, gn - 30:gn], op=ALU.mult),
                        R=[("bank", b), ("RA", "sg", j), ("RA", "ubuf", par)], W=sres)
                    cfns = []
                    off = 0
                    for seg in segs:
                        n = seg.n
                        ub = ubuf_v[:, par, off:off + 30 + n]

                        def convfn(e, ub=ub, par=par, cb_=cb_, seg=seg, n=n):
                            ins = None
                            for tp in range(KW):
                                ins = e.matmul(bank(cb_, 128, n, seg.col0), lhsT=dg31_v[:, par, tp, :], rhs=ub[:, tp:tp + n],
                                               start=(tp == 0), stop=(tp == KW - 1))
                            return ins
                        cfns.append(convfn)
                        off += 30 + n
                else:
                    off = 0
                    cfns = []
                    for seg in segs:
                        stv, stres = st_conv(seg, l)
                        n = seg.n
                        assert n >= 30
                        ub = ubuf_v[:, par, off:off + 30 + n]
                        S.op("act", lambda e, ub=ub, stv=stv, cc=cc: e.activation(out=ub[:, 0:30], in_=stv[:, cc, :], func=AF.Copy),
                             R=[stres], W=[("RA", "ubuf", par)])
                        S.op("dve", lambda e, ub=ub, b=b, j=j, seg=seg, n=n: e.tensor_tensor(
                            out=ub[:, 30:30 + n], in0=bank(b, 128, n, seg.col0), in1=sg_v[:, j, seg.col0:seg.col0 + n], op=ALU.mult),
                            R=[("bank", b), ("RA", "sg", j)], W=[("RA", "ubuf", par)])
                        S.op("dve", lambda e, stv=stv, cc=cc, b=b, j=j, seg=seg, n=n: e.tensor_tensor(
                            out=stv[:, cc, :], in0=bank(b, 128, 30, seg.col0 + n - 30),
                            in1=sg_v[:, j, seg.col0 + n - 30:seg.col0 + n], op=ALU.mult),
                            R=[("bank", b), ("RA", "sg", j), ("RA", "ubuf", par)], W=[stres])

                        def convfn(e, ub=ub, par=par, cb_=cb_, seg=seg, n=n):
                            ins = None
                            for tp in range(KW):
                                ins = e.matmul(bank(cb_, 128, n, seg.col0), lhsT=dg31_v[:, par, tp, :], rhs=ub[:, tp:tp + n],
                                               start=(tp == 0), stop=(tp == KW - 1))
                            return ins
                        cfns.append(convfn)
                        off += 30 + n

                def part2(cfns=cfns, par=par, cb_=cb_, cc=cc):
                    for f in cfns:
                        S.op("pe", f, R=[("RA", "ubuf", par), ("RA", "dg31", par)], W=[("bank", cb_)])
                    S.op("act", lambda e: e.activation(out=acc_v[:, par, 0:w], in_=bank(cb_, 128, w),
                                                       func=AF.Identity, bias=P_("bdw", cc)),
                         R=[("bank", cb_), "prm"], W=[("RA", "acc", par)])
                if pend["p2"] is not None:
                    pend["p2"][0]()
                    if pend["cc"] is not None:
                        conv_ln(*pend["cc"])
                    pend["cc"] = pend["p2"][1]
                pend["p2"] = (part2, (cc, par))
        pend["p2"][0]()
        if pend["cc"] is not None:
            conv_ln(*pend["cc"])
        conv_ln(*pend["p2"][1])
        pq = {"f": None}
        for ld in range(3 * H // 2):
            si, slot = wload("win", l, li, KC * 256); li += 1
            for j in range(2):
                i = ld * 2 + j
                b = next_mm_bank()
                mm_group(b, si, slot, KC, 256, j * 128, lambda k: xh_t[:, k, 0:w], w, ["xh"])
                par = i % 2
                cb_ = 6 + par
                S.op("dve", lambda e, i=i, par=par: e.tensor_tensor(
                    out=dg4_v[:, par], in0=idb_t[:, :].unsqueeze(1).to_broadcast([128, SCW, 128]),
                    in1=P_("wsc", i * SCW, i * SCW + SCW).unsqueeze(2).to_broadcast([128, SCW, 128]), op=ALU.mult),
                    R=["idb", "prm"], W=[("RA", "dg4", par)])
                if grp:
                    CBg = cbuf_v[:, par, 0:2 * (3 + gn)].rearrange("p (s t) -> p s t", t=3 + gn)
                    STq = stq_S[:, :, i, :]
                    BKq = bank(b, 128, 2 * gn, gc0).rearrange("p (s t) -> p s t", t=gn)
                    sres = [("stq", "A"), ("stq", "B")]
                    S.op("act", lambda e, CBg=CBg, STq=STq: e.activation(out=CBg[:, :, 0:3], in_=STq, func=AF.Copy),
                         R=sres, W=[("RA", "cbuf", par)])
                    S.op("act", lambda e, CBg=CBg, BKq=BKq: e.activation(out=CBg[:, :, 3:3 + gn], in_=BKq, func=AF.Copy),
                         R=[("bank", b)], W=[("RA", "cbuf", par)])
                    S.op("act", lambda e, STq=STq, BKq=BKq: e.activation(out=STq, in_=BKq[:, :, gn - 3:gn], func=AF.Copy),
                         R=[("bank", b), ("RA", "cbuf", par)], W=sres)
                    c4s = []
                    off = 0
                    for seg in segs:
                        n = seg.n
                        cb = cbuf_v[:, par, off:off + 3 + n]

                        def c4fn(e, cb=cb, par=par, cb_=cb_, seg=seg, n=n):
                            ins = None
                            for tp in range(SCW):
                                ins = e.matmul(bank(cb_, 128, n, seg.col0), lhsT=dg4_v[:, par, tp, :], rhs=cb[:, tp:tp + n],
                                               start=(tp == 0), stop=(tp == SCW - 1))
                            return ins
                        c4s.append(c4fn)
                        off += 3 + n
                else:
                    off = 0
                    c4s = []
                    for seg in segs:
                        stv, stres = st_qkv(seg, l)
                        n = seg.n
                        cb = cbuf_v[:, par, off:off + 3 + n]
                        S.op("act", lambda e, cb=cb, stv=stv, i=i: e.activation(out=cb[:, 0:3], in_=stv[:, i, :], func=AF.Copy),
                             R=[stres], W=[("RA", "cbuf", par)])
                        S.op("act", lambda e, cb=cb, b=b, seg=seg, n=n: e.activation(out=cb[:, 3:3 + n], in_=bank(b, 128, n, seg.col0),
                                                                                 func=AF.Copy),
                             R=[("bank", b)], W=[("RA", "cbuf", par)])
                        S.op("act", lambda e, stv=stv, i=i, b=b, seg=seg, n=n: e.activation(
                            out=stv[:, i, :], in_=bank(b, 128, 3, seg.col0 + n - 3), func=AF.Copy),
                            R=[("bank", b), ("RA", "cbuf", par)], W=[stres])

                        def c4fn(e, cb=cb, par=par, cb_=cb_, seg=seg, n=n):
                            ins = None
                            for tp in range(SCW):
                                ins = e.matmul(bank(cb_, 128, n, seg.col0), lhsT=dg4_v[:, par, tp, :], rhs=cb[:, tp:tp + n],
                                               start=(tp == 0), stop=(tp == SCW - 1))
                            return ins
                        c4s.append(c4fn)
                        off += 3 + n

                def q2(c4s=c4s, par=par, cb_=cb_, i=i):
                    for f in c4s:
                        S.op("pe", f, R=[("RA", "cbuf", par), ("RA", "dg4", par)], W=[("bank", cb_)])
                    S.op("act", lambda e: e.activation(out=qkv_v[:, i, 0:w], in_=bank(cb_, 128, w), func=AF.Silu),
                         R=[("bank", cb_)], W=[QKV])
                if pq["f"] is not None:
                    pq["f"]()
                pq["f"] = q2
        pq["f"]()
        for ld in range(H // 2):
            si, slot = wload("win", l, li, KC * 256); li += 1
            for j in range(2):
                h = ld * 2 + j
                b = next_mm_bank()
                mm_group(b, si, slot, KC, 256, j * 128, lambda k: xh_t[:, k, 0:w], w, ["xh"])
                S.op("act", lambda e, b=b, h=h: e.activation(out=zs_v[:, h, 0:w], in_=bank(b, 128, w), func=AF.Silu),
                     R=[("bank", b)], W=[ZS])
        nch = w // CK
        b5 = bank(5, 64, nch * 2 * H).rearrange("p (c h) -> p c h", c=nch)

        def bafn(e):
            ins = None
            for ci in range(nch):
                for k in range(KC):
                    ins = e.matmul(b5[:, ci, :], lhsT=xh_t[:, k, ci * CK:(ci + 1) * CK], rhs=wba_t[:, k * 2 * H:(k + 1) * 2 * H],
                                   start=(k == 0), stop=(k == KC - 1))
            return ins
        S.op("pe", bafn, R=["xh", "wba"], W=[("bank", 5)])
        L1 = lambda i: l1_t[:, i, 0:nch, :]
        S.op("act", lambda e: e.activation(out=L1(2), in_=b5[:, :, 0:H], func=AF.Sigmoid), R=[("bank", 5)], W=["l1"])
        dtb = P_("dtb")[0:64, :].unsqueeze(1).to_broadcast([64, nch, H])
        alg = P_("alog")[0:64, :]
        S.op("dve", lambda e: e.tensor_tensor(out=L1(0), in0=b5[:, :, H:2 * H], in1=dtb, op=ALU.add),
             R=[("bank", 5), "prm"], W=["l1"])
        S.op("act", lambda e: e.activation(out=L1(3), in_=L1(0), func=AF.Abs), R=[], W=["l1"])
        S.op("act", lambda e: e.activation(out=L1(3), in_=L1(3), func=AF.Exp, scale=-1.0), R=[], W=["l1"])
        S.op("act", lambda e: e.activation(out=L1(3), in_=L1(3), func=AF.Ln, bias=cst_t[0:64, 705:706]), R=[], W=["l1"])
        S.op("dve", lambda e: e.scalar_tensor_tensor(out=L1(0), in0=L1(0), scalar=0.0, in1=L1(3), op0=ALU.max, op1=ALU.add),
             R=[], W=["l1"])
        S.op("act", lambda e: e.activation(out=l1_t[:, 4, 0, :], in_=alg, func=AF.Exp), R=["prm"], W=["l1"])
        S.op("dve", lambda e: e.scalar_tensor_tensor(out=L1(1), in0=L1(0), scalar=-1.0,
                                                     in1=l1_t[:, 4, 0, :].unsqueeze(1).to_broadcast([64, nch, H]),
                                                     op0=ALU.mult, op1=ALU.mult), R=[], W=["l1"])
        S.fence("RA")
        l2norm_tile(w)
        for seg in segs:
            if seg.seq != "P":
                b = seg.sidx - 1
                S.op("sp", lambda e, b=b: e.dma_start(out=S_S[:, :, :].rearrange("p h d -> p (h d)"), in_=sg_d[l, b]),
                     W=[("S", "S")], dma="SS")
            for cj in range(seg.n // CK):
                c0 = seg.col0 + cj * CK
                gdn_chunk(l, seg, c0 // CK, c0, w)
            if seg.seq != "P":
                S.op("sp", lambda e, seg=seg: e.dma_start(out=og_d[l, seg.sidx], in_=S_S[:, :, :].rearrange("p h d -> p (h d)")),
                     R=[("S", "S")], dma="out_SS")
            elif seg.last:
                S.op("sp", lambda e: e.dma_start(out=og_d[l, 0], in_=S_P[:, l].rearrange("p h d -> p (h d)")),
                     R=[("S", "P", l)], dma="out")
        for seg in segs:
            if seg.last:
                stv, stres = st_conv(seg, l)
                S.op("sp", lambda e, stv=stv, seg=seg: e.dma_start(out=oc_d[l, seg.sidx], in_=stv.rearrange("p c j -> p (c j)")),
                     R=[stres], dma="out_stc" + seg.seq)
                stv, stres = st_qkv(seg, l)
                S.op("sp", lambda e, stv=stv, seg=seg: e.dma_start(out=oq_d[l, seg.sidx], in_=stv.rearrange("p c j -> p (c j)")),
                     R=[stres], dma="out_stq" + seg.seq)
        S.fence("RA")
        pend2 = {"f": None}
        for ld in range(c.NLOUT):
            si, slot = wload("wout", l, ld, KM * 256)
            for j in range(2):
                n = ld * 2 + j
                b = next_mm_bank()
                mm_group(b, si, slot, KM, 256, j * 128, lambda k: m_chunk(k)[:, 0:w], w, ["mA", "xh"])
                nxt = out_block_epilogue(b, w, y_v, ("RA", "y"), n, n == 0, n == KC - 1)
                if pend2["f"] is not None:
                    pend2["f"]()
                pend2["f"] = nxt
        pend2["f"]()
        pend2["f"] = None
        residual_epilogue(w, y_v, ("RA", "y"), "g2")
        rmsnorm_to_xh(w, "g3")
        S.fence("RA")
        S.fence("RB")
        for half in range(2):
            for pr in range(FH // 2):
                for which in range(2):
                    ld = (half * (FH // 2) + pr) * 2 + which
                    si, slot = wload("wup", l, ld, KC * 256)
                    for j in range(2):
                        hc = half * FH + pr * 2 + j
                        ch = hc + which * FCH
                        b = next_mm_bank()
                        mm_group(b, si, slot, KC, 256, j * 128, lambda k: xh_t[:, k, 0:w], w, ["xh"])
                        par = j
                        if grp:
                            Fg = fbuf_v[:, par, 0:2 * (2 + gn)].rearrange("p (s t) -> p s t", t=2 + gn)
                            STf = stf_S[:, :, ch, :]
                            BKf = bank(b, 128, 2 * gn, gc0).rearrange("p (s t) -> p s t", t=gn)
                            AVg = accf_v[:, par, gc0:gc0 + 2 * gn].rearrange("p (s t) -> p s t", t=gn)
                            sres = [("stf", "A"), ("stf", "B")]
                            S.op("act", lambda e, Fg=Fg, STf=STf: e.activation(out=Fg[:, :, 0:2], in_=STf, func=AF.Copy),
                                 R=sres, W=[("RA", "fbuf", par)])
                            S.op("act", lambda e, Fg=Fg, BKf=BKf: e.activation(out=Fg[:, :, 2:2 + gn], in_=BKf, func=AF.Copy),
                                 R=[("bank", b)], W=[("RA", "fbuf", par)])
                            S.op("dve", lambda e, Fg=Fg, AVg=AVg, ch=ch: e.tensor_scalar(
                                out=AVg, in0=Fg[:, :, 0:gn], scalar1=P_("wf", ch * FCW), scalar2=P_("bf", ch),
                                op0=ALU.mult, op1=ALU.add), R=[("RA", "fbuf", par), "prm"], W=[("RA", "accf", par)])
                            for tp in range(1, FCW):
                                S.op("dve", lambda e, Fg=Fg, AVg=AVg, ch=ch, tp=tp: e.scalar_tensor_tensor(
                                    out=AVg, in0=Fg[:, :, tp:tp + gn], scalar=P_("wf", ch * FCW + tp), in1=AVg,
                                    op0=ALU.mult, op1=ALU.add), R=[("RA", "fbuf", par), "prm"], W=[("RA", "accf", par)])
                            S.op("act", lambda e, Fg=Fg, STf=STf: e.activation(out=STf, in_=Fg[:, :, gn:gn + 2], func=AF.Copy),
                                 R=[("RA", "fbuf", par)], W=sres)
                        else:
                            off = 0
                            for seg in segs:
                                stv, stres = st_ffn(seg, l)
                                n = seg.n
                                fb = fbuf_v[:, par, off:off + 2 + n]
                                S.op("act", lambda e, fb=fb, stv=stv, ch=ch: e.activation(out=fb[:, 0:2], in_=stv[:, ch, :], func=AF.Copy),
                                     R=[stres], W=[("RA", "fbuf", par)])
                                S.op("act", lambda e, fb=fb, b=b, seg=seg, n=n: e.activation(out=fb[:, 2:2 + n], in_=bank(b, 128, n, seg.col0),
                                                                                         func=AF.Copy),
                                     R=[("bank", b)], W=[("RA", "fbuf", par)])
                                av = accf_v[:, par, seg.col0:seg.col0 + n]
                                S.op("dve", lambda e, fb=fb, av=av, ch=ch, n=n: e.tensor_scalar(
                                    out=av, in0=fb[:, 0:n], scalar1=P_("wf", ch * FCW), scalar2=P_("bf", ch),
                                    op0=ALU.mult, op1=ALU.add), R=[("RA", "fbuf", par), "prm"], W=[("RA", "accf", par)])
                                for tp in range(1, FCW):
                                    S.op("dve", lambda e, fb=fb, av=av, ch=ch, n=n, tp=tp: e.scalar_tensor_tensor(
                                        out=av, in0=fb[:, tp:tp + n], scalar=P_("wf", ch * FCW + tp), in1=av,
                                        op0=ALU.mult, op1=ALU.add), R=[("RA", "fbuf", par), "prm"], W=[("RA", "accf", par)])
                                S.op("act", lambda e, fb=fb, stv=stv, ch=ch, n=n: e.activation(out=stv[:, ch, :], in_=fb[:, n:n + 2], func=AF.Copy),
                                     R=[("RA", "fbuf", par)], W=[stres])
                                off += 2 + n
                        if which == 0:
                            S.op("act", lambda e, j=j, par=par: e.activation(out=sgate_v[:, j, 0:w], in_=accf_v[:, par, 0:w], func=AF.Silu),
                                 R=[("RA", "accf", par)], W=[("RA", "sgate", j)])
                        else:
                            S.op("dve", lambda e, j=j, par=par, ai=hc - half * FH: e.tensor_tensor(
                                out=a_v[:, ai, 0:w], in0=accf_v[:, par, 0:w], in1=sgate_v[:, j, 0:w], op=ALU.mult),
                                R=[("RA", "accf", par), ("RA", "sgate", j)], W=[("RA", "a")])
            for n in range(KC):
                si, slot = wload("wdn", l, half * KC + n, FH * 128)
                b = next_mm_bank()
                mm_group(b, si, slot, FH, 128, 0, lambda k: a_v[:, k, 0:w], w, [("RA", "a")])
                if half == 0:
                    S.op("act", lambda e, b=b, n=n: e.activation(out=y2_v[:, n, 0:w], in_=bank(b, 128, w), func=AF.Copy),
                         R=[("bank", b)], W=[("RB", "y2", n)])
                else:
                    S.op("dve", lambda e, b=b, n=n: e.tensor_tensor(out=y2_v[:, n, 0:w], in0=y2_v[:, n, 0:w], in1=bank(b, 128, w),
                                                                  op=ALU.add), R=[("bank", b)], W=[("RB", "y2", n)])
                    par = n % 2
                    S.op("act", lambda e, n=n, par=par: e.activation(out=sqn_t[:, par, 0:w], in_=y2_v[:, n, 0:w], func=AF.Square),
                         R=[("RB", "y2", n)], W=[("sqn", par)])
                    nxt = (lambda n=n, par=par: S.op("pe", lambda e: e.matmul(bank(0, 128, w), lhsT=onesD, rhs=sqn_t[:, par, 0:w],
                                                                              start=(n == 0), stop=(n == KC - 1)),
                                                     R=[("sqn", par), "cst"], W=[("bank", 0)]))
                    if pend2["f"] is not None:
                        pend2["f"]()
                    pend2["f"] = nxt
        pend2["f"]()
        pend2["f"] = None
        residual_epilogue(w, y2_v, ("RB", "y2"), "g4")
        for seg in segs:
            if seg.last:
                stv, stres = st_ffn(seg, l)
                S.op("sp", lambda e, stv=stv, seg=seg: e.dma_start(out=of_d[l, seg.sidx], in_=stv.rearrange("p c j -> p (c j)")),
                     R=[stres], dma="out_stf" + seg.seq)

    for ti, tile in enumerate(tiles):
        cur["ti"] = ti
        w = tile["w"]
        seg0 = tile["segs"][0]
        xres = [("x", k) for k in range(KC)]
        has_p = seg0.seq == "P"
        if has_p:
            S.op("sp", lambda e, seg0=seg0: e.dma_start(out=x_t[:, :, 0:seg0.n], in_=xp_d[:, :, seg0.tok0:seg0.tok0 + seg0.n]),
                 W=xres, dma="xin")
        else:
            seg0 = Seg("P", 0, 0, False, False, 0, 0)
        has_s = tile["segs"][-1].seq != "P"
        if has_s:
            S.op("sp", lambda e, seg0=seg0, w=w: e.dma_start(out=x_t[:, :, seg0.n:w], in_=xs_d[:, :, 0:w - seg0.n]),
                 W=xres, dma="xin")
        for l in range(L):
            layer(l, tile)
        if has_p:
            S.op("sp", lambda e, seg0=seg0: e.dma_start(out=yp_d[:, :, seg0.tok0:seg0.tok0 + seg0.n], in_=x_t[:, :, 0:seg0.n]),
                 R=xres, dma="out_x")
        if has_s:
            S.op("sp", lambda e, seg0=seg0, w=w: e.dma_start(out=ys_d[:, :, 0:w - seg0.n], in_=x_t[:, :, seg0.n:w]),
                 R=xres, dma="out_x")
    S.wait_all("sp", [k for k in S.cnt if k[0] == "dma" and k[1].startswith("out")])
    S.emit(nc)
    es.close()
    return nc, S


def fm(a):
    sh = a.shape
    nt, nf = sh[-2], sh[-1]
    b = a.reshape(sh[:-2] + (nt, nf // 128, 128))
    nd = b.ndim
    perm = tuple(range(nd - 3)) + (nd - 1, nd - 2, nd - 3)
    return np.ascontiguousarray(b.transpose(perm))


def unfm(a):
    nd = a.ndim
    perm = tuple(range(nd - 3)) + (nd - 1, nd - 2, nd - 3)
    b = a.transpose(perm)
    return np.ascontiguousarray(b.reshape(b.shape[:-2] + (b.shape[-2] * b.shape[-1],)))


def wblk(W, cols):
    K = W.shape[0]
    sub = W[:, cols].reshape(K // 128, 128, len(cols)).transpose(1, 0, 2)
    return np.ascontiguousarray(sub.reshape(128, -1))


def make_consts():
    cst = np.zeros((128, 708), np.float32)
    cst[:, 0:128] = np.eye(128, dtype=np.float32)
    cst[:, 128:256] = 1.0
    cst[:, 384:512] = 1.0 / 128.0
    k = np.arange(64)[:, None]
    i = np.arange(64)[None, :]
    cst[0:64, 512:576] = (k <= i)
    cst[0:64, 576:640] = np.where(i >= k, 0.0, -1e30)
    cst[0:64, 640:704] = (i > k)
    cst[:, 704] = EPS
    cst[:, 705] = 1.0
    return cst


def prep_shared(c, inp):
    L, H, CC, KC, FCH, FH = c.L, c.H, c.CC, c.KC, c.FCH, c.FH
    cst = make_consts()
    cst[:, 256:384] = 1.0 / c.D
    CW = CC * 128
    GW = H * 128
    c1 = 2 * CW
    c2 = c1 + 3 * GW
    c3 = c2 + GW
    win = np.empty((L, c.NLIN, 128, KC * 256), np.float32)
    wba = np.empty((L, 128, KC * 2 * H), np.float32)
    wout = np.empty((L, c.NLOUT, 128, c.KM * 256), np.float32)
    wup = np.empty((L, c.NLUP, 128, KC * 256), np.float32)
    wdn = np.empty((L, c.NLDN, 128, FH * 128), np.float32)
    prm = np.zeros((L, 128, c.NPL), np.float32)
    ar = np.arange
    for l in range(L):
        Wi = inp["w_in"][l]
        li = 0
        for pair in range(CC // 2):
            win[l, li] = wblk(Wi, CW + pair * 256 + ar(256)); li += 1
            win[l, li] = wblk(Wi, pair * 256 + ar(256)); li += 1
        for ld in range(3 * H // 2):
            win[l, li] = wblk(Wi, c1 + ld * 256 + ar(256)); li += 1
        for ld in range(H // 2):
            win[l, li] = wblk(Wi, c2 + ld * 256 + ar(256)); li += 1
        wba[l] = wblk(Wi, c3 + ar(2 * H))
        Wo = inp["w_out"][l]
        for ld in range(c.NLOUT):
            wout[l, ld] = wblk(Wo, ld * 256 + ar(256))
        Wu = inp["w_up"][l]
        for half in range(2):
            for pr in range(FH // 2):
                for which in range(2):
                    ld = (half * (FH // 2) + pr) * 2 + which
                    col0 = which * c.FFN + (half * FH + pr * 2) * 128
                    wup[l, ld] = wblk(Wu, col0 + ar(256))
        Wd = inp["w_down"][l]
        for half in range(2):
            for n in range(KC):
                wdn[l, half * KC + n] = wblk(Wd[half * FH * 128:(half + 1) * FH * 128], n * 128 + ar(128))

        def put(name, arr):
            o, n = c.P[name]
            prm[l, :, o:o + n] = arr.reshape(128, n)

        def colvec(v):
            return v.reshape(-1, 128).T

        put("g1", colvec(inp["g_pre_mix"][l]))
        put("g2", colvec(inp["g_post_mix"][l]))
        put("g3", colvec(inp["g_pre_ffn"][l]))
        put("g4", colvec(inp["g_post_ffn"][l]))
        put("wdw", inp["w_dw"][l].reshape(KW, CC, 128).transpose(2, 1, 0))
        put("bdw", colvec(inp["b_dw"][l]))
        put("gng", colvec(inp["gn_g"][l]))
        put("gnb", colvec(inp["gn_b"][l]))
        put("wsc", inp["w_sc"][l].reshape(SCW, 3 * H, 128).transpose(2, 1, 0))
        put("on", inp["onorm_g"][l].reshape(128, 1))
        put("wf", inp["w_ffn_dw"][l].reshape(FCW, 2 * FCH, 128).transpose(2, 1, 0))
        put("bf", colvec(inp["b_ffn_dw"][l]))
        put("alog", np.broadcast_to(inp["a_log"][l][None, :], (128, H)))
        put("dtb", np.broadcast_to(inp["dt_bias"][l][None, :], (128, H)))
    return dict(prm=prm, cst=cst, win=win, wba=wba, wout=wout, wup=wup, wdn=wdn)


def prep_core(c, inp, core):
    L = c.L
    d = {}
    d["xp"] = fm(inp["x_prompt"][core])
    xs = inp["x_sample"][2 * core:2 * core + 2]
    d["xs"] = fm(xs.reshape(2 * c.DSEQ, c.D))
    sl = slice(2 * core, 2 * core + 2)
    d["st_conv"] = fm(inp["state_conv"][:, sl]).reshape(L, 2, 128, -1)
    d["st_qkv"] = fm(inp["state_qkv_conv"][:, sl]).reshape(L, 2, 128, -1)
    d["st_ffn"] = fm(inp["state_ffn_conv"][:, sl]).reshape(L, 2, 128, -1)
    sg = inp["state_gdn"][:, sl]
    d["st_gdn"] = np.ascontiguousarray(sg.transpose(0, 1, 3, 2, 4)).reshape(L, 2, 128, -1)
    return d


def run(c, inp, ncores):
    nc, S = build(c)
    shared = prep_shared(c, inp)
    in_maps = []
    for core in range(ncores):
        d = dict(shared)
        d.update(prep_core(c, inp, core))
        in_maps.append(d)
    res = run_bass_kernel_spmd(nc, in_maps, core_ids=list(range(ncores)))
    L, H, CC, FCH = c.L, c.H, c.CC, c.FCH
    B = ncores
    yp = np.empty((B, c.SEQ, c.D), np.float32)
    ys = np.empty((2 * B, c.DSEQ, c.D), np.float32)
    conv = np.empty((L, 3 * B, KW - 1, CC * 128), np.float32)
    qkv = np.empty((L, 3 * B, SCW - 1, 3 * H * 128), np.float32)
    gdn = np.empty((L, 3 * B, H, 128, 128), np.float32)
    ffn = np.empty((L, 3 * B, FCW - 1, 2 * FCH * 128), np.float32)
    for core in range(ncores):
        r = res.results[core]
        yp[core] = unfm(r["yp"])
        ys[2 * core:2 * core + 2] = unfm(r["ys"]).reshape(2, c.DSEQ, c.D)
        for s, bi in ((0, core), (1, B + 2 * core), (2, B + 2 * core + 1)):
            conv[:, bi] = unfm(r["o_conv"][:, s].reshape(L, 128, CC, KW - 1))
            qkv[:, bi] = unfm(r["o_qkv"][:, s].reshape(L, 128, 3 * H, SCW - 1))
            ffn[:, bi] = unfm(r["o_ffn"][:, s].reshape(L, 128, 2 * FCH, FCW - 1))
            gdn[:, bi] = r["o_gdn"][:, s].reshape(L, 128, H, 128).transpose(0, 2, 1, 3)
    return (yp, ys, conv[:, :B], qkv[:, :B], gdn[:, :B], ffn[:, :B],
            conv[:, B:], qkv[:, B:], gdn[:, B:], ffn[:, B:])


def kernel(**inputs):
    inp = {k: np.asarray(v) for k, v in inputs.items()}
    c = Cfg()
    return run(c, inp, 8)
```

```python
import numpy as np
import concourse.bass as bass
import concourse.mybir as mybir
from concourse.bass_utils import run_bass_kernel_spmd

F32 = mybir.dt.float32
BF16 = mybir.dt.bfloat16
ALU = mybir.AluOpType
AF = mybir.ActivationFunctionType
EPS = 1e-6
ENGS = ["pe", "act", "dve", "pool", "sp"]
KW = 31
SCW = 4
FCW = 3
CK = 64


class Cfg:
    def __init__(s, D=2048, CC=8, H=8, FFN=5632, L=4, SEQ=2048, DSEQ=64, PT=(512, 512, 512, 512), NSLOT=4, MERGE=False):
        s.MERGE = MERGE
        s.D, s.CC, s.H, s.FFN, s.L, s.SEQ, s.DSEQ, s.PT, s.NSLOT = D, CC, H, FFN, L, SEQ, DSEQ, tuple(PT), NSLOT
        assert sum(PT) == SEQ and all(p % 64 == 0 for p in PT)
        s.KC = D // 128
        s.FCH = FFN // 128
        s.FH = s.FCH // 2
        s.KM = CC + H
        s.NIN = 2 * CC * 128 + 4 * H * 128 + 2 * H
        s.TW = max(max(PT), (PT[-1] if MERGE else 0) + 2 * DSEQ)
        assert s.TW <= 512
        s.HC = H * CK
        s.HD = H * 128
        o = 0
        s.P = {}
        for name, n in [("g1", s.KC), ("g2", s.KC), ("g3", s.KC), ("g4", s.KC),
                        ("wdw", CC * KW), ("bdw", CC), ("gng", CC), ("gnb", CC),
                        ("wsc", 3 * H * SCW), ("on", 1),
                        ("wf", 2 * s.FCH * FCW), ("bf", 2 * s.FCH),
                        ("alog", H), ("dtb", H)]:
            s.P[name] = (o, n)
            o += n
        s.NPL = o
        s.SLOTW = max(s.KC * 256, s.KM * 256, s.FH * 128)
        s.NLIN = CC + 3 * H // 2 + H // 2
        s.NLOUT = s.KC // 2
        s.NLUP = s.FCH
        s.NLDN = 2 * s.KC


class Sched:
    def __init__(self):
        self.items = {e: [] for e in ENGS}
        self.cnt = {}
        self.known = {e: {} for e in ENGS}
        self.res = {}
        self.region_base = {}
        self.epoch = 0
        self.n_ops = 0

    def new_epoch(self):
        self.epoch += 1

    def _get(self, name):
        r = self.res.get(name)
        if r is None:
            base = {}
            if isinstance(name, tuple) and name[0] in self.region_base:
                base = dict(self.region_base[name[0]])
            r = {"w": None, "r": base}
            self.res[name] = r
        return r

    def fence(self, region):
        base = dict(self.region_base.get(region, {}))
        for name in list(self.res):
            if isinstance(name, tuple) and name[0] == region:
                r = self.res.pop(name)
                if r["w"] is not None:
                    k, v = r["w"]
                    base[k] = max(base.get(k, 0), v)
                for k, v in r["r"].items():
                    base[k] = max(base.get(k, 0), v)
        self.region_base[region] = base

    def op(self, eng, fn, R=(), W=(), dma=None):
        deps = {}

        def add(k, v):
            if v > deps.get(k, 0):
                deps[k] = v

        for n in R:
            r = self._get(n)
            if r["w"] is not None:
                add(*r["w"])
        for n in W:
            r = self._get(n)
            if r["w"] is not None:
                add(*r["w"])
            for k, v in r["r"].items():
                add(k, v)
        waits = []
        kn = self.known[eng]
        for k, v in deps.items():
            if dma is None and eng == "pe" and k[0] == "pe":
                continue
            if kn.get(k, 0) >= v:
                continue
            kn[k] = v
            waits.append((k, v))
        if dma is None:
            key = (eng, self.epoch)
            self.cnt[key] = self.cnt.get(key, 0) + 1
            tok = (key, self.cnt[key])
        else:
            key = ("dma", dma)
            self.cnt[key] = self.cnt.get(key, 0) + 16
            tok = (key, self.cnt[key])
        self.items[eng].append((waits, fn, key, dma is not None))
        for n in R:
            r = self._get(n)
            k, v = tok
            if v > r["r"].get(k, 0):
                r["r"][k] = v
        for n in W:
            r = self._get(n)
            r["w"] = tok
            r["r"] = {}
        self.n_ops += 1
        return tok

    def wait_all(self, eng, keys):
        waits = []
        for k in keys:
            v = self.cnt.get(k, 0)
            if v:
                waits.append((k, v))
        self.items[eng].append((waits, None, None, False))

    def emit(self, nc):
        sems = {}
        for k in self.cnt:
            sems[k] = nc.alloc_semaphore("s_%s_%s" % (k[0], str(k[1])))
        blk_engs = {"pe": "tensor", "act": "scalar", "dve": "vector", "pool": "gpsimd", "sp": "sync"}
        with nc.Block() as block:
            for e in ENGS:
                items = self.items[e]

                def body(engine, items=items):
                    for waits, fn, key, is_dma in items:
                        for k, v in waits:
                            engine.wait_ge(sems[k], v)
                        if fn is None:
                            continue
                        ins = fn(engine)
                        ins.then_inc(sems[key], 16 if is_dma else 1)

                getattr(block, blk_engs[e])(body)


class Seg:
    def __init__(s, seq, col0, n, first, last, tok0, sidx):
        s.seq, s.col0, s.n, s.first, s.last, s.tok0, s.sidx = seq, col0, n, first, last, tok0, sidx


def build(cfg):
    c = cfg
    nc = bass.Bass("TRN2", target_bir_lowering=False)
    S = Sched()
    D, KC, CC, H, L, TW, FCH, FH, KM, HC, HD = c.D, c.KC, c.CC, c.H, c.L, c.TW, c.FCH, c.FH, c.KM, c.HC, c.HD
    NPT = len(c.PT)

    def din(name, shape):
        return nc.dram_tensor(name, list(shape), F32, kind="ExternalInput").ap()

    def dout(name, shape):
        return nc.dram_tensor(name, list(shape), F32, kind="ExternalOutput").ap()

    xp_d = din("xp", [128, KC, c.SEQ])
    xs_d = din("xs", [128, KC, 2 * c.DSEQ])
    sc_d = din("st_conv", [L, 2, 128, CC * 30])
    sq_d = din("st_qkv", [L, 2, 128, 3 * H * 3])
    sg_d = din("st_gdn", [L, 2, 128, H * 128])
    sf_d = din("st_ffn", [L, 2, 128, 2 * FCH * 2])
    prm_d = din("prm", [L, 128, c.NPL])
    cst_d = din("cst", [128, 708])
    win_d = din("win", [L, c.NLIN, 128, KC * 256])
    wba_d = din("wba", [L, 128, KC * 2 * H])
    wout_d = din("wout", [L, c.NLOUT, 128, KM * 256])
    wup_d = din("wup", [L, c.NLUP, 128, KC * 256])
    wdn_d = din("wdn", [L, c.NLDN, 128, FH * 128])

    def dscr(name, shape):
        return nc.dram_tensor(name, list(shape), BF16, kind="Internal").ap()

    scr = {"win": dscr("win_s", [L, c.NLIN, 128, KC * 256]), "wout": dscr("wout_s", [L, c.NLOUT, 128, KM * 256]),
           "wup": dscr("wup_s", [L, c.NLUP, 128, KC * 256]), "wdn": dscr("wdn_s", [L, c.NLDN, 128, FH * 128])}
    wsrc = {"win": win_d, "wout": wout_d, "wup": wup_d, "wdn": wdn_d}
    yp_d = dout("yp", [128, KC, c.SEQ])
    ys_d = dout("ys", [128, KC, 2 * c.DSEQ])
    oc_d = dout("o_conv", [L, 3, 128, CC * 30])
    oq_d = dout("o_qkv", [L, 3, 128, 3 * H * 3])
    og_d = dout("o_gdn", [L, 3, 128, H * 128])
    of_d = dout("o_ffn", [L, 3, 128, 2 * FCH * 2])

    import contextlib
    es = contextlib.ExitStack()

    def sb(name, shape, dt=F32):
        return es.enter_context(nc.sbuf_tensor("sb_" + name, list(shape), dt))

    RA_N = max(10 * HC + 4 * HD, KC * TW, FH * TW // 2 + 2 * (6 + TW) + 4 * TW,
               (90 + TW) + (10 + TW) + (KW + SCW) * 128 + 8 * TW) + 64
    RB_N = max((3 * H + H) * TW // 2, KC * TW)
    x_t = sb("x", [128, KC, TW])
    xh_t = sb("xh", [128, KC, TW], BF16)
    mA_t = sb("mA", [128, CC, TW], BF16)
    RA = sb("RA", [128, RA_N])
    RB = sb("RB", [128, RB_N])
    slots = [sb("slot%d" % i, [128, c.SLOTW], BF16) for i in range(c.NSLOT)]
    wba_t = sb("wba", [128, KC * 2 * H], BF16)
    prm_t = sb("prm", [128, c.NPL])
    cst_t = sb("cst", [128, 708])
    idb_t = sb("idb", [128, 128], BF16)
    stc_P = sb("stc_P", [128, L, CC, 30])
    stq_P = sb("stq_P", [128, L, 3 * H, 3])
    stf_P = sb("stf_P", [128, L, 2 * FCH, 2])
    S_P = sb("S_P", [128, L, H, 128])
    stc_S = sb("stc_S", [128, 2, CC, 30])
    stq_S = sb("stq_S", [128, 2, 3 * H, 3])
    stf_S = sb("stf_S", [128, 2, 2 * FCH, 2])
    S_S = sb("S_S", [128, H, 128])
    Sbf = sb("Sbf", [128, H, 128], BF16)
    rstd_t = sb("rstd", [128, TW])
    sqn_t = sb("sqn", [128, 2, TW])
    NCH = TW // CK
    l1_t = sb("l1", [64, 8, NCH, H])
    l1c_t = sb("l1c", [128, 8, H])
    ps = es.enter_context(nc.psum_tensor("ps", [128, 8 * 512], F32))

    def bank(b, npart=128, n=512, off=0):
        return ps[0:npart, b * 512 + off: b * 512 + off + n]

    identF = cst_t[:, 0:128]
    ones1 = cst_t[:, 128:256]
    onesD = cst_t[:, 256:384]
    ones128 = cst_t[:, 384:512]
    Umask = cst_t[0:64, 512:576]
    negmask = cst_t[0:64, 576:640]
    strict01 = cst_t[0:64, 640:704]
    ident64 = cst_t[0:64, 0:64]

    def P_(name, a=None, b=None):
        o, n = c.P[name]
        if a is None:
            return prm_t[:, o:o + n]
        return prm_t[:, o + a:o + (b if b is not None else a + 1)]

    def m_chunk(j):
        if j < CC:
            return mA_t[:, j, :]
        return xh_t[:, KC - H + (j - CC), :]

    def RAv(off, n, dt=F32):
        if dt == F32:
            return RA[:, off:off + n]
        return RA[:, off:off + n].bitcast(BF16)

    def RBv(off, n, dt=F32):
        if dt == F32:
            return RB[:, off:off + n]
        return RB[:, off:off + n].bitcast(BF16)

    qkv_v = RBv(0, 3 * H * TW // 2, BF16).rearrange("p (c t) -> p c t", t=TW)
    zs_v = RBv(3 * H * TW // 2, H * TW // 2, BF16).rearrange("p (c t) -> p c t", t=TW)
    y2_v = RBv(0, KC * TW).rearrange("p (c t) -> p c t", t=TW)
    y_v = RAv(0, KC * TW).rearrange("p (c t) -> p c t", t=TW)
    a_v = RAv(0, FH * TW // 2, BF16).rearrange("p (c t) -> p c t", t=TW)
    o = 0
    ubuf_v = RAv(o, 90 + TW, BF16).rearrange("p (a t) -> p a t", a=2); o += 90 + TW
    cbuf_v = RAv(o, 10 + TW, BF16).rearrange("p (a t) -> p a t", a=2); o += 10 + TW
    dg31_v = RAv(o, KW * 128, BF16).rearrange("p (a j c) -> p a j c", a=2, j=KW); o += KW * 128
    dg4_v = RAv(o, SCW * 128, BF16).rearrange("p (a j c) -> p a j c", a=2, j=SCW); o += SCW * 128
    acc_v = RAv(o, 2 * TW).rearrange("p (a t) -> p a t", a=2); o += 2 * TW
    sg_v = RAv(o, 2 * TW).rearrange("p (a t) -> p a t", a=2); o += 2 * TW
    mu_v = RAv(o, TW); o += TW
    var_v = RAv(o, TW); o += TW
    cen_v = RAv(o, TW); o += TW
    sqA_v = RAv(o, TW); o += TW
    assert o <= RA_N, (o, RA_N)
    o = FH * TW // 2
    fbuf_v = RAv(o, 2 * (6 + TW)).rearrange("p (a t) -> p a t", a=2); o += 2 * (6 + TW)
    accf_v = RAv(o, 2 * TW).rearrange("p (a t) -> p a t", a=2); o += 2 * TW
    sgate_v = RAv(o, 2 * TW).rearrange("p (a t) -> p a t", a=2); o += 2 * TW

    def gslot(i, npart=128):
        return RA[0:npart, i * HC:(i + 1) * HC].rearrange("p (h t) -> p h t", h=H)

    def gslot_bf(i):
        return RA[:, i * HC:(i + 1) * HC].bitcast(BF16).rearrange("p (a h t) -> p a h t", a=2, h=H)

    def gbig(i):
        o_ = 10 * HC + i * HD
        return RA[0:64, o_:o_ + HD // 2].bitcast(BF16).rearrange("p (h d) -> p h d", h=H)

    def gslot_h(i, npart=128):
        return RA[0:npart, i * HC:i * HC + HC // 2].bitcast(BF16).rearrange("p (h t) -> p h t", h=H)

    tiles = []
    tok = 0
    for t, pw in enumerate(c.PT):
        segs = [Seg("P", 0, pw, t == 0, t == NPT - 1, tok, 0)]
        wt = pw
        if t == NPT - 1 and c.MERGE:
            segs.append(Seg("A", pw, c.DSEQ, True, True, 0, 1))
            segs.append(Seg("B", pw + c.DSEQ, c.DSEQ, True, True, 0, 2))
            wt = pw + 2 * c.DSEQ
        tiles.append(dict(w=wt, segs=segs))
        tok += pw
    if not c.MERGE:
        tiles.append(dict(w=2 * c.DSEQ, segs=[Seg("A", 0, c.DSEQ, True, True, 0, 1),
                                              Seg("B", c.DSEQ, c.DSEQ, True, True, 0, 2)]))

    QKV = ("RB", "qkv")
    ZS = ("RB", "zs")
    ring = {"i": 0}

    cur = {"ti": 0}

    def wload(kind, l, idx, ncols):
        i = ring["i"] % c.NSLOT
        ring["i"] += 1
        slot = slots[i]
        if cur["ti"] == 0:
            src_ap = wsrc[kind][l, idx]
            S.op("pool", lambda e, slot=slot, src_ap=src_ap, ncols=ncols:
                 e.dma_start(out=slot[:, 0:ncols], in_=src_ap),
                 W=[("slot", i)], dma="w%d" % i)
            dst_ap = scr[kind][l, idx]
            S.op("sp", lambda e, slot=slot, dst_ap=dst_ap, ncols=ncols:
                 e.dma_start(out=dst_ap, in_=slot[:, 0:ncols]),
                 R=[("slot", i)], W=[("scr", kind, l, idx)], dma="ws%d" % i)
        else:
            src_ap = scr[kind][l, idx]
            S.op("sp", lambda e, slot=slot, src_ap=src_ap, ncols=ncols:
                 e.dma_start(out=slot[:, 0:ncols], in_=src_ap),
                 R=[("scr", kind, l, idx)], W=[("slot", i)], dma="w%d" % i)
        return i, slot

    mmrot = {"i": 0}

    def next_mm_bank():
        b = 1 + (mmrot["i"] % 3)
        mmrot["i"] += 1
        return b

    def mm_group(b, slot_i, slot, nk, ncols_per_k, col_off, rhs_fn, w, rhs_res, start=True, stop=True, m=128):
        def fn(e):
            ins = None
            for k in range(nk):
                ins = e.matmul(bank(b, m, w), lhsT=slot[:, k * ncols_per_k + col_off:k * ncols_per_k + col_off + m],
                               rhs=rhs_fn(k), start=(start and k == 0), stop=(stop and k == nk - 1))
            return ins
        S.op("pe", fn, R=[("slot", slot_i)] + list(rhs_res), W=[("bank", b)])

    def st_conv(seg, l):
        return (stc_P[:, l], ("stc", "P", l)) if seg.seq == "P" else (stc_S[:, seg.sidx - 1], ("stc", seg.seq))

    def st_qkv(seg, l):
        return (stq_P[:, l], ("stq", "P", l)) if seg.seq == "P" else (stq_S[:, seg.sidx - 1], ("stq", seg.seq))

    def st_ffn(seg, l):
        return (stf_P[:, l], ("stf", "P", l)) if seg.seq == "P" else (stf_S[:, seg.sidx - 1], ("stf", seg.seq))

    def st_S(seg, l):
        return (S_P[:, l], ("S", "P", l)) if seg.seq == "P" else (S_S[:, :, :], ("S", "S"))

    def rsqrt_eps(out_ap, in_ap, R, W):
        npart = out_ap.shape[0]
        S.op("act", lambda e: e.activation(out=out_ap, in_=in_ap, func=AF.Ln, bias=cst_t[0:npart, 704:705]), R=list(R) + ["cst"], W=W)
        S.op("act", lambda e: e.activation(out=out_ap, in_=out_ap, func=AF.Exp, scale=-0.5), R=[], W=W)

    S.op("sp", lambda e: e.dma_start(out=cst_t[:, :], in_=cst_d), W=["cst"], dma="cst")
    S.op("dve", lambda e: e.tensor_copy(out=idb_t[:, :], in_=identF), R=["cst"], W=["idb"])
    S.op("dve", lambda e: e.memset(stc_P[:, :, :, :].rearrange("p a b c -> p (a b c)"), 0.0), W=[("stc", "P", l) for l in range(L)])
    S.op("dve", lambda e: e.memset(stq_P[:, :, :, :].rearrange("p a b c -> p (a b c)"), 0.0), W=[("stq", "P", l) for l in range(L)])
    S.op("dve", lambda e: e.memset(stf_P[:, :, :, :].rearrange("p a b c -> p (a b c)"), 0.0), W=[("stf", "P", l) for l in range(L)])
    S.op("dve", lambda e: e.memset(S_P[:, :, :, :].rearrange("p a b c -> p (a b c)"), 0.0), W=[("S", "P", l) for l in range(L)])

    def rmsnorm_to_xh(w, gname):
        for k in range(KC):
            par = k % 2
            S.op("act", lambda e, k=k, par=par: e.activation(out=sqn_t[:, par, 0:w], in_=x_t[:, k, 0:w], func=AF.Square),
                 R=[("x", k)], W=[("sqn", par)])
            S.op("pe", lambda e, k=k, par=par: e.matmul(bank(0, 128, w), lhsT=onesD, rhs=sqn_t[:, par, 0:w],
                                                          start=(k == 0), stop=(k == KC - 1)),
                 R=[("sqn", par), "cst"], W=[("bank", 0)])
        rsqrt_eps(rstd_t[:, 0:w], bank(0, 128, w), [("bank", 0)], ["rstd"])
        for k in range(KC):
            S.op("dve", lambda e, k=k: e.scalar_tensor_tensor(out=xh_t[:, k, 0:w], in0=x_t[:, k, 0:w],
                                                                scalar=P_(gname, k), in1=rstd_t[:, 0:w],
                                                                op0=ALU.mult, op1=ALU.mult),
                 R=[("x", k), "rstd", "prm"], W=["xh"])

    def residual_epilogue(w, yv, yres, gname):
        rsqrt_eps(rstd_t[:, 0:w], bank(0, 128, w), [("bank", 0)], ["rstd"])
        for k in range(KC):
            S.op("dve", lambda e, k=k: e.scalar_tensor_tensor(out=yv[:, k, 0:w], in0=yv[:, k, 0:w],
                                                                scalar=P_(gname, k), in1=rstd_t[:, 0:w],
                                                                op0=ALU.mult, op1=ALU.mult),
                 R=["rstd", "prm"], W=[(yres, k)])
            S.op("dve", lambda e, k=k: e.tensor_tensor(out=x_t[:, k, 0:w], in0=x_t[:, k, 0:w], in1=yv[:, k, 0:w],
                                                         op=ALU.add),
                 R=[(yres, k)], W=[("x", k)])

    def out_block_epilogue(b, w, yv, yres, n, first, last):
        par = n % 2
        S.op("act", lambda e: e.activation(out=yv[:, n, 0:w], in_=bank(b, 128, w), func=AF.Copy),
             R=[("bank", b)], W=[(yres, n)])
        S.op("act", lambda e: e.activation(out=sqn_t[:, par, 0:w], in_=bank(b, 128, w), func=AF.Square),
             R=[("bank", b)], W=[("sqn", par)])
        return lambda: S.op("pe", lambda e: e.matmul(bank(0, 128, w), lhsT=onesD, rhs=sqn_t[:, par, 0:w], start=first, stop=last),
                            R=[("sqn", par), "cst"], W=[("bank", 0)])

    def gdn_chunk(l, seg, ci, c0, w):
        Sv, Sres = st_S(seg, l)
        G = ("RA",)

        def g(n):
            return ("RA", n)

        qv = qkv_v[:, 0:H, c0:c0 + CK]
        kv = qkv_v[:, H:2 * H, c0:c0 + CK]
        vv = qkv_v[:, 2 * H:3 * H, c0:c0 + CK]
        s0, s2, s3, s4 = [gslot(i) for i in (0, 2, 3, 4)]
        s5 = gslot_h(5)
        s4h = gslot_h(4)
        S.op("act", lambda e: e.activation(out=Sbf[:, :, :], in_=Sv, func=AF.Copy), R=[Sres], W=["Sbf"])
        qn = qv
        kn = kv
        kbg, kd, vb, vnew = gbig(0), gbig(1), gbig(2), gbig(3)
        bkf = lambda b, npart=128: bank(b, npart, HC).rearrange("p (h t) -> p h t", h=H)
        gL = l1_t[:, 1, ci, :]
        bL = l1_t[:, 2, ci, :]
        G_sb = l1c_t[0:64, 0, :]
        eG = l1c_t[0:64, 1, :]
        bEG = l1c_t[0:64, 2, :]
        kdsc = l1c_t[0:64, 3, :]
        gl128 = l1c_t[:, 4, :]
        dGl = l1c_t[0:64, 5, :]

        b1bf = bank(1, 64, HD // 2).bitcast(BF16).rearrange("p (h d) -> p h d", h=H)
        b2bf = bank(2, 64, HD // 2).bitcast(BF16).rearrange("p (h d) -> p h d", h=H)

        def tr_fn(dst, src):
            def fn(e):
                ins = None
                for h in range(H):
                    ins = e.transpose(dst[:, h, :], src[:, h, :], idb_t[:, :])
                return ins
            return fn
        S.op("pe", tr_fn(b1bf, kn), R=[("RB", "qn", ci), "idb"], W=[("bank", 1)])
        S.op("pe", tr_fn(b2bf, vv), R=[QKV, "idb"], W=[("bank", 2)])
        b7 = bank(7, 128, 2 * H)

        def l1mm(e):
            e.matmul(b7[0:64, 0:H], lhsT=Umask, rhs=gL, start=True, stop=True)
            return e.matmul(b7[:, H:2 * H], lhsT=ones1[0:64, :], rhs=gL, start=True, stop=True)
        S.op("pe", l1mm, R=["l1", "cst"], W=[("bank", 7)])
        S.op("act", lambda e: e.activation(out=G_sb, in_=b7[0:64, 0:H], func=AF.Copy), R=[("bank", 7)], W=["l1c0"])
        S.op("act", lambda e: e.activation(out=eG, in_=b7[0:64, 0:H], func=AF.Exp), R=[("bank", 7)], W=["l1c1"])
        S.op("act", lambda e: e.activation(out=gl128, in_=b7[:, H:2 * H], func=AF.Exp), R=[("bank", 7)], W=["l1c4"])
        S.op("dve", lambda e: e.tensor_tensor(out=dGl, in0=b7[0:64, H:2 * H], in1=G_sb, op=ALU.subtract),
             R=[("bank", 7), "l1c0"], W=["l1c5"])
        S.op("act", lambda e: e.activation(out=kdsc, in_=dGl, func=AF.Exp), R=["l1c5"], W=["l1c3"])
        S.op("dve", lambda e: e.tensor_tensor(out=bEG, in0=bL, in1=eG, op=ALU.mult), R=["l1", "l1c1"], W=["l1c2"])
        s2_64, s3_64 = gslot(2, 64), gslot(3, 64)
        S.op("dve", lambda e: e.tensor_tensor(out=s2_64, in0=Umask.unsqueeze(1).to_broadcast([64, H, CK]),
                                              in1=gL.unsqueeze(2).to_broadcast([64, H, CK]), op=ALU.mult),
             R=["l1", "cst"], W=[g("s2")])
        S.op("pe", lambda e: e.matmul(bank(5, 128, HC), lhsT=ones1[0:64, :], rhs=s2_64.rearrange("p h t -> p (h t)"),
                                      start=True, stop=True), R=[g("s2"), "cst"], W=[("bank", 5)])
        S.op("dve", lambda e: e.tensor_tensor(out=s3_64, in0=ident64.unsqueeze(1).to_broadcast([64, H, CK]),
                                              in1=bL.unsqueeze(2).to_broadcast([64, H, CK]), op=ALU.mult),
             R=["l1", "cst"], W=[g("s3")])
        S.op("pe", lambda e: e.matmul(bank(6, 64, HC), lhsT=ones1[0:64, 0:64], rhs=s3_64.rearrange("p h t -> p (h t)"),
                                      start=True, stop=True), R=[g("s3"), "cst"], W=[("bank", 6)])
        S.op("dve", lambda e: e.tensor_tensor(out=s2_64, in0=bkf(5, 64), in1=G_sb.unsqueeze(2).to_broadcast([64, H, CK]),
                                              op=ALU.subtract), R=[("bank", 5), "l1c0"], W=[g("s2")])
        S.op("dve", lambda e: e.tensor_tensor(out=s2_64, in0=s2_64, in1=negmask.unsqueeze(1).to_broadcast([64, H, CK]),
                                              op=ALU.add), R=["cst"], W=[g("s2")])
        S.op("act", lambda e: e.activation(out=s3_64, in_=s2_64, func=AF.Exp), R=[g("s2")], W=[g("s3")])
        S.op("act", lambda e: e.activation(out=s4, in_=bkf(5), func=AF.Exp), R=[("bank", 5)], W=[g("s4")])
        S.op("dve", lambda e: e.tensor_tensor(out=s5, in0=qn, in1=s4, op=ALU.mult), R=[("RB", "qn", ci), g("s4")], W=[g("s5")])
        def kkfn(e):
            ins = None
            for h in range(H):
                ins = e.matmul(bank(3, 64, CK, h * CK), lhsT=kn[:, h, :], rhs=kn[:, h, :], start=True, stop=True)
            return ins

        def qkfn(e):
            ins = None
            for h in range(H):
                ins = e.matmul(bank(4, 64, CK, h * CK), lhsT=kn[:, h, :], rhs=qn[:, h, :], start=True, stop=True)
            return ins
        S.op("pe", kkfn, R=[("RB", "qn", ci)], W=[("bank", 3)])
        S.op("pe", qkfn, R=[("RB", "qn", ci)], W=[("bank", 4)])
        s6_64, s7_64, s8_64, s9_64 = gslot_h(6, 64), gslot_h(7, 64), gslot_h(8, 64), gslot_h(9, 64)
        b6bf = bank(6, 64, HC // 2).bitcast(BF16).rearrange("p (h t) -> p h t", h=H)
        S.op("dve", lambda e: e.tensor_tensor(out=s6_64, in0=bkf(4, 64), in1=s3_64, op=ALU.mult),
             R=[("bank", 4), g("s3")], W=[g("s6")])
        S.op("dve", lambda e: e.tensor_tensor(out=s2_64, in0=s3_64, in1=strict01.unsqueeze(1).to_broadcast([64, H, CK]),
                                              op=ALU.mult), R=[g("s3"), "cst"], W=[g("s2")])
        S.op("dve", lambda e: e.tensor_tensor(out=s2_64, in0=s2_64, in1=bkf(6, 64), op=ALU.mult),
             R=[("bank", 6)], W=[g("s2")])
        S.op("dve", lambda e: e.scalar_tensor_tensor(out=s7_64, in0=bkf(3, 64), scalar=-1.0, in1=s2_64,
                                                     op0=ALU.mult, op1=ALU.mult),
             R=[("bank", 3), g("s2")], W=[g("s7")])
        def ptfn(e):
            ins = None
            for h in range(H):
                ins = e.transpose(b6bf[:, h, :], s7_64[:, h, :], idb_t[0:64, 0:64])
            return ins
        S.op("pe", ptfn, R=[g("s7"), "idb"], W=[("bank", 6)])
        S.op("act", lambda e: e.activation(out=s8_64, in_=b6bf, func=AF.Copy), R=[("bank", 6)], W=[g("s8")])
        S.op("dve", lambda e: e.tensor_tensor(out=s9_64, in0=s7_64, in1=ident64.unsqueeze(1).to_broadcast([64, H, CK]),
                                              op=ALU.add), R=[g("s7"), "cst"], W=[g("s9")])
        S.op("dve", lambda e: e.tensor_tensor(out=kbg, in0=b1bf, in1=bEG.unsqueeze(2).to_broadcast([64, H, 128]), op=ALU.mult),
             R=[("bank", 1), "l1c2"], W=[g("kbg")])
        S.op("dve", lambda e: e.tensor_tensor(out=kd, in0=b1bf, in1=kdsc.unsqueeze(2).to_broadcast([64, H, 128]), op=ALU.mult),
             R=[("bank", 1), "l1c3"], W=[g("kd")])
        S.op("dve", lambda e: e.tensor_tensor(out=vb, in0=b2bf, in1=bL.unsqueeze(2).to_broadcast([64, H, 128]), op=ALU.mult),
             R=[("bank", 2), "l1"], W=[g("vb")])
        for kk in range(1, 6):
            def sqfn(e, kk=kk):
                ins = None
                for h in range(H):
                    if kk < 5:
                        e.matmul(bank(3, 64, CK, h * CK), lhsT=s8_64[:, h, :], rhs=s7_64[:, h, :], start=True, stop=True)
                    ins = e.matmul(bank(4, 64, CK, h * CK), lhsT=s7_64[:, h, :], rhs=s8_64[:, h, :], start=True, stop=True)
                return ins
            S.op("pe", sqfn, R=[g("s7"), g("s8")], W=[("bank", 3), ("bank", 4)])
            if kk < 5:
                S.op("act", lambda e: e.activation(out=s7_64, in_=bkf(3, 64), func=AF.Copy), R=[("bank", 3)], W=[g("s7")])
            S.op("dve", lambda e: e.tensor_copy(out=s8_64, in_=bkf(4, 64)), R=[("bank", 4)], W=[g("s8")])

            def xfn(e):
                ins = None
                for h in range(H):
                    ins = e.matmul(bank(5, 64, CK, h * CK), lhsT=s8_64[:, h, :], rhs=s9_64[:, h, :], start=True, stop=True)
                return ins
            S.op("pe", xfn, R=[g("s8"), g("s9")], W=[("bank", 5)])
            S.op("dve", lambda e: e.tensor_tensor(out=s9_64, in0=s9_64, in1=bkf(5, 64), op=ALU.add),
                 R=[("bank", 5)], W=[g("s9")])
        def wtfn(e):
            ins = None
            for h in range(H):
                ins = e.matmul(bank(1, 128, CK, h * CK), lhsT=kbg[:, h, :], rhs=s9_64[:, h, :], start=True, stop=True)
            return ins
        S.op("pe", wtfn, R=[g("kbg"), g("s9")], W=[("bank", 1)])
        S.op("act", lambda e: e.activation(out=s4h, in_=bkf(1), func=AF.Copy, scale=-1.0), R=[("bank", 1)], W=[g("s4")])
        vps = ps[0:64, 6 * 512:6 * 512 + HD].rearrange("p (h d) -> p h d", h=H)
        sps = ps[:, 6 * 512:6 * 512 + HD].rearrange("p (h d) -> p h d", h=H)

        def vnfn(e):
            ins = None
            for h in range(H):
                e.matmul(vps[:, h, :], lhsT=s9_64[:, h, :], rhs=vb[:, h, :], start=True, stop=False)
                ins = e.matmul(vps[:, h, :], lhsT=s4h[:, h, :], rhs=Sbf[:, h, :], start=False, stop=True)
            return ins
        S.op("pe", vnfn, R=[g("s9"), g("vb"), g("s4"), "Sbf"], W=[("bank", 6), ("bank", 7)])
        S.op("act", lambda e: e.activation(out=vnew, in_=vps, func=AF.Copy), R=[("bank", 6), ("bank", 7)], W=[g("vnew")])

        def otfn(e):
            ins = None
            for h in range(H):
                e.matmul(bank(2, 128, CK, h * CK), lhsT=Sbf[:, h, :], rhs=s5[:, h, :], start=True, stop=False)
                ins = e.matmul(bank(2, 128, CK, h * CK), lhsT=vnew[:, h, :], rhs=s6_64[:, h, :], start=False, stop=True)
            return ins
        S.op("pe", otfn, R=["Sbf", g("s5"), g("vnew"), g("s6")], W=[("bank", 2)])

        def supfn(e):
            ins = None
            for h in range(H):
                ins = e.matmul(sps[:, h, :], lhsT=kd[:, h, :], rhs=vnew[:, h, :], start=True, stop=True)
            return ins
        S.op("pe", supfn, R=[g("kd"), g("vnew")], W=[("bank", 6), ("bank", 7)])
        S.op("dve", lambda e: e.tensor_tensor(out=Sv, in0=Sv, in1=gl128.unsqueeze(2).to_broadcast([128, H, 128]), op=ALU.mult),
             R=["l1c4"], W=[Sres])
        S.op("dve", lambda e: e.tensor_tensor(out=Sv, in0=Sv, in1=sps, op=ALU.add),
             R=[("bank", 6), ("bank", 7)], W=[Sres])
        S.op("act", lambda e: e.activation(out=s3, in_=bkf(2), func=AF.Copy), R=[("bank", 2)], W=[g("s3")])
        S.op("act", lambda e: e.activation(out=s0, in_=bkf(2), func=AF.Square), R=[("bank", 2)], W=[g("s0")])
        S.op("pe", lambda e: e.matmul(bank(0, 128, HC), lhsT=ones128, rhs=s0.rearrange("p h t -> p (h t)"),
                                      start=True, stop=True), R=[g("s0"), "cst"], W=[("bank", 0)])
        rsqrt_eps(s2, bkf(0), [("bank", 0)], [g("s2")])
        S.op("dve", lambda e: e.tensor_tensor(out=s3, in0=s3, in1=s2, op=ALU.mult), R=[g("s2")], W=[g("s3")])
        mg = xh_t[:, KC - H:KC, c0:c0 + CK]
        S.op("dve", lambda e: e.scalar_tensor_tensor(out=mg, in0=s3, scalar=P_("on", 0), in1=zs_v[:, :, c0:c0 + CK],
                                                     op0=ALU.mult, op1=ALU.mult),
             R=[g("s3"), "prm", ZS], W=["xh"])

    def l2norm_tile(w):
        for ci in range(w // CK):
            c0 = ci * CK
            pr = ci % 2
            sA = gslot(0) if pr == 0 else gslot(3)
            sB = gslot(2) if pr == 0 else gslot(4)
            bk = 0 if pr == 0 else 3
            nA, nB = (("RA", "s0"), ("RA", "s2")) if pr == 0 else (("RA", "s3"), ("RA", "s4"))
            for which, scl in ((0, 128.0 ** -0.5), (1, 1.0)):
                src = qkv_v[:, which * H:(which + 1) * H, c0:c0 + CK]
                S.op("act", lambda e, src=src, sA=sA: e.activation(out=sA, in_=src, func=AF.Square), R=[QKV], W=[nA])
                S.op("pe", lambda e, sA=sA, bk=bk: e.matmul(bank(bk, 128, HC), lhsT=ones1, rhs=sA.rearrange("p h t -> p (h t)"),
                                                           start=True, stop=True), R=[nA, "cst"], W=[("bank", bk)])
                rsqrt_eps(sB, bank(bk, 128, HC).rearrange("p (h t) -> p h t", h=H), [("bank", bk)], [nB])
                S.op("dve", lambda e, src=src, sB=sB, scl=scl: e.scalar_tensor_tensor(
                    out=src, in0=src, scalar=scl, in1=sB, op0=ALU.mult, op1=ALU.mult), R=[nB], W=[("RB", "qn", ci)])

    def layer(l, tile):
        w = tile["w"]
        segs = tile["segs"]
        grp = (len(segs) == 2 and all(sg_.seq != "P" for sg_ in segs) and segs[0].n == segs[1].n
               and segs[1].col0 == segs[0].col0 + segs[0].n)
        gn = segs[0].n
        gc0 = segs[0].col0
        S.new_epoch()
        S.op("sp", lambda e: e.dma_start(out=prm_t[:, :], in_=prm_d[l]), W=["prm"], dma="prm")
        S.op("pool", lambda e: e.dma_start(out=wba_t[:, :], in_=wba_d[l]), W=["wba"], dma="wba")
        for seg in segs:
            if seg.seq != "P":
                b = seg.sidx - 1
                S.op("sp", lambda e, b=b: e.dma_start(out=stc_S[:, b].rearrange("p c j -> p (c j)"), in_=sc_d[l, b]),
                     W=[("stc", seg.seq)], dma="stc" + seg.seq)
                S.op("sp", lambda e, b=b: e.dma_start(out=stq_S[:, b].rearrange("p c j -> p (c j)"), in_=sq_d[l, b]),
                     W=[("stq", seg.seq)], dma="stq" + seg.seq)
                S.op("sp", lambda e, b=b: e.dma_start(out=stf_S[:, b].rearrange("p c j -> p (c j)"), in_=sf_d[l, b]),
                     W=[("stf", seg.seq)], dma="stf" + seg.seq)
        S.fence("RA")
        S.fence("RB")
        rmsnorm_to_xh(w, "g1")
        pend = {"cc": None, "p2": None}

        def conv_ln(cc, par):
            av = acc_v[:, par, 0:w]
            S.op("act", lambda e, av=av: e.activation(out=sqA_v[:, 0:w], in_=av, func=AF.Square),
                 R=[("RA", "acc", par)], W=[("RA", "sqA")])
            S.op("pe", lambda e, av=av: e.matmul(bank(0, 128, w), lhsT=ones128, rhs=av, start=True, stop=True),
                 R=[("RA", "acc", par), "cst"], W=[("bank", 0)])
            S.op("pe", lambda e: e.matmul(bank(4, 128, w), lhsT=ones128, rhs=sqA_v[:, 0:w], start=True, stop=True),
                 R=[("RA", "sqA"), "cst"], W=[("bank", 4)])
            S.op("act", lambda e: e.activation(out=mu_v[:, 0:w], in_=bank(0, 128, w), func=AF.Copy),
                 R=[("bank", 0)], W=[("RA", "mu")])
            S.op("dve", lambda e: e.tensor_tensor(out=var_v[:, 0:w], in0=mu_v[:, 0:w], in1=mu_v[:, 0:w], op=ALU.mult),
                 R=[("RA", "mu")], W=[("RA", "var")])
            S.op("dve", lambda e: e.tensor_tensor(out=var_v[:, 0:w], in0=bank(4, 128, w), in1=var_v[:, 0:w], op=ALU.subtract),
                 R=[("bank", 4)], W=[("RA", "var")])
            rsqrt_eps(var_v[:, 0:w], var_v[:, 0:w], [], [("RA", "var")])
            S.op("dve", lambda e, av=av: e.tensor_tensor(out=cen_v[:, 0:w], in0=av, in1=mu_v[:, 0:w], op=ALU.subtract),
                 R=[("RA", "acc", par), ("RA", "mu")], W=[("RA", "cen")])
            S.op("dve", lambda e: e.tensor_tensor(out=cen_v[:, 0:w], in0=cen_v[:, 0:w], in1=var_v[:, 0:w], op=ALU.mult),
                 R=[("RA", "var")], W=[("RA", "cen")])
            S.op("act", lambda e, cc=cc: e.activation(out=mA_t[:, cc, 0:w], in_=cen_v[:, 0:w], func=AF.Silu,
                                                      bias=P_("gnb", cc), scale=P_("gng", cc)),
                 R=[("RA", "cen"), "prm"], W=["mA"])

        li = 0
        for pair in range(CC // 2):
            si, slot = wload("win", l, li, KC * 256); li += 1
            for j in range(2):
                b = next_mm_bank()
                mm_group(b, si, slot, KC, 256, j * 128, lambda k: xh_t[:, k, 0:w], w, ["xh"])
                S.op("act", lambda e, b=b, j=j: e.activation(out=sg_v[:, j, 0:w], in_=bank(b, 128, w), func=AF.Sigmoid),
                     R=[("bank", b)], W=[("RA", "sg", j)])
            si, slot = wload("win", l, li, KC * 256); li += 1
            for j in range(2):
                cc = pair * 2 + j
                b = next_mm_bank()
                mm_group(b, si, slot, KC, 256, j * 128, lambda k: xh_t[:, k, 0:w], w, ["xh"])
                par = cc % 2
                cb_ = 6 + par
                S.op("dve", lambda e, cc=cc, par=par: e.tensor_tensor(
                    out=dg31_v[:, par], in0=idb_t[:, :].unsqueeze(1).to_broadcast([128, KW, 128]),
                    in1=P_("wdw", cc * KW, cc * KW + KW).unsqueeze(2).to_broadcast([128, KW, 128]), op=ALU.mult),
                    R=["idb", "prm"], W=[("RA", "dg31", par)])
                if grp:
                    UB = ubuf_v[:, par, 0:2 * (30 + gn)].rearrange("p (s t) -> p s t", t=30 + gn)
                    STc = stc_S[:, :, cc, :]
                    BKc = bank(b, 128, 2 * gn, gc0).rearrange("p (s t) -> p s t", t=gn)
                    SGc = sg_v[:, j, gc0:gc0 + 2 * gn].rearrange("p (s t) -> p s t", t=gn)
                    sres = [("stc", "A"), ("stc", "B")]
                    S.op("act", lambda e, UB=UB, STc=STc: e.activation(out=UB[:, :, 0:30], in_=STc, func=AF.Copy),
                         R=sres, W=[("RA", "ubuf", par)])
                    S.op("dve", lambda e, UB=UB, BKc=BKc, SGc=SGc: e.tensor_tensor(out=UB[:, :, 30:30 + gn], in0=BKc, in1=SGc, op=ALU.mult),
                         R=[("bank", b), ("RA", "sg", j)], W=[("RA", "ubuf", par)])
                    S.op("dve", lambda e, STc=STc, BKc=BKc, SGc=SGc: e.tensor_tensor(
                        out=STc, in0=BKc[:, :, gn - 30:gn], in1=SGc[:, :, gn - 30:gn], op=ALU.mult),
                        R=[("bank", b), ("RA", "sg", j), ("RA", "ubuf", par)], W=sres)
                    cfns = []
                    off = 0
                    for seg in segs:
                        n = seg.n
                        ub = ubuf_v[:, par, off:off + 30 + n]

                        def convfn(e, ub=ub, par=par, cb_=cb_, seg=seg, n=n):
                            ins = None
                            for tp in range(KW):
                                ins = e.matmul(bank(cb_, 128, n, seg.col0), lhsT=dg31_v[:, par, tp, :], rhs=ub[:, tp:tp + n],
                                               start=(tp == 0), stop=(tp == KW - 1))
                            return ins
                        cfns.append(convfn)
                        off += 30 + n
                else:
                    off = 0
                    cfns = []
                    for seg in segs:
                        stv, stres = st_conv(seg, l)
                        n = seg.n
                        assert n >= 30
                        ub = ubuf_v[:, par, off:off + 30 + n]
                        S.op("act", lambda e, ub=ub, stv=stv, cc=cc: e.activation(out=ub[:, 0:30], in_=stv[:, cc, :], func=AF.Copy),
                             R=[stres], W=[("RA", "ubuf", par)])
                        S.op("dve", lambda e, ub=ub, b=b, j=j, seg=seg, n=n: e.tensor_tensor(
                            out=ub[:, 30:30 + n], in0=bank(b, 128, n, seg.col0), in1=sg_v[:, j, seg.col0:seg.col0 + n], op=ALU.mult),
                            R=[("bank", b), ("RA", "sg", j)], W=[("RA", "ubuf", par)])
                        S.op("dve", lambda e, stv=stv, cc=cc, b=b, j=j, seg=seg, n=n: e.tensor_tensor(
                            out=stv[:, cc, :], in0=bank(b, 128, 30, seg.col0 + n - 30),
                            in1=sg_v[:, j, seg.col0 + n - 30:seg.col0 + n], op=ALU.mult),
                            R=[("bank", b), ("RA", "sg", j), ("RA", "ubuf", par)], W=[stres])

                        def convfn(e, ub=ub, par=par, cb_=cb_, seg=seg, n=n):
                            ins = None
                            for tp in range(KW):
                                ins = e.matmul(bank(cb_, 128, n, seg.col0), lhsT=dg31_v[:, par, tp, :], rhs=ub[:, tp:tp + n],
                                               start=(tp == 0), stop=(tp == KW - 1))
                            return ins
                        cfns.append(convfn)
                        off += 30 + n

                def part2(cfns=cfns, par=par, cb_=cb_, cc=cc):
                    for f in cfns:
                        S.op("pe", f, R=[("RA", "ubuf", par), ("RA", "dg31", par)], W=[("bank", cb_)])
                    S.op("act", lambda e: e.activation(out=acc_v[:, par, 0:w], in_=bank(cb_, 128, w),
                                                       func=AF.Identity, bias=P_("bdw", cc)),
                         R=[("bank", cb_), "prm"], W=[("RA", "acc", par)])
                if pend["p2"] is not None:
                    pend["p2"][0]()
                    if pend["cc"] is not None:
                        conv_ln(*pend["cc"])
                    pend["cc"] = pend["p2"][1]
                pend["p2"] = (part2, (cc, par))
        pend["p2"][0]()
        if pend["cc"] is not None:
            conv_ln(*pend["cc"])
        conv_ln(*pend["p2"][1])
        pq = {"f": None}
        for ld in range(3 * H // 2):
            si, slot = wload("win", l, li, KC * 256); li += 1
            for j in range(2):
                i = ld * 2 + j
                b = next_mm_bank()
                mm_group(b, si, slot, KC, 256, j * 128, lambda k: xh_t[:, k, 0:w], w, ["xh"])
                par = i % 2
                cb_ = 6 + par
                S.op("dve", lambda e, i=i, par=par: e.tensor_tensor(
                    out=dg4_v[:, par], in0=idb_t[:, :].unsqueeze(1).to_broadcast([128, SCW, 128]),
                    in1=P_("wsc", i * SCW, i * SCW + SCW).unsqueeze(2).to_broadcast([128, SCW, 128]), op=ALU.mult),
                    R=["idb", "prm"], W=[("RA", "dg4", par)])
                if grp:
                    CBg = cbuf_v[:, par, 0:2 * (3 + gn)].rearrange("p (s t) -> p s t", t=3 + gn)
                    STq = stq_S[:, :, i, :]
                    BKq = bank(b, 128, 2 * gn, gc0).rearrange("p (s t) -> p s t", t=gn)
                    sres = [("stq", "A"), ("stq", "B")]
                    S.op("act", lambda e, CBg=CBg, STq=STq: e.activation(out=CBg[:, :, 0:3], in_=STq, func=AF.Copy),
                         R=sres, W=[("RA", "cbuf", par)])
                    S.op("act", lambda e, CBg=CBg, BKq=BKq: e.activation(out=CBg[:, :, 3:3 + gn], in_=BKq, func=AF.Copy),
                         R=[("bank", b)], W=[("RA", "cbuf", par)])
                    S.op("act", lambda e, STq=STq, BKq=BKq: e.activation(out=STq, in_=BKq[:, :, gn - 3:gn], func=AF.Copy),
                         R=[("bank", b), ("RA", "cbuf", par)], W=sres)
                    c4s = []
                    off = 0
                    for seg in segs:
                        n = seg.n
                        cb = cbuf_v[:, par, off:off + 3 + n]

                        def c4fn(e, cb=cb, par=par, cb_=cb_, seg=seg, n=n):
                            ins = None
                            for tp in range(SCW):
                                ins = e.matmul(bank(cb_, 128, n, seg.col0), lhsT=dg4_v[:, par, tp, :], rhs=cb[:, tp:tp + n],
                                               start=(tp == 0), stop=(tp == SCW - 1))
                            return ins
                        c4s.append(c4fn)
                        off += 3 + n
                else:
                    off = 0
                    c4s = []
                    for seg in segs:
                        stv, stres = st_qkv(seg, l)
                        n = seg.n
                        cb = cbuf_v[:, par, off:off + 3 + n]
                        S.op("act", lambda e, cb=cb, stv=stv, i=i: e.activation(out=cb[:, 0:3], in_=stv[:, i, :], func=AF.Copy),
                             R=[stres], W=[("RA", "cbuf", par)])
                        S.op("act", lambda e, cb=cb, b=b, seg=seg, n=n: e.activation(out=cb[:, 3:3 + n], in_=bank(b, 128, n, seg.col0),
                                                                                 func=AF.Copy),
                             R=[("bank", b)], W=[("RA", "cbuf", par)])
                        S.op("act", lambda e, stv=stv, i=i, b=b, seg=seg, n=n: e.activation(
                            out=stv[:, i, :], in_=bank(b, 128, 3, seg.col0 + n - 3), func=AF.Copy),
                            R=[("bank", b), ("RA", "cbuf", par)], W=[stres])

                        def c4fn(e, cb=cb, par=par, cb_=cb_, seg=seg, n=n):
                            ins = None
                            for tp in range(SCW):
                                ins = e.matmul(bank(cb_, 128, n, seg.col0), lhsT=dg4_v[:, par, tp, :], rhs=cb[:, tp:tp + n],
                                               start=(tp == 0), stop=(tp == SCW - 1))
                            return ins
                        c4s.append(c4fn)
                        off += 3 + n

                def q2(c4s=c4s, par=par, cb_=cb_, i=i):
                    for f in c4s:
                        S.op("pe", f, R=[("RA", "cbuf", par), ("RA", "dg4", par)], W=[("bank", cb_)])
                    S.op("act", lambda e: e.activation(out=qkv_v[:, i, 0:w], in_=bank(cb_, 128, w), func=AF.Silu),
                         R=[("bank", cb_)], W=[QKV])
                if pq["f"] is not None:
                    pq["f"]()
                pq["f"] = q2
        pq["f"]()
        for ld in range(H // 2):
            si, slot = wload("win", l, li, KC * 256); li += 1
            for j in range(2):
                h = ld * 2 + j
                b = next_mm_bank()
                mm_group(b, si, slot, KC, 256, j * 128, lambda k: xh_t[:, k, 0:w], w, ["xh"])
                S.op("act", lambda e, b=b, h=h: e.activation(out=zs_v[:, h, 0:w], in_=bank(b, 128, w), func=AF.Silu),
                     R=[("bank", b)], W=[ZS])
        nch = w // CK
        b5 = bank(5, 64, nch * 2 * H).rearrange("p (c h) -> p c h", c=nch)

        def bafn(e):
            ins = None
            for ci in range(nch):
                for k in range(KC):
                    ins = e.matmul(b5[:, ci, :], lhsT=xh_t[:, k, ci * CK:(ci + 1) * CK], rhs=wba_t[:, k * 2 * H:(k + 1) * 2 * H],
                                   start=(k == 0), stop=(k == KC - 1))
            return ins
        S.op("pe", bafn, R=["xh", "wba"], W=[("bank", 5)])
        L1 = lambda i: l1_t[:, i, 0:nch, :]
        S.op("act", lambda e: e.activation(out=L1(2), in_=b5[:, :, 0:H], func=AF.Sigmoid), R=[("bank", 5)], W=["l1"])
        dtb = P_("dtb")[0:64, :].unsqueeze(1).to_broadcast([64, nch, H])
        alg = P_("alog")[0:64, :]
        S.op("dve", lambda e: e.tensor_tensor(out=L1(0), in0=b5[:, :, H:2 * H], in1=dtb, op=ALU.add),
             R=[("bank", 5), "prm"], W=["l1"])
        S.op("act", lambda e: e.activation(out=L1(3), in_=L1(0), func=AF.Abs), R=[], W=["l1"])
        S.op("act", lambda e: e.activation(out=L1(3), in_=L1(3), func=AF.Exp, scale=-1.0), R=[], W=["l1"])
        S.op("act", lambda e: e.activation(out=L1(3), in_=L1(3), func=AF.Ln, bias=cst_t[0:64, 705:706]), R=[], W=["l1"])
        S.op("dve", lambda e: e.scalar_tensor_tensor(out=L1(0), in0=L1(0), scalar=0.0, in1=L1(3), op0=ALU.max, op1=ALU.add),
             R=[], W=["l1"])
        S.op("act", lambda e: e.activation(out=l1_t[:, 4, 0, :], in_=alg, func=AF.Exp), R=["prm"], W=["l1"])
        S.op("dve", lambda e: e.scalar_tensor_tensor(out=L1(1), in0=L1(0), scalar=-1.0,
                                                     in1=l1_t[:, 4, 0, :].unsqueeze(1).to_broadcast([64, nch, H]),
                                                     op0=ALU.mult, op1=ALU.mult), R=[], W=["l1"])
        S.fence("RA")
        l2norm_tile(w)
        for seg in segs:
            if seg.seq != "P":
                b = seg.sidx - 1
                S.op("sp", lambda e, b=b: e.dma_start(out=S_S[:, :, :].rearrange("p h d -> p (h d)"), in_=sg_d[l, b]),
                     W=[("S", "S")], dma="SS")
            for cj in range(seg.n // CK):
                c0 = seg.col0 + cj * CK
                gdn_chunk(l, seg, c0 // CK, c0, w)
            if seg.seq != "P":
                S.op("sp", lambda e, seg=seg: e.dma_start(out=og_d[l, seg.sidx], in_=S_S[:, :, :].rearrange("p h d -> p (h d)")),
                     R=[("S", "S")], dma="out_SS")
            elif seg.last:
                S.op("sp", lambda e: e.dma_start(out=og_d[l, 0], in_=S_P[:, l].rearrange("p h d -> p (h d)")),
                     R=[("S", "P", l)], dma="out")
        for seg in segs:
            if seg.last:
                stv, stres = st_conv(seg, l)
                S.op("sp", lambda e, stv=stv, seg=seg: e.dma_start(out=oc_d[l, seg.sidx], in_=stv.rearrange("p c j -> p (c j)")),
                     R=[stres], dma="out_stc" + seg.seq)
                stv, stres = st_qkv(seg, l)
                S.op("sp", lambda e, stv=stv, seg=seg: e.dma_start(out=oq_d[l, seg.sidx], in_=stv.rearrange("p c j -> p (c j)")),
                     R=[stres], dma="out_stq" + seg.seq)
        S.fence("RA")
        pend2 = {"f": None}
        for ld in range(c.NLOUT):
            si, slot = wload("wout", l, ld, KM * 256)
            for j in range(2):
                n = ld * 2 + j
                b = next_mm_bank()
                mm_group(b, si, slot, KM, 256, j * 128, lambda k: m_chunk(k)[:, 0:w], w, ["mA", "xh"])
                nxt = out_block_epilogue(b, w, y_v, ("RA", "y"), n, n == 0, n == KC - 1)
                if pend2["f"] is not None:
                    pend2["f"]()
                pend2["f"] = nxt
        pend2["f"]()
        pend2["f"] = None
        residual_epilogue(w, y_v, ("RA", "y"), "g2")
        rmsnorm_to_xh(w, "g3")
        S.fence("RA")
        S.fence("RB")
        for half in range(2):
            for pr in range(FH // 2):
                for which in range(2):
                    ld = (half * (FH // 2) + pr) * 2 + which
                    si, slot = wload("wup", l, ld, KC * 256)
                    for j in range(2):
                        hc = half * FH + pr * 2 + j
                        ch = hc + which * FCH
                        b = next_mm_bank()
                        mm_group(b, si, slot, KC, 256, j * 128, lambda k: xh_t[:, k, 0:w], w, ["xh"])
                        par = j
                        if grp:
                            Fg = fbuf_v[:, par, 0:2 * (2 + gn)].rearrange("p (s t) -> p s t", t=2 + gn)
                            STf = stf_S[:, :, ch, :]
                            BKf = bank(b, 128, 2 * gn, gc0).rearrange("p (s t) -> p s t", t=gn)
                            AVg = accf_v[:, par, gc0:gc0 + 2 * gn].rearrange("p (s t) -> p s t", t=gn)
                            sres = [("stf", "A"), ("stf", "B")]
                            S.op("act", lambda e, Fg=Fg, STf=STf: e.activation(out=Fg[:, :, 0:2], in_=STf, func=AF.Copy),
                                 R=sres, W=[("RA", "fbuf", par)])
                            S.op("act", lambda e, Fg=Fg, BKf=BKf: e.activation(out=Fg[:, :, 2:2 + gn], in_=BKf, func=AF.Copy),
                                 R=[("bank", b)], W=[("RA", "fbuf", par)])
                            S.op("dve", lambda e, Fg=Fg, AVg=AVg, ch=ch: e.tensor_scalar(
                                out=AVg, in0=Fg[:, :, 0:gn], scalar1=P_("wf", ch * FCW), scalar2=P_("bf", ch),
                                op0=ALU.mult, op1=ALU.add), R=[("RA", "fbuf", par), "prm"], W=[("RA", "accf", par)])
                            for tp in range(1, FCW):
                                S.op("dve", lambda e, Fg=Fg, AVg=AVg, ch=ch, tp=tp: e.scalar_tensor_tensor(
                                    out=AVg, in0=Fg[:, :, tp:tp + gn], scalar=P_("wf", ch * FCW + tp), in1=AVg,
                                    op0=ALU.mult, op1=ALU.add), R=[("RA", "fbuf", par), "prm"], W=[("RA", "accf", par)])
                            S.op("act", lambda e, Fg=Fg, STf=STf: e.activation(out=STf, in_=Fg[:, :, gn:gn + 2], func=AF.Copy),
                                 R=[("RA", "fbuf", par)], W=sres)
                        else:
                            off = 0
                            for seg in segs:
                                stv, stres = st_ffn(seg, l)
                                n = seg.n
                                fb = fbuf_v[:, par, off:off + 2 + n]
                                S.op("act", lambda e, fb=fb, stv=stv, ch=ch: e.activation(out=fb[:, 0:2], in_=stv[:, ch, :], func=AF.Copy),
                                     R=[stres], W=[("RA", "fbuf", par)])
                                S.op("act", lambda e, fb=fb, b=b, seg=seg, n=n: e.activation(out=fb[:, 2:2 + n], in_=bank(b, 128, n, seg.col0),
                                                                                         func=AF.Copy),
                                     R=[("bank", b)], W=[("RA", "fbuf", par)])
                                av = accf_v[:, par, seg.col0:seg.col0 + n]
                                S.op("dve", lambda e, fb=fb, av=av, ch=ch, n=n: e.tensor_scalar(
                                    out=av, in0=fb[:, 0:n], scalar1=P_("wf", ch * FCW), scalar2=P_("bf", ch),
                                    op0=ALU.mult, op1=ALU.add), R=[("RA", "fbuf", par), "prm"], W=[("RA", "accf", par)])
                                for tp in range(1, FCW):
                                    S.op("dve", lambda e, fb=fb, av=av, ch=ch, n=n, tp=tp: e.scalar_tensor_tensor(
                                        out=av, in0=fb[:, tp:tp + n], scalar=P_("wf", ch * FCW + tp), in1=av,
                                        op0=ALU.mult, op1=ALU.add), R=[("RA", "fbuf", par), "prm"], W=[("RA", "accf", par)])
                                S.op("act", lambda e, fb=fb, stv=stv, ch=ch, n=n: e.activation(out=stv[:, ch, :], in_=fb[:, n:n + 2], func=AF.Copy),
                                     R=[("RA", "fbuf", par)], W=[stres])
                                off += 2 + n
                        if which == 0:
                            S.op("act", lambda e, j=j, par=par: e.activation(out=sgate_v[:, j, 0:w], in_=accf_v[:, par, 0:w], func=AF.Silu),
                                 R=[("RA", "accf", par)], W=[("RA", "sgate", j)])
                        else:
                            S.op("dve", lambda e, j=j, par=par, ai=hc - half * FH: e.tensor_tensor(
                                out=a_v[:, ai, 0:w], in0=accf_v[:, par, 0:w], in1=sgate_v[:, j, 0:w], op=ALU.mult),
                                R=[("RA", "accf", par), ("RA", "sgate", j)], W=[("RA", "a")])
            for n in range(KC):
                si, slot = wload("wdn", l, half * KC + n, FH * 128)
                b = next_mm_bank()
                mm_group(b, si, slot, FH, 128, 0, lambda k: a_v[:, k, 0:w], w, [("RA", "a")])
                if half == 0:
                    S.op("act", lambda e, b=b, n=n: e.activation(out=y2_v[:, n, 0:w], in_=bank(b, 128, w), func=AF.Copy),
                         R=[("bank", b)], W=[("RB", "y2", n)])
                else:
                    S.op("dve", lambda e, b=b, n=n: e.tensor_tensor(out=y2_v[:, n, 0:w], in0=y2_v[:, n, 0:w], in1=bank(b, 128, w),
                                                                  op=ALU.add), R=[("bank", b)], W=[("RB", "y2", n)])
                    par = n % 2
                    S.op("act", lambda e, n=n, par=par: e.activation(out=sqn_t[:, par, 0:w], in_=y2_v[:, n, 0:w], func=AF.Square),
                         R=[("RB", "y2", n)], W=[("sqn", par)])
                    nxt = (lambda n=n, par=par: S.op("pe", lambda e: e.matmul(bank(0, 128, w), lhsT=onesD, rhs=sqn_t[:, par, 0:w],
                                                                              start=(n == 0), stop=(n == KC - 1)),
                                                     R=[("sqn", par), "cst"], W=[("bank", 0)]))
                    if pend2["f"] is not None:
                        pend2["f"]()
                    pend2["f"] = nxt
        pend2["f"]()
        pend2["f"] = None
        residual_epilogue(w, y2_v, ("RB", "y2"), "g4")
        for seg in segs:
            if seg.last:
                stv, stres = st_ffn(seg, l)
                S.op("sp", lambda e, stv=stv, seg=seg: e.dma_start(out=of_d[l, seg.sidx], in_=stv.rearrange("p c j -> p (c j)")),
                     R=[stres], dma="out_stf" + seg.seq)

    for ti, tile in enumerate(tiles):
        cur["ti"] = ti
        w = tile["w"]
        seg0 = tile["segs"][0]
        xres = [("x", k) for k in range(KC)]
        has_p = seg0.seq == "P"
        if has_p:
            S.op("sp", lambda e, seg0=seg0: e.dma_start(out=x_t[:, :, 0:seg0.n], in_=xp_d[:, :, seg0.tok0:seg0.tok0 + seg0.n]),
                 W=xres, dma="xin")
        else:
            seg0 = Seg("P", 0, 0, False, False, 0, 0)
        has_s = tile["segs"][-1].seq != "P"
        if has_s:
            S.op("sp", lambda e, seg0=seg0, w=w: e.dma_start(out=x_t[:, :, seg0.n:w], in_=xs_d[:, :, 0:w - seg0.n]),
                 W=xres, dma="xin")
        for l in range(L):
            layer(l, tile)
        if has_p:
            S.op("sp", lambda e, seg0=seg0: e.dma_start(out=yp_d[:, :, seg0.tok0:seg0.tok0 + seg0.n], in_=x_t[:, :, 0:seg0.n]),
                 R=xres, dma="out_x")
        if has_s:
            S.op("sp", lambda e, seg0=seg0, w=w: e.dma_start(out=ys_d[:, :, 0:w - seg0.n], in_=x_t[:, :, seg0.n:w]),
                 R=xres, dma="out_x")
    S.wait_all("sp", [k for k in S.cnt if k[0] == "dma" and k[1].startswith("out")])
    S.emit(nc)
    es.close()
    return nc, S


def fm(a):
    sh = a.shape
    nt, nf = sh[-2], sh[-1]
    b = a.reshape(sh[:-2] + (nt, nf // 128, 128))
    nd = b.ndim
    perm = tuple(range(nd - 3)) + (nd - 1, nd - 2, nd - 3)
    return np.ascontiguousarray(b.transpose(perm))


def unfm(a):
    nd = a.ndim
    perm = tuple(range(nd - 3)) + (nd - 1, nd - 2, nd - 3)
    b = a.transpose(perm)
    return np.ascontiguousarray(b.reshape(b.shape[:-2] + (b.shape[-2] * b.shape[-1],)))


def wblk(W, cols):
    K = W.shape[0]
    sub = W[:, cols].reshape(K // 128, 128, len(cols)).transpose(1, 0, 2)
    return np.ascontiguousarray(sub.reshape(128, -1))


def make_consts():
    cst = np.zeros((128, 708), np.float32)
    cst[:, 0:128] = np.eye(128, dtype=np.float32)
    cst[:, 128:256] = 1.0
    cst[:, 384:512] = 1.0 / 128.0
    k = np.arange(64)[:, None]
    i = np.arange(64)[None, :]
    cst[0:64, 512:576] = (k <= i)
    cst[0:64, 576:640] = np.where(i >= k, 0.0, -1e30)
    cst[0:64, 640:704] = (i > k)
    cst[:, 704] = EPS
    cst[:, 705] = 1.0
    return cst


def prep_shared(c, inp):
    L, H, CC, KC, FCH, FH = c.L, c.H, c.CC, c.KC, c.FCH, c.FH
    cst = make_consts()
    cst[:, 256:384] = 1.0 / c.D
    CW = CC * 128
    GW = H * 128
    c1 = 2 * CW
    c2 = c1 + 3 * GW
    c3 = c2 + GW
    win = np.empty((L, c.NLIN, 128, KC * 256), np.float32)
    wba = np.empty((L, 128, KC * 2 * H), np.float32)
    wout = np.empty((L, c.NLOUT, 128, c.KM * 256), np.float32)
    wup = np.empty((L, c.NLUP, 128, KC * 256), np.float32)
    wdn = np.empty((L, c.NLDN, 128, FH * 128), np.float32)
    prm = np.zeros((L, 128, c.NPL), np.float32)
    ar = np.arange
    for l in range(L):
        Wi = inp["w_in"][l]
        li = 0
        for pair in range(CC // 2):
            win[l, li] = wblk(Wi, CW + pair * 256 + ar(256)); li += 1
            win[l, li] = wblk(Wi, pair * 256 + ar(256)); li += 1
        for ld in range(3 * H // 2):
            win[l, li] = wblk(Wi, c1 + ld * 256 + ar(256)); li += 1
        for ld in range(H // 2):
            win[l, li] = wblk(Wi, c2 + ld * 256 + ar(256)); li += 1
        wba[l] = wblk(Wi, c3 + ar(2 * H))
        Wo = inp["w_out"][l]
        for ld in range(c.NLOUT):
            wout[l, ld] = wblk(Wo, ld * 256 + ar(256))
        Wu = inp["w_up"][l]
        for half in range(2):
            for pr in range(FH // 2):
                for which in range(2):
                    ld = (half * (FH // 2) + pr) * 2 + which
                    col0 = which * c.FFN + (half * FH + pr * 2) * 128
                    wup[l, ld] = wblk(Wu, col0 + ar(256))
        Wd = inp["w_down"][l]
        for half in range(2):
            for n in range(KC):
                wdn[l, half * KC + n] = wblk(Wd[half * FH * 128:(half + 1) * FH * 128], n * 128 + ar(128))

        def put(name, arr):
            o, n = c.P[name]
            prm[l, :, o:o + n] = arr.reshape(128, n)

        def colvec(v):
            return v.reshape(-1, 128).T

        put("g1", colvec(inp["g_pre_mix"][l]))
        put("g2", colvec(inp["g_post_mix"][l]))
        put("g3", colvec(inp["g_pre_ffn"][l]))
        put("g4", colvec(inp["g_post_ffn"][l]))
        put("wdw", inp["w_dw"][l].reshape(KW, CC, 128).transpose(2, 1, 0))
        put("bdw", colvec(inp["b_dw"][l]))
        put("gng", colvec(inp["gn_g"][l]))
        put("gnb", colvec(inp["gn_b"][l]))
        put("wsc", inp["w_sc"][l].reshape(SCW, 3 * H, 128).transpose(2, 1, 0))
        put("on", inp["onorm_g"][l].reshape(128, 1))
        put("wf", inp["w_ffn_dw"][l].reshape(FCW, 2 * FCH, 128).transpose(2, 1, 0))
        put("bf", colvec(inp["b_ffn_dw"][l]))
        put("alog", np.broadcast_to(inp["a_log"][l][None, :], (128, H)))
        put("dtb", np.broadcast_to(inp["dt_bias"][l][None, :], (128, H)))
    return dict(prm=prm, cst=cst, win=win, wba=wba, wout=wout, wup=wup, wdn=wdn)


def prep_core(c, inp, core):
    L = c.L
    d = {}
    d["xp"] = fm(inp["x_prompt"][core])
    xs = inp["x_sample"][2 * core:2 * core + 2]
    d["xs"] = fm(xs.reshape(2 * c.DSEQ, c.D))
    sl = slice(2 * core, 2 * core + 2)
    d["st_conv"] = fm(inp["state_conv"][:, sl]).reshape(L, 2, 128, -1)
    d["st_qkv"] = fm(inp["state_qkv_conv"][:, sl]).reshape(L, 2, 128, -1)
    d["st_ffn"] = fm(inp["state_ffn_conv"][:, sl]).reshape(L, 2, 128, -1)
    sg = inp["state_gdn"][:, sl]
    d["st_gdn"] = np.ascontiguousarray(sg.transpose(0, 1, 3, 2, 4)).reshape(L, 2, 128, -1)
    return d


def run(c, inp, ncores):
    nc, S = build(c)
    shared = prep_shared(c, inp)
    in_maps = []
    for core in range(ncores):
        d = dict(shared)
        d.update(prep_core(c, inp, core))
        in_maps.append(d)
    res = run_bass_kernel_spmd(nc, in_maps, core_ids=list(range(ncores)))
    L, H, CC, FCH = c.L, c.H, c.CC, c.FCH
    B = ncores
    yp = np.empty((B, c.SEQ, c.D), np.float32)
    ys = np.empty((2 * B, c.DSEQ, c.D), np.float32)
    conv = np.empty((L, 3 * B, KW - 1, CC * 128), np.float32)
    qkv = np.empty((L, 3 * B, SCW - 1, 3 * H * 128), np.float32)
    gdn = np.empty((L, 3 * B, H, 128, 128), np.float32)
    ffn = np.empty((L, 3 * B, FCW - 1, 2 * FCH * 128), np.float32)
    for core in range(ncores):
        r = res.results[core]
        yp[core] = unfm(r["yp"])
        ys[2 * core:2 * core + 2] = unfm(r["ys"]).reshape(2, c.DSEQ, c.D)
        for s, bi in ((0, core), (1, B + 2 * core), (2, B + 2 * core + 1)):
            conv[:, bi] = unfm(r["o_conv"][:, s].reshape(L, 128, CC, KW - 1))
            qkv[:, bi] = unfm(r["o_qkv"][:, s].reshape(L, 128, 3 * H, SCW - 1))
            ffn[:, bi] = unfm(r["o_ffn"][:, s].reshape(L, 128, 2 * FCH, FCW - 1))
            gdn[:, bi] = r["o_gdn"][:, s].reshape(L, 128, H, 128).transpose(0, 2, 1, 3)
    return (yp, ys, conv[:, :B], qkv[:, :B], gdn[:, :B], ffn[:, :B],
            conv[:, B:], qkv[:, B:], gdn[:, B:], ffn[:, B:])


def kernel(**inputs):
    inp = {k: np.asarray(v) for k, v in inputs.items()}
    c = Cfg()
    return run(c, inp, 8)
```

```python
import numpy as np
import concourse.bass as bass
import concourse.mybir as mybir
from concourse.bass_utils import run_bass_kernel_spmd

F32 = mybir.dt.float32
BF16 = mybir.dt.bfloat16
ALU = mybir.AluOpType
AF = mybir.ActivationFunctionType
EPS = 1e-6
ENGS = ["pe", "act", "dve", "pool", "sp"]
KW = 31
SCW = 4
FCW = 3
CK = 64


class Cfg:
    def __init__(s, D=2048, CC=8, H=8, FFN=5632, L=4, SEQ=2048, DSEQ=64, PT=(512, 512, 512, 512), NSLOT=4, MERGE=False):
        s.MERGE = MERGE
        s.D, s.CC, s.H, s.FFN, s.L, s.SEQ, s.DSEQ, s.PT, s.NSLOT = D, CC, H, FFN, L, SEQ, DSEQ, tuple(PT), NSLOT
        assert sum(PT) == SEQ and all(p % 64 == 0 for p in PT)
        s.KC = D // 128
        s.FCH = FFN // 128
        s.FH = s.FCH // 2
        s.KM = CC + H
        s.NIN = 2 * CC * 128 + 4 * H * 128 + 2 * H
        s.TW = max(max(PT), (PT[-1] if MERGE else 0) + 2 * DSEQ)
        assert s.TW <= 512
        s.HC = H * CK
        s.HD = H * 128
        o = 0
        s.P = {}
        for name, n in [("g1", s.KC), ("g2", s.KC), ("g3", s.KC), ("g4", s.KC),
                        ("wdw", CC * KW), ("bdw", CC), ("gng", CC), ("gnb", CC),
                        ("wsc", 3 * H * SCW), ("on", 1),
                        ("wf", 2 * s.FCH * FCW), ("bf", 2 * s.FCH),
                        ("alog", H), ("dtb", H)]:
            s.P[name] = (o, n)
            o += n
        s.NPL = o
        s.SLOTW = max(s.KC * 256, s.KM * 256, s.FH * 128)
        s.NLIN = CC + 3 * H // 2 + H // 2
        s.NLOUT = s.KC // 2
        s.NLUP = s.FCH
        s.NLDN = 2 * s.KC


class Sched:
    def __init__(self):
        self.items = {e: [] for e in ENGS}
        self.cnt = {}
        self.known = {e: {} for e in ENGS}
        self.res = {}
        self.region_base = {}
        self.epoch = 0
        self.n_ops = 0

    def new_epoch(self):
        self.epoch += 1

    def _get(self, name):
        r = self.res.get(name)
        if r is None:
            base = {}
            if isinstance(name, tuple) and name[0] in self.region_base:
                base = dict(self.region_base[name[0]])
            r = {"w": None, "r": base}
            self.res[name] = r
        return r

    def fence(self, region):
        base = dict(self.region_base.get(region, {}))
        for name in list(self.res):
            if isinstance(name, tuple) and name[0] == region:
                r = self.res.pop(name)
                if r["w"] is not None:
                    k, v = r["w"]
                    base[k] = max(base.get(k, 0), v)
                for k, v in r["r"].items():
                    base[k] = max(base.get(k, 0), v)
        self.region_base[region] = base

    def op(self, eng, fn, R=(), W=(), dma=None):
        deps = {}

        def add(k, v):
            if v > deps.get(k, 0):
                deps[k] = v

        for n in R:
            r = self._get(n)
            if r["w"] is not None:
                add(*r["w"])
        for n in W:
            r = self._get(n)
            if r["w"] is not None:
                add(*r["w"])
            for k, v in r["r"].items():
                add(k, v)
        waits = []
        kn = self.known[eng]
        for k, v in deps.items():
            if dma is None and eng == "pe" and k[0] == "pe":
                continue
            if kn.get(k, 0) >= v:
                continue
            kn[k] = v
            waits.append((k, v))
        if dma is None:
            key = (eng, self.epoch)
            self.cnt[key] = self.cnt.get(key, 0) + 1
            tok = (key, self.cnt[key])
        else:
            key = ("dma", dma)
            self.cnt[key] = self.cnt.get(key, 0) + 16
            tok = (key, self.cnt[key])
        self.items[eng].append((waits, fn, key, dma is not None))
        for n in R:
            r = self._get(n)
            k, v = tok
            if v > r["r"].get(k, 0):
                r["r"][k] = v
        for n in W:
            r = self._get(n)
            r["w"] = tok
            r["r"] = {}
        self.n_ops += 1
        return tok

    def wait_all(self, eng, keys):
        waits = []
        for k in keys:
            v = self.cnt.get(k, 0)
            if v:
                waits.append((k, v))
        self.items[eng].append((waits, None, None, False))

    def emit(self, nc):
        sems = {}
        for k in self.cnt:
            sems[k] = nc.alloc_semaphore("s_%s_%s" % (k[0], str(k[1])))
        blk_engs = {"pe": "tensor", "act": "scalar", "dve": "vector", "pool": "gpsimd", "sp": "sync"}
        with nc.Block() as block:
            for e in ENGS:
                items = self.items[e]

                def body(engine, items=items):
                    for waits, fn, key, is_dma in items:
                        for k, v in waits:
                            engine.wait_ge(sems[k], v)
                        if fn is None:
                            continue
                        ins = fn(engine)
                        ins.then_inc(sems[key], 16 if is_dma else 1)

                getattr(block, blk_engs[e])(body)


class Seg:
    def __init__(s, seq, col0, n, first, last, tok0, sidx):
        s.seq, s.col0, s.n, s.first, s.last, s.tok0, s.sidx = seq, col0, n, first, last, tok0, sidx


def build(cfg):
    c = cfg
    nc = bass.Bass("TRN2", target_bir_lowering=False)
    S = Sched()
    D, KC, CC, H, L, TW, FCH, FH, KM, HC, HD = c.D, c.KC, c.CC, c.H, c.L, c.TW, c.FCH, c.FH, c.KM, c.HC, c.HD
    NPT = len(c.PT)

    def din(name, shape):
        return nc.dram_tensor(name, list(shape), F32, kind="ExternalInput").ap()

    def dout(name, shape):
        return nc.dram_tensor(name, list(shape), F32, kind="ExternalOutput").ap()

    xp_d = din("xp", [128, KC, c.SEQ])
    xs_d = din("xs", [128, KC, 2 * c.DSEQ])
    sc_d = din("st_conv", [L, 2, 128, CC * 30])
    sq_d = din("st_qkv", [L, 2, 128, 3 * H * 3])
    sg_d = din("st_gdn", [L, 2, 128, H * 128])
    sf_d = din("st_ffn", [L, 2, 128, 2 * FCH * 2])
    prm_d = din("prm", [L, 128, c.NPL])
    cst_d = din("cst", [128, 708])
    win_d = din("win", [L, c.NLIN, 128, KC * 256])
    wba_d = din("wba", [L, 128, KC * 2 * H])
    wout_d = din("wout", [L, c.NLOUT, 128, KM * 256])
    wup_d = din("wup", [L, c.NLUP, 128, KC * 256])
    wdn_d = din("wdn", [L, c.NLDN, 128, FH * 128])

    def dscr(name, shape):
        return nc.dram_tensor(name, list(shape), BF16, kind="Internal").ap()

    scr = {"win": dscr("win_s", [L, c.NLIN, 128, KC * 256]), "wout": dscr("wout_s", [L, c.NLOUT, 128, KM * 256]),
           "wup": dscr("wup_s", [L, c.NLUP, 128, KC * 256]), "wdn": dscr("wdn_s", [L, c.NLDN, 128, FH * 128])}
    wsrc = {"win": win_d, "wout": wout_d, "wup": wup_d, "wdn": wdn_d}
    yp_d = dout("yp", [128, KC, c.SEQ])
    ys_d = dout("ys", [128, KC, 2 * c.DSEQ])
    oc_d = dout("o_conv", [L, 3, 128, CC * 30])
    oq_d = dout("o_qkv", [L, 3, 128, 3 * H * 3])
    og_d = dout("o_gdn", [L, 3, 128, H * 128])
    of_d = dout("o_ffn", [L, 3, 128, 2 * FCH * 2])

    import contextlib
    es = contextlib.ExitStack()

    def sb(name, shape, dt=F32):
        return es.enter_context(nc.sbuf_tensor("sb_" + name, list(shape), dt))

    RA_N = max(10 * HC + 4 * HD, KC * TW, FH * TW // 2 + 2 * (6 + TW) + 4 * TW,
               (90 + TW) + (10 + TW) + (KW + SCW) * 128 + 8 * TW) + 64
    RB_N = max((3 * H + H) * TW // 2, KC * TW)
    x_t = sb("x", [128, KC, TW])
    xh_t = sb("xh", [128, KC, TW], BF16)
    mA_t = sb("mA", [128, CC, TW], BF16)
    RA = sb("RA", [128, RA_N])
    RB = sb("RB", [128, RB_N])
    slots = [sb("slot%d" % i, [128, c.SLOTW], BF16) for i in range(c.NSLOT)]
    wba_t = sb("wba", [128, KC * 2 * H], BF16)
    prm_t = sb("prm", [128, c.NPL])
    cst_t = sb("cst", [128, 708])
    idb_t = sb("idb", [128, 128], BF16)
    stc_P = sb("stc_P", [128, L, CC, 30])
    stq_P = sb("stq_P", [128, L, 3 * H, 3])
    stf_P = sb("stf_P", [128, L, 2 * FCH, 2])
    S_P = sb("S_P", [128, L, H, 128])
    stc_S = sb("stc_S", [128, 2, CC, 30])
    stq_S = sb("stq_S", [128, 2, 3 * H, 3])
    stf_S = sb("stf_S", [128, 2, 2 * FCH, 2])
    S_S = sb("S_S", [128, H, 128])
    Sbf = sb("Sbf", [128, H, 128], BF16)
    rstd_t = sb("rstd", [128, TW])
    sqn_t = sb("sqn", [128, 2, TW])
    NCH = TW // CK
    l1_t = sb("l1", [64, 8, NCH, H])
    l1c_t = sb("l1c", [128, 8, H])
    ps = es.enter_context(nc.psum_tensor("ps", [128, 8 * 512], F32))

    def bank(b, npart=128, n=512, off=0):
        return ps[0:npart, b * 512 + off: b * 512 + off + n]

    identF = cst_t[:, 0:128]
    ones1 = cst_t[:, 128:256]
    onesD = cst_t[:, 256:384]
    ones128 = cst_t[:, 384:512]
    Umask = cst_t[0:64, 512:576]
    negmask = cst_t[0:64, 576:640]
    strict01 = cst_t[0:64, 640:704]
    ident64 = cst_t[0:64, 0:64]

    def P_(name, a=None, b=None):
        o, n = c.P[name]
        if a is None:
            return prm_t[:, o:o + n]
        return prm_t[:, o + a:o + (b if b is not None else a + 1)]

    def m_chunk(j):
        if j < CC:
            return mA_t[:, j, :]
        return xh_t[:, KC - H + (j - CC), :]

    def RAv(off, n, dt=F32):
        if dt == F32:
            return RA[:, off:off + n]
        return RA[:, off:off + n].bitcast(BF16)

    def RBv(off, n, dt=F32):
        if dt == F32:
            return RB[:, off:off + n]
        return RB[:, off:off + n].bitcast(BF16)

    qkv_v = RBv(0, 3 * H * TW // 2, BF16).rearrange("p (c t) -> p c t", t=TW)
    zs_v = RBv(3 * H * TW // 2, H * TW // 2, BF16).rearrange("p (c t) -> p c t", t=TW)
    y2_v = RBv(0, KC * TW).rearrange("p (c t) -> p c t", t=TW)
    y_v = RAv(0, KC * TW).rearrange("p (c t) -> p c t", t=TW)
    a_v = RAv(0, FH * TW // 2, BF16).rearrange("p (c t) -> p c t", t=TW)
    o = 0
    ubuf_v = RAv(o, 90 + TW, BF16).rearrange("p (a t) -> p a t", a=2); o += 90 + TW
    cbuf_v = RAv(o, 10 + TW, BF16).rearrange("p (a t) -> p a t", a=2); o += 10 + TW
    dg31_v = RAv(o, KW * 128, BF16).rearrange("p (a j c) -> p a j c", a=2, j=KW); o += KW * 128
    dg4_v = RAv(o, SCW * 128, BF16).rearrange("p (a j c) -> p a j c", a=2, j=SCW); o += SCW * 128
    acc_v = RAv(o, 2 * TW).rearrange("p (a t) -> p a t", a=2); o += 2 * TW
    sg_v = RAv(o, 2 * TW).rearrange("p (a t) -> p a t", a=2); o += 2 * TW
    mu_v = RAv(o, TW); o += TW
    var_v = RAv(o, TW); o += TW
    cen_v = RAv(o, TW); o += TW
    sqA_v = RAv(o, TW); o += TW
    assert o <= RA_N, (o, RA_N)
    o = FH * TW // 2
    fbuf_v = RAv(o, 2 * (6 + TW)).rearrange("p (a t) -> p a t", a=2); o += 2 * (6 + TW)
    accf_v = RAv(o, 2 * TW).rearrange("p (a t) -> p a t", a=2); o += 2 * TW
    sgate_v = RAv(o, 2 * TW).rearrange("p (a t) -> p a t", a=2); o += 2 * TW

    def gslot(i, npart=128):
        return RA[0:npart, i * HC:(i + 1) * HC].rearrange("p (h t) -> p h t", h=H)

    def gslot_bf(i):
        return RA[:, i * HC:(i + 1) * HC].bitcast(BF16).rearrange("p (a h t) -> p a h t", a=2, h=H)

    def gbig(i):
        o_ = 10 * HC + i * HD
        return RA[0:64, o_:o_ + HD // 2].bitcast(BF16).rearrange("p (h d) -> p h d", h=H)

    def gslot_h(i, npart=128):
        return RA[0:npart, i * HC:i * HC + HC // 2].bitcast(BF16).rearrange("p (h t) -> p h t", h=H)

    tiles = []
    tok = 0
    for t, pw in enumerate(c.PT):
        segs = [Seg("P", 0, pw, t == 0, t == NPT - 1, tok, 0)]
        wt = pw
        if t == NPT - 1 and c.MERGE:
            segs.append(Seg("A", pw, c.DSEQ, True, True, 0, 1))
            segs.append(Seg("B", pw + c.DSEQ, c.DSEQ, True, True, 0, 2))
            wt = pw + 2 * c.DSEQ
        tiles.append(dict(w=wt, segs=segs))
        tok += pw
    if not c.MERGE:
        tiles.append(dict(w=2 * c.DSEQ, segs=[Seg("A", 0, c.DSEQ, True, True, 0, 1),
                                              Seg("B", c.DSEQ, c.DSEQ, True, True, 0, 2)]))

    QKV = ("RB", "qkv")
    ZS = ("RB", "zs")
    ring = {"i": 0}

    cur = {"ti": 0}

    def wload(kind, l, idx, ncols):
        i = ring["i"] % c.NSLOT
        ring["i"] += 1
        slot = slots[i]
        if cur["ti"] == 0:
            src_ap = wsrc[kind][l, idx]
            S.op("pool", lambda e, slot=slot, src_ap=src_ap, ncols=ncols:
                 e.dma_start(out=slot[:, 0:ncols], in_=src_ap),
                 W=[("slot", i)], dma="w%d" % i)
            dst_ap = scr[kind][l, idx]
            S.op("sp", lambda e, slot=slot, dst_ap=dst_ap, ncols=ncols:
                 e.dma_start(out=dst_ap, in_=slot[:, 0:ncols]),
                 R=[("slot", i)], W=[("scr", kind, l, idx)], dma="ws%d" % i)
        else:
            src_ap = scr[kind][l, idx]
            S.op("sp", lambda e, slot=slot, src_ap=src_ap, ncols=ncols:
                 e.dma_start(out=slot[:, 0:ncols], in_=src_ap),
                 R=[("scr", kind, l, idx)], W=[("slot", i)], dma="w%d" % i)
        return i, slot

    mmrot = {"i": 0}

    mmrot["banks"] = [1, 2, 3]

    def next_mm_bank():
        bl = mmrot["banks"]
        b = bl[mmrot["i"] % len(bl)]
        mmrot["i"] += 1
        return b

    def mm_group(b, slot_i, slot, nk, ncols_per_k, col_off, rhs_fn, w, rhs_res, start=True, stop=True, m=128):
        def fn(e):
            ins = None
            for k in range(nk):
                ins = e.matmul(bank(b, m, w), lhsT=slot[:, k * ncols_per_k + col_off:k * ncols_per_k + col_off + m],
                               rhs=rhs_fn(k), start=(start and k == 0), stop=(stop and k == nk - 1))
            return ins
        S.op("pe", fn, R=[("slot", slot_i)] + list(rhs_res), W=[("bank", b)])

    def st_conv(seg, l):
        return (stc_P[:, l], ("stc", "P", l)) if seg.seq == "P" else (stc_S[:, seg.sidx - 1], ("stc", seg.seq))

    def st_qkv(seg, l):
        return (stq_P[:, l], ("stq", "P", l)) if seg.seq == "P" else (stq_S[:, seg.sidx - 1], ("stq", seg.seq))

    def st_ffn(seg, l):
        return (stf_P[:, l], ("stf", "P", l)) if seg.seq == "P" else (stf_S[:, seg.sidx - 1], ("stf", seg.seq))

    def st_S(seg, l):
        return (S_P[:, l], ("S", "P", l)) if seg.seq == "P" else (S_S[:, :, :], ("S", "S"))

    def rsqrt_eps(out_ap, in_ap, R, W):
        npart = out_ap.shape[0]
        S.op("act", lambda e: e.activation(out=out_ap, in_=in_ap, func=AF.Ln, bias=cst_t[0:npart, 704:705]), R=list(R) + ["cst"], W=W)
        S.op("act", lambda e: e.activation(out=out_ap, in_=out_ap, func=AF.Exp, scale=-0.5), R=[], W=W)

    S.op("sp", lambda e: e.dma_start(out=cst_t[:, :], in_=cst_d), W=["cst"], dma="cst")
    S.op("dve", lambda e: e.tensor_copy(out=idb_t[:, :], in_=identF), R=["cst"], W=["idb"])
    S.op("dve", lambda e: e.memset(stc_P[:, :, :, :].rearrange("p a b c -> p (a b c)"), 0.0), W=[("stc", "P", l) for l in range(L)])
    S.op("dve", lambda e: e.memset(stq_P[:, :, :, :].rearrange("p a b c -> p (a b c)"), 0.0), W=[("stq", "P", l) for l in range(L)])
    S.op("dve", lambda e: e.memset(stf_P[:, :, :, :].rearrange("p a b c -> p (a b c)"), 0.0), W=[("stf", "P", l) for l in range(L)])
    S.op("dve", lambda e: e.memset(S_P[:, :, :, :].rearrange("p a b c -> p (a b c)"), 0.0), W=[("S", "P", l) for l in range(L)])

    def rmsnorm_to_xh(w, gname):
        for k in range(KC):
            par = k % 2
            S.op("act", lambda e, k=k, par=par: e.activation(out=sqn_t[:, par, 0:w], in_=x_t[:, k, 0:w], func=AF.Square),
                 R=[("x", k)], W=[("sqn", par)])
            S.op("pe", lambda e, k=k, par=par: e.matmul(bank(0, 128, w), lhsT=onesD, rhs=sqn_t[:, par, 0:w],
                                                          start=(k == 0), stop=(k == KC - 1)),
                 R=[("sqn", par), "cst"], W=[("bank", 0)])
        rsqrt_eps(rstd_t[:, 0:w], bank(0, 128, w), [("bank", 0)], ["rstd"])
        for k in range(KC):
            S.op("dve", lambda e, k=k: e.scalar_tensor_tensor(out=xh_t[:, k, 0:w], in0=x_t[:, k, 0:w],
                                                                scalar=P_(gname, k), in1=rstd_t[:, 0:w],
                                                                op0=ALU.mult, op1=ALU.mult),
                 R=[("x", k), "rstd", "prm"], W=["xh"])

    def residual_epilogue(w, yv, yres, gname):
        rsqrt_eps(rstd_t[:, 0:w], bank(0, 128, w), [("bank", 0)], ["rstd"])
        for k in range(KC):
            S.op("dve", lambda e, k=k: e.scalar_tensor_tensor(out=yv[:, k, 0:w], in0=yv[:, k, 0:w],
                                                                scalar=P_(gname, k), in1=rstd_t[:, 0:w],
                                                                op0=ALU.mult, op1=ALU.mult),
                 R=["rstd", "prm"], W=[(yres, k)])
            S.op("dve", lambda e, k=k: e.tensor_tensor(out=x_t[:, k, 0:w], in0=x_t[:, k, 0:w], in1=yv[:, k, 0:w],
                                                         op=ALU.add),
                 R=[(yres, k)], W=[("x", k)])

    def out_block_epilogue(b, w, yv, yres, n, first, last):
        par = n % 2
        S.op("act", lambda e: e.activation(out=yv[:, n, 0:w], in_=bank(b, 128, w), func=AF.Copy),
             R=[("bank", b)], W=[(yres, n)])
        S.op("act", lambda e: e.activation(out=sqn_t[:, par, 0:w], in_=bank(b, 128, w), func=AF.Square),
             R=[("bank", b)], W=[("sqn", par)])
        return lambda: S.op("pe", lambda e: e.matmul(bank(0, 128, w), lhsT=onesD, rhs=sqn_t[:, par, 0:w], start=first, stop=last),
                            R=[("sqn", par), "cst"], W=[("bank", 0)])

    def gdn_chunk(l, seg, ci, c0, w):
        Sv, Sres = st_S(seg, l)
        G = ("RA",)

        def g(n):
            return ("RA", n)

        qv = qkv_v[:, 0:H, c0:c0 + CK]
        kv = qkv_v[:, H:2 * H, c0:c0 + CK]
        vv = qkv_v[:, 2 * H:3 * H, c0:c0 + CK]
        s0, s2, s3, s4 = [gslot(i) for i in (0, 2, 3, 4)]
        s5 = gslot_h(5)
        s4h = gslot_h(4)
        S.op("act", lambda e: e.activation(out=Sbf[:, :, :], in_=Sv, func=AF.Copy), R=[Sres], W=["Sbf"])
        qn = qv
        kn = kv
        kbg, kd, vb, vnew = gbig(0), gbig(1), gbig(2), gbig(3)
        bkf = lambda b, npart=128: bank(b, npart, HC).rearrange("p (h t) -> p h t", h=H)
        gL = l1_t[:, 1, ci, :]
        bL = l1_t[:, 2, ci, :]
        G_sb = l1c_t[0:64, 0, :]
        eG = l1c_t[0:64, 1, :]
        bEG = l1c_t[0:64, 2, :]
        kdsc = l1c_t[0:64, 3, :]
        gl128 = l1c_t[:, 4, :]
        dGl = l1c_t[0:64, 5, :]

        b1bf = bank(1, 64, HD // 2).bitcast(BF16).rearrange("p (h d) -> p h d", h=H)
        b2bf = bank(2, 64, HD // 2).bitcast(BF16).rearrange("p (h d) -> p h d", h=H)

        def tr_fn(dst, src):
            def fn(e):
                ins = None
                for h in range(H):
                    ins = e.transpose(dst[:, h, :], src[:, h, :], idb_t[:, :])
                return ins
            return fn
        S.op("pe", tr_fn(b1bf, kn), R=[("RB", "qn", ci), "idb"], W=[("bank", 1)])
        S.op("pe", tr_fn(b2bf, vv), R=[QKV, "idb"], W=[("bank", 2)])
        b7 = bank(7, 128, 2 * H)

        def l1mm(e):
            e.matmul(b7[0:64, 0:H], lhsT=Umask, rhs=gL, start=True, stop=True)
            return e.matmul(b7[:, H:2 * H], lhsT=ones1[0:64, :], rhs=gL, start=True, stop=True)
        S.op("pe", l1mm, R=["l1", "cst"], W=[("bank", 7)])
        S.op("act", lambda e: e.activation(out=G_sb, in_=b7[0:64, 0:H], func=AF.Copy), R=[("bank", 7)], W=["l1c0"])
        S.op("act", lambda e: e.activation(out=eG, in_=b7[0:64, 0:H], func=AF.Exp), R=[("bank", 7)], W=["l1c1"])
        S.op("act", lambda e: e.activation(out=gl128, in_=b7[:, H:2 * H], func=AF.Exp), R=[("bank", 7)], W=["l1c4"])
        S.op("dve", lambda e: e.tensor_tensor(out=dGl, in0=b7[0:64, H:2 * H], in1=G_sb, op=ALU.subtract),
             R=[("bank", 7), "l1c0"], W=["l1c5"])
        S.op("act", lambda e: e.activation(out=kdsc, in_=dGl, func=AF.Exp), R=["l1c5"], W=["l1c3"])
        S.op("dve", lambda e: e.tensor_tensor(out=bEG, in0=bL, in1=eG, op=ALU.mult), R=["l1", "l1c1"], W=["l1c2"])
        s2_64, s3_64 = gslot(2, 64), gslot(3, 64)
        S.op("dve", lambda e: e.tensor_tensor(out=s2_64, in0=Umask.unsqueeze(1).to_broadcast([64, H, CK]),
                                              in1=gL.unsqueeze(2).to_broadcast([64, H, CK]), op=ALU.mult),
             R=["l1", "cst"], W=[g("s2")])
        S.op("pe", lambda e: e.matmul(bank(5, 128, HC), lhsT=ones1[0:64, :], rhs=s2_64.rearrange("p h t -> p (h t)"),
                                      start=True, stop=True), R=[g("s2"), "cst"], W=[("bank", 5)])
        S.op("dve", lambda e: e.tensor_tensor(out=s3_64, in0=ident64.unsqueeze(1).to_broadcast([64, H, CK]),
                                              in1=bL.unsqueeze(2).to_broadcast([64, H, CK]), op=ALU.mult),
             R=["l1", "cst"], W=[g("s3")])
        S.op("pe", lambda e: e.matmul(bank(6, 64, HC), lhsT=ones1[0:64, 0:64], rhs=s3_64.rearrange("p h t -> p (h t)"),
                                      start=True, stop=True), R=[g("s3"), "cst"], W=[("bank", 6)])
        S.op("dve", lambda e: e.tensor_tensor(out=s2_64, in0=bkf(5, 64), in1=G_sb.unsqueeze(2).to_broadcast([64, H, CK]),
                                              op=ALU.subtract), R=[("bank", 5), "l1c0"], W=[g("s2")])
        S.op("dve", lambda e: e.tensor_tensor(out=s2_64, in0=s2_64, in1=negmask.unsqueeze(1).to_broadcast([64, H, CK]),
                                              op=ALU.add), R=["cst"], W=[g("s2")])
        S.op("act", lambda e: e.activation(out=s3_64, in_=s2_64, func=AF.Exp), R=[g("s2")], W=[g("s3")])
        S.op("act", lambda e: e.activation(out=s4, in_=bkf(5), func=AF.Exp), R=[("bank", 5)], W=[g("s4")])
        S.op("dve", lambda e: e.tensor_tensor(out=s5, in0=qn, in1=s4, op=ALU.mult), R=[("RB", "qn", ci), g("s4")], W=[g("s5")])
        def kkfn(e):
            ins = None
            for h in range(H):
                ins = e.matmul(bank(3, 64, CK, h * CK), lhsT=kn[:, h, :], rhs=kn[:, h, :], start=True, stop=True)
            return ins

        def qkfn(e):
            ins = None
            for h in range(H):
                ins = e.matmul(bank(4, 64, CK, h * CK), lhsT=kn[:, h, :], rhs=qn[:, h, :], start=True, stop=True)
            return ins
        S.op("pe", kkfn, R=[("RB", "qn", ci)], W=[("bank", 3)])
        S.op("pe", qkfn, R=[("RB", "qn", ci)], W=[("bank", 4)])
        s6_64, s7_64, s8_64, s9_64 = gslot_h(6, 64), gslot_h(7, 64), gslot_h(8, 64), gslot_h(9, 64)
        b6bf = bank(6, 64, HC // 2).bitcast(BF16).rearrange("p (h t) -> p h t", h=H)
        S.op("dve", lambda e: e.tensor_tensor(out=s6_64, in0=bkf(4, 64), in1=s3_64, op=ALU.mult),
             R=[("bank", 4), g("s3")], W=[g("s6")])
        S.op("dve", lambda e: e.tensor_tensor(out=s2_64, in0=s3_64, in1=strict01.unsqueeze(1).to_broadcast([64, H, CK]),
                                              op=ALU.mult), R=[g("s3"), "cst"], W=[g("s2")])
        S.op("dve", lambda e: e.tensor_tensor(out=s2_64, in0=s2_64, in1=bkf(6, 64), op=ALU.mult),
             R=[("bank", 6)], W=[g("s2")])
        S.op("dve", lambda e: e.scalar_tensor_tensor(out=s7_64, in0=bkf(3, 64), scalar=-1.0, in1=s2_64,
                                                     op0=ALU.mult, op1=ALU.mult),
             R=[("bank", 3), g("s2")], W=[g("s7")])
        def ptfn(e):
            ins = None
            for h in range(H):
                ins = e.transpose(b6bf[:, h, :], s7_64[:, h, :], idb_t[0:64, 0:64])
            return ins
        S.op("pe", ptfn, R=[g("s7"), "idb"], W=[("bank", 6)])
        S.op("act", lambda e: e.activation(out=s8_64, in_=b6bf, func=AF.Copy), R=[("bank", 6)], W=[g("s8")])
        S.op("dve", lambda e: e.tensor_tensor(out=s9_64, in0=s7_64, in1=ident64.unsqueeze(1).to_broadcast([64, H, CK]),
                                              op=ALU.add), R=[g("s7"), "cst"], W=[g("s9")])
        for kk in range(1, 6):
            def sqfn(e, kk=kk):
                ins = None
                for h in range(H):
                    if kk < 5:
                        e.matmul(bank(3, 64, CK, h * CK), lhsT=s8_64[:, h, :], rhs=s7_64[:, h, :], start=True, stop=True)
                    ins = e.matmul(bank(4, 64, CK, h * CK), lhsT=s7_64[:, h, :], rhs=s8_64[:, h, :], start=True, stop=True)
                return ins
            S.op("pe", sqfn, R=[g("s7"), g("s8")], W=[("bank", 3), ("bank", 4)])
            if kk < 5:
                S.op("act", lambda e: e.activation(out=s7_64, in_=bkf(3, 64), func=AF.Copy), R=[("bank", 3)], W=[g("s7")])
            S.op("dve", lambda e: e.tensor_copy(out=s8_64, in_=bkf(4, 64)), R=[("bank", 4)], W=[g("s8")])

            def xfn(e):
                ins = None
                for h in range(H):
                    ins = e.matmul(bank(5, 64, CK, h * CK), lhsT=s8_64[:, h, :], rhs=s9_64[:, h, :], start=True, stop=True)
                return ins
            S.op("pe", xfn, R=[g("s8"), g("s9")], W=[("bank", 5)])
            S.op("dve", lambda e: e.tensor_tensor(out=s9_64, in0=s9_64, in1=bkf(5, 64), op=ALU.add),
                 R=[("bank", 5)], W=[g("s9")])
        S.op("dve", lambda e: e.tensor_tensor(out=kbg, in0=b1bf, in1=bEG.unsqueeze(2).to_broadcast([64, H, 128]), op=ALU.mult),
             R=[("bank", 1), "l1c2"], W=[g("kbg")])
        S.op("dve", lambda e: e.tensor_tensor(out=kd, in0=b1bf, in1=kdsc.unsqueeze(2).to_broadcast([64, H, 128]), op=ALU.mult),
             R=[("bank", 1), "l1c3"], W=[g("kd")])
        S.op("dve", lambda e: e.tensor_tensor(out=vb, in0=b2bf, in1=bL.unsqueeze(2).to_broadcast([64, H, 128]), op=ALU.mult),
             R=[("bank", 2), "l1"], W=[g("vb")])
        def wtfn(e):
            ins = None
            for h in range(H):
                ins = e.matmul(bank(1, 128, CK, h * CK), lhsT=kbg[:, h, :], rhs=s9_64[:, h, :], start=True, stop=True)
            return ins
        S.op("pe", wtfn, R=[g("kbg"), g("s9")], W=[("bank", 1)])
        S.op("act", lambda e: e.activation(out=s4h, in_=bkf(1), func=AF.Copy, scale=-1.0), R=[("bank", 1)], W=[g("s4")])
        vps = ps[0:64, 6 * 512:6 * 512 + HD].rearrange("p (h d) -> p h d", h=H)
        sps = ps[:, 6 * 512:6 * 512 + HD].rearrange("p (h d) -> p h d", h=H)

        def vnfn(e):
            ins = None
            for h in range(H):
                e.matmul(vps[:, h, :], lhsT=s9_64[:, h, :], rhs=vb[:, h, :], start=True, stop=False)
                ins = e.matmul(vps[:, h, :], lhsT=s4h[:, h, :], rhs=Sbf[:, h, :], start=False, stop=True)
            return ins
        S.op("pe", vnfn, R=[g("s9"), g("vb"), g("s4"), "Sbf"], W=[("bank", 6), ("bank", 7)])
        S.op("act", lambda e: e.activation(out=vnew, in_=vps, func=AF.Copy), R=[("bank", 6), ("bank", 7)], W=[g("vnew")])

        def otfn(e):
            ins = None
            for h in range(H):
                e.matmul(bank(2, 128, CK, h * CK), lhsT=Sbf[:, h, :], rhs=s5[:, h, :], start=True, stop=False)
                ins = e.matmul(bank(2, 128, CK, h * CK), lhsT=vnew[:, h, :], rhs=s6_64[:, h, :], start=False, stop=True)
            return ins
        S.op("pe", otfn, R=["Sbf", g("s5"), g("vnew"), g("s6")], W=[("bank", 2)])

        def supfn(e):
            ins = None
            for h in range(H):
                ins = e.matmul(sps[:, h, :], lhsT=kd[:, h, :], rhs=vnew[:, h, :], start=True, stop=True)
            return ins
        S.op("pe", supfn, R=[g("kd"), g("vnew")], W=[("bank", 6), ("bank", 7)])
        S.op("dve", lambda e: e.tensor_tensor(out=Sv, in0=Sv, in1=gl128.unsqueeze(2).to_broadcast([128, H, 128]), op=ALU.mult),
             R=["l1c4"], W=[Sres])
        S.op("dve", lambda e: e.tensor_tensor(out=Sv, in0=Sv, in1=sps, op=ALU.add),
             R=[("bank", 6), ("bank", 7)], W=[Sres])
        S.op("act", lambda e: e.activation(out=s3, in_=bkf(2), func=AF.Copy), R=[("bank", 2)], W=[g("s3")])
        S.op("act", lambda e: e.activation(out=s0, in_=bkf(2), func=AF.Square), R=[("bank", 2)], W=[g("s0")])
        S.op("pe", lambda e: e.matmul(bank(0, 128, HC), lhsT=ones128, rhs=s0.rearrange("p h t -> p (h t)"),
                                      start=True, stop=True), R=[g("s0"), "cst"], W=[("bank", 0)])
        rsqrt_eps(s2, bkf(0), [("bank", 0)], [g("s2")])
        S.op("dve", lambda e: e.tensor_tensor(out=s3, in0=s3, in1=s2, op=ALU.mult), R=[g("s2")], W=[g("s3")])
        mg = xh_t[:, KC - H:KC, c0:c0 + CK]
        S.op("dve", lambda e: e.scalar_tensor_tensor(out=mg, in0=s3, scalar=P_("on", 0), in1=zs_v[:, :, c0:c0 + CK],
                                                     op0=ALU.mult, op1=ALU.mult),
             R=[g("s3"), "prm", ZS], W=["xh"])

    def l2norm_tile(w):
        for ci in range(w // CK):
            c0 = ci * CK
            pr = ci % 2
            sA = gslot(0) if pr == 0 else gslot(3)
            sB = gslot(2) if pr == 0 else gslot(4)
            bk = 0 if pr == 0 else 3
            nA, nB = (("RA", "s0"), ("RA", "s2")) if pr == 0 else (("RA", "s3"), ("RA", "s4"))
            for which, scl in ((0, 128.0 ** -0.5), (1, 1.0)):
                src = qkv_v[:, which * H:(which + 1) * H, c0:c0 + CK]
                S.op("act", lambda e, src=src, sA=sA: e.activation(out=sA, in_=src, func=AF.Square), R=[QKV], W=[nA])
                S.op("pe", lambda e, sA=sA, bk=bk: e.matmul(bank(bk, 128, HC), lhsT=ones1, rhs=sA.rearrange("p h t -> p (h t)"),
                                                           start=True, stop=True), R=[nA, "cst"], W=[("bank", bk)])
                rsqrt_eps(sB, bank(bk, 128, HC).rearrange("p (h t) -> p h t", h=H), [("bank", bk)], [nB])
                S.op("dve", lambda e, src=src, sB=sB, scl=scl: e.scalar_tensor_tensor(
                    out=src, in0=src, scalar=scl, in1=sB, op0=ALU.mult, op1=ALU.mult), R=[nB], W=[("RB", "qn", ci)])

    def layer(l, tile):
        w = tile["w"]
        segs = tile["segs"]
        grp = (len(segs) == 2 and all(sg_.seq != "P" for sg_ in segs) and segs[0].n == segs[1].n
               and segs[1].col0 == segs[0].col0 + segs[0].n)
        gn = segs[0].n
        gc0 = segs[0].col0
        S.new_epoch()
        S.op("sp", lambda e: e.dma_start(out=prm_t[:, :], in_=prm_d[l]), W=["prm"], dma="prm")
        S.op("pool", lambda e: e.dma_start(out=wba_t[:, :], in_=wba_d[l]), W=["wba"], dma="wba")
        for seg in segs:
            if seg.seq != "P":
                b = seg.sidx - 1
                S.op("sp", lambda e, b=b: e.dma_start(out=stc_S[:, b].rearrange("p c j -> p (c j)"), in_=sc_d[l, b]),
                     W=[("stc", seg.seq)], dma="stc" + seg.seq)
                S.op("sp", lambda e, b=b: e.dma_start(out=stq_S[:, b].rearrange("p c j -> p (c j)"), in_=sq_d[l, b]),
                     W=[("stq", seg.seq)], dma="stq" + seg.seq)
                S.op("sp", lambda e, b=b: e.dma_start(out=stf_S[:, b].rearrange("p c j -> p (c j)"), in_=sf_d[l, b]),
                     W=[("stf", seg.seq)], dma="stf" + seg.seq)
        S.fence("RA")
        S.fence("RB")
        mmrot["banks"] = [1, 2, 3]
        rmsnorm_to_xh(w, "g1")
        pend = {"cc": None, "p2": None}

        def conv_ln(cc, par):
            av = acc_v[:, par, 0:w]
            S.op("act", lambda e, av=av: e.activation(out=sqA_v[:, 0:w], in_=av, func=AF.Square),
                 R=[("RA", "acc", par)], W=[("RA", "sqA")])
            S.op("pe", lambda e, av=av: e.matmul(bank(0, 128, w), lhsT=ones128, rhs=av, start=True, stop=True),
                 R=[("RA", "acc", par), "cst"], W=[("bank", 0)])
            S.op("pe", lambda e: e.matmul(bank(4, 128, w), lhsT=ones128, rhs=sqA_v[:, 0:w], start=True, stop=True),
                 R=[("RA", "sqA"), "cst"], W=[("bank", 4)])
            S.op("act", lambda e: e.activation(out=mu_v[:, 0:w], in_=bank(0, 128, w), func=AF.Copy),
                 R=[("bank", 0)], W=[("RA", "mu")])
            S.op("dve", lambda e: e.tensor_tensor(out=var_v[:, 0:w], in0=mu_v[:, 0:w], in1=mu_v[:, 0:w], op=ALU.mult),
                 R=[("RA", "mu")], W=[("RA", "var")])
            S.op("dve", lambda e: e.tensor_tensor(out=var_v[:, 0:w], in0=bank(4, 128, w), in1=var_v[:, 0:w], op=ALU.subtract),
                 R=[("bank", 4)], W=[("RA", "var")])
            rsqrt_eps(var_v[:, 0:w], var_v[:, 0:w], [], [("RA", "var")])
            S.op("dve", lambda e, av=av: e.tensor_tensor(out=cen_v[:, 0:w], in0=av, in1=mu_v[:, 0:w], op=ALU.subtract),
                 R=[("RA", "acc", par), ("RA", "mu")], W=[("RA", "cen")])
            S.op("dve", lambda e: e.tensor_tensor(out=cen_v[:, 0:w], in0=cen_v[:, 0:w], in1=var_v[:, 0:w], op=ALU.mult),
                 R=[("RA", "var")], W=[("RA", "cen")])
            S.op("act", lambda e, cc=cc: e.activation(out=mA_t[:, cc, 0:w], in_=cen_v[:, 0:w], func=AF.Silu,
                                                      bias=P_("gnb", cc), scale=P_("gng", cc)),
                 R=[("RA", "cen"), "prm"], W=["mA"])

        li = 0
        for pair in range(CC // 2):
            si, slot = wload("win", l, li, KC * 256); li += 1
            for j in range(2):
                b = next_mm_bank()
                mm_group(b, si, slot, KC, 256, j * 128, lambda k: xh_t[:, k, 0:w], w, ["xh"])
                S.op("act", lambda e, b=b, j=j: e.activation(out=sg_v[:, j, 0:w], in_=bank(b, 128, w), func=AF.Sigmoid),
                     R=[("bank", b)], W=[("RA", "sg", j)])
            si, slot = wload("win", l, li, KC * 256); li += 1
            for j in range(2):
                cc = pair * 2 + j
                b = next_mm_bank()
                mm_group(b, si, slot, KC, 256, j * 128, lambda k: xh_t[:, k, 0:w], w, ["xh"])
                par = cc % 2
                cb_ = 6 + par
                S.op("dve", lambda e, cc=cc, par=par: e.tensor_tensor(
                    out=dg31_v[:, par], in0=idb_t[:, :].unsqueeze(1).to_broadcast([128, KW, 128]),
                    in1=P_("wdw", cc * KW, cc * KW + KW).unsqueeze(2).to_broadcast([128, KW, 128]), op=ALU.mult),
                    R=["idb", "prm"], W=[("RA", "dg31", par)])
                if grp:
                    UB = ubuf_v[:, par, 0:2 * (30 + gn)].rearrange("p (s t) -> p s t", t=30 + gn)
                    STc = stc_S[:, :, cc, :]
                    BKc = bank(b, 128, 2 * gn, gc0).rearrange("p (s t) -> p s t", t=gn)
                    SGc = sg_v[:, j, gc0:gc0 + 2 * gn].rearrange("p (s t) -> p s t", t=gn)
                    sres = [("stc", "A"), ("stc", "B")]
                    S.op("act", lambda e, UB=UB, STc=STc: e.activation(out=UB[:, :, 0:30], in_=STc, func=AF.Copy),
                         R=sres, W=[("RA", "ubuf", par)])
                    S.op("dve", lambda e, UB=UB, BKc=BKc, SGc=SGc: e.tensor_tensor(out=UB[:, :, 30:30 + gn], in0=BKc, in1=SGc, op=ALU.mult),
                         R=[("bank", b), ("RA", "sg", j)], W=[("RA", "ubuf", par)])
                    S.op("dve", lambda e, STc=STc, BKc=BKc, SGc=SGc: e.tensor_tensor(
                        out=STc, in0=BKc[:, :, gn - 30:gn], in1=SGc[:, :, gn - 30:gn], op=ALU.mult),
                        R=[("bank", b), ("RA", "sg", j), ("RA", "ubuf", par)], W=sres)
                    cfns = []
                    off = 0
                    for seg in segs:
                        n = seg.n
                        ub = ubuf_v[:, par, off:off + 30 + n]

                        def convfn(e, ub=ub, par=par, cb_=cb_, seg=seg, n=n):
                            ins = None
                            for tp in range(KW):
                                ins = e.matmul(bank(cb_, 128, n, seg.col0), lhsT=dg31_v[:, par, tp, :], rhs=ub[:, tp:tp + n],
                                               start=(tp == 0), stop=(tp == KW - 1))
                            return ins
                        cfns.append(convfn)
                        off += 30 + n
                else:
                    off = 0
                    cfns = []
                    for seg in segs:
                        stv, stres = st_conv(seg, l)
                        n = seg.n
                        assert n >= 30
                        ub = ubuf_v[:, par, off:off + 30 + n]
                        S.op("act", lambda e, ub=ub, stv=stv, cc=cc: e.activation(out=ub[:, 0:30], in_=stv[:, cc, :], func=AF.Copy),
                             R=[stres], W=[("RA", "ubuf", par)])
                        S.op("dve", lambda e, ub=ub, b=b, j=j, seg=seg, n=n: e.tensor_tensor(
                            out=ub[:, 30:30 + n], in0=bank(b, 128, n, seg.col0), in1=sg_v[:, j, seg.col0:seg.col0 + n], op=ALU.mult),
                            R=[("bank", b), ("RA", "sg", j)], W=[("RA", "ubuf", par)])
                        S.op("dve", lambda e, stv=stv, cc=cc, b=b, j=j, seg=seg, n=n: e.tensor_tensor(
                            out=stv[:, cc, :], in0=bank(b, 128, 30, seg.col0 + n - 30),
                            in1=sg_v[:, j, seg.col0 + n - 30:seg.col0 + n], op=ALU.mult),
                            R=[("bank", b), ("RA", "sg", j), ("RA", "ubuf", par)], W=[stres])

                        def convfn(e, ub=ub, par=par, cb_=cb_, seg=seg, n=n):
                            ins = None
                            for tp in range(KW):
                                ins = e.matmul(bank(cb_, 128, n, seg.col0), lhsT=dg31_v[:, par, tp, :], rhs=ub[:, tp:tp + n],
                                               start=(tp == 0), stop=(tp == KW - 1))
                            return ins
                        cfns.append(convfn)
                        off += 30 + n

                def part2(cfns=cfns, par=par, cb_=cb_, cc=cc):
                    for f in cfns:
                        S.op("pe", f, R=[("RA", "ubuf", par), ("RA", "dg31", par)], W=[("bank", cb_)])
                    S.op("act", lambda e: e.activation(out=acc_v[:, par, 0:w], in_=bank(cb_, 128, w),
                                                       func=AF.Identity, bias=P_("bdw", cc)),
                         R=[("bank", cb_), "prm"], W=[("RA", "acc", par)])
                if pend["p2"] is not None:
                    pend["p2"][0]()
                    if pend["cc"] is not None:
                        conv_ln(*pend["cc"])
                    pend["cc"] = pend["p2"][1]
                pend["p2"] = (part2, (cc, par))
        pend["p2"][0]()
        if pend["cc"] is not None:
            conv_ln(*pend["cc"])
        conv_ln(*pend["p2"][1])
        pq = {"f": None}
        for ld in range(3 * H // 2):
            si, slot = wload("win", l, li, KC * 256); li += 1
            for j in range(2):
                i = ld * 2 + j
                b = next_mm_bank()
                mm_group(b, si, slot, KC, 256, j * 128, lambda k: xh_t[:, k, 0:w], w, ["xh"])
                par = i % 2
                cb_ = 6 + par
                S.op("dve", lambda e, i=i, par=par: e.tensor_tensor(
                    out=dg4_v[:, par], in0=idb_t[:, :].unsqueeze(1).to_broadcast([128, SCW, 128]),
                    in1=P_("wsc", i * SCW, i * SCW + SCW).unsqueeze(2).to_broadcast([128, SCW, 128]), op=ALU.mult),
                    R=["idb", "prm"], W=[("RA", "dg4", par)])
                if grp:
                    CBg = cbuf_v[:, par, 0:2 * (3 + gn)].rearrange("p (s t) -> p s t", t=3 + gn)
                    STq = stq_S[:, :, i, :]
                    BKq = bank(b, 128, 2 * gn, gc0).rearrange("p (s t) -> p s t", t=gn)
                    sres = [("stq", "A"), ("stq", "B")]
                    S.op("act", lambda e, CBg=CBg, STq=STq: e.activation(out=CBg[:, :, 0:3], in_=STq, func=AF.Copy),
                         R=sres, W=[("RA", "cbuf", par)])
                    S.op("act", lambda e, CBg=CBg, BKq=BKq: e.activation(out=CBg[:, :, 3:3 + gn], in_=BKq, func=AF.Copy),
                         R=[("bank", b)], W=[("RA", "cbuf", par)])
                    S.op("act", lambda e, STq=STq, BKq=BKq: e.activation(out=STq, in_=BKq[:, :, gn - 3:gn], func=AF.Copy),
                         R=[("bank", b), ("RA", "cbuf", par)], W=sres)
                    c4s = []
                    off = 0
                    for seg in segs:
                        n = seg.n
                        cb = cbuf_v[:, par, off:off + 3 + n]

                        def c4fn(e, cb=cb, par=par, cb_=cb_, seg=seg, n=n):
                            ins = None
                            for tp in range(SCW):
                                ins = e.matmul(bank(cb_, 128, n, seg.col0), lhsT=dg4_v[:, par, tp, :], rhs=cb[:, tp:tp + n],
                                               start=(tp == 0), stop=(tp == SCW - 1))
                            return ins
                        c4s.append(c4fn)
                        off += 3 + n
                else:
                    off = 0
                    c4s = []
                    for seg in segs:
                        stv, stres = st_qkv(seg, l)
                        n = seg.n
                        cb = cbuf_v[:, par, off:off + 3 + n]
                        S.op("act", lambda e, cb=cb, stv=stv, i=i: e.activation(out=cb[:, 0:3], in_=stv[:, i, :], func=AF.Copy),
                             R=[stres], W=[("RA", "cbuf", par)])
                        S.op("act", lambda e, cb=cb, b=b, seg=seg, n=n: e.activation(out=cb[:, 3:3 + n], in_=bank(b, 128, n, seg.col0),
                                                                                 func=AF.Copy),
                             R=[("bank", b)], W=[("RA", "cbuf", par)])
                        S.op("act", lambda e, stv=stv, i=i, b=b, seg=seg, n=n: e.activation(
                            out=stv[:, i, :], in_=bank(b, 128, 3, seg.col0 + n - 3), func=AF.Copy),
                            R=[("bank", b), ("RA", "cbuf", par)], W=[stres])

                        def c4fn(e, cb=cb, par=par, cb_=cb_, seg=seg, n=n):
                            ins = None
                            for tp in range(SCW):
                                ins = e.matmul(bank(cb_, 128, n, seg.col0), lhsT=dg4_v[:, par, tp, :], rhs=cb[:, tp:tp + n],
                                               start=(tp == 0), stop=(tp == SCW - 1))
                            return ins
                        c4s.append(c4fn)
                        off += 3 + n

                def q2(c4s=c4s, par=par, cb_=cb_, i=i):
                    for f in c4s:
                        S.op("pe", f, R=[("RA", "cbuf", par), ("RA", "dg4", par)], W=[("bank", cb_)])
                    S.op("act", lambda e: e.activation(out=qkv_v[:, i, 0:w], in_=bank(cb_, 128, w), func=AF.Silu),
                         R=[("bank", cb_)], W=[QKV])
                if pq["f"] is not None:
                    pq["f"]()
                pq["f"] = q2
        pq["f"]()
        for ld in range(H // 2):
            si, slot = wload("win", l, li, KC * 256); li += 1
            for j in range(2):
                h = ld * 2 + j
                b = next_mm_bank()
                mm_group(b, si, slot, KC, 256, j * 128, lambda k: xh_t[:, k, 0:w], w, ["xh"])
                S.op("act", lambda e, b=b, h=h: e.activation(out=zs_v[:, h, 0:w], in_=bank(b, 128, w), func=AF.Silu),
                     R=[("bank", b)], W=[ZS])
        nch = w // CK
        b5 = bank(5, 64, nch * 2 * H).rearrange("p (c h) -> p c h", c=nch)

        def bafn(e):
            ins = None
            for ci in range(nch):
                for k in range(KC):
                    ins = e.matmul(b5[:, ci, :], lhsT=xh_t[:, k, ci * CK:(ci + 1) * CK], rhs=wba_t[:, k * 2 * H:(k + 1) * 2 * H],
                                   start=(k == 0), stop=(k == KC - 1))
            return ins
        S.op("pe", bafn, R=["xh", "wba"], W=[("bank", 5)])
        L1 = lambda i: l1_t[:, i, 0:nch, :]
        S.op("act", lambda e: e.activation(out=L1(2), in_=b5[:, :, 0:H], func=AF.Sigmoid), R=[("bank", 5)], W=["l1"])
        dtb = P_("dtb")[0:64, :].unsqueeze(1).to_broadcast([64, nch, H])
        alg = P_("alog")[0:64, :]
        S.op("dve", lambda e: e.tensor_tensor(out=L1(0), in0=b5[:, :, H:2 * H], in1=dtb, op=ALU.add),
             R=[("bank", 5), "prm"], W=["l1"])
        S.op("act", lambda e: e.activation(out=L1(3), in_=L1(0), func=AF.Abs), R=[], W=["l1"])
        S.op("act", lambda e: e.activation(out=L1(3), in_=L1(3), func=AF.Exp, scale=-1.0), R=[], W=["l1"])
        S.op("act", lambda e: e.activation(out=L1(3), in_=L1(3), func=AF.Ln, bias=cst_t[0:64, 705:706]), R=[], W=["l1"])
        S.op("dve", lambda e: e.scalar_tensor_tensor(out=L1(0), in0=L1(0), scalar=0.0, in1=L1(3), op0=ALU.max, op1=ALU.add),
             R=[], W=["l1"])
        S.op("act", lambda e: e.activation(out=l1_t[:, 4, 0, :], in_=alg, func=AF.Exp), R=["prm"], W=["l1"])
        S.op("dve", lambda e: e.scalar_tensor_tensor(out=L1(1), in0=L1(0), scalar=-1.0,
                                                     in1=l1_t[:, 4, 0, :].unsqueeze(1).to_broadcast([64, nch, H]),
                                                     op0=ALU.mult, op1=ALU.mult), R=[], W=["l1"])
        S.fence("RA")
        l2norm_tile(w)
        for seg in segs:
            if seg.seq != "P":
                b = seg.sidx - 1
                S.op("sp", lambda e, b=b: e.dma_start(out=S_S[:, :, :].rearrange("p h d -> p (h d)"), in_=sg_d[l, b]),
                     W=[("S", "S")], dma="SS")
            for cj in range(seg.n // CK):
                c0 = seg.col0 + cj * CK
                gdn_chunk(l, seg, c0 // CK, c0, w)
            if seg.seq != "P":
                S.op("sp", lambda e, seg=seg: e.dma_start(out=og_d[l, seg.sidx], in_=S_S[:, :, :].rearrange("p h d -> p (h d)")),
                     R=[("S", "S")], dma="out_SS")
            elif seg.last:
                S.op("sp", lambda e: e.dma_start(out=og_d[l, 0], in_=S_P[:, l].rearrange("p h d -> p (h d)")),
                     R=[("S", "P", l)], dma="out")
        for seg in segs:
            if seg.last:
                stv, stres = st_conv(seg, l)
                S.op("sp", lambda e, stv=stv, seg=seg: e.dma_start(out=oc_d[l, seg.sidx], in_=stv.rearrange("p c j -> p (c j)")),
                     R=[stres], dma="out_stc" + seg.seq)
                stv, stres = st_qkv(seg, l)
                S.op("sp", lambda e, stv=stv, seg=seg: e.dma_start(out=oq_d[l, seg.sidx], in_=stv.rearrange("p c j -> p (c j)")),
                     R=[stres], dma="out_stq" + seg.seq)
        S.fence("RA")
        pend2 = {"f": None}
        mmrot["banks"] = [1, 2, 3, 4, 5, 6, 7]
        for ld in range(c.NLOUT):
            si, slot = wload("wout", l, ld, KM * 256)
            for j in range(2):
                n = ld * 2 + j
                b = next_mm_bank()
                mm_group(b, si, slot, KM, 256, j * 128, lambda k: m_chunk(k)[:, 0:w], w, ["mA", "xh"])
                nxt = out_block_epilogue(b, w, y_v, ("RA", "y"), n, n == 0, n == KC - 1)
                if pend2["f"] is not None:
                    pend2["f"]()
                pend2["f"] = nxt
        pend2["f"]()
        pend2["f"] = None
        residual_epilogue(w, y_v, ("RA", "y"), "g2")
        rmsnorm_to_xh(w, "g3")
        S.fence("RA")
        S.fence("RB")
        for half in range(2):
            for pr in range(FH // 2):
                for which in range(2):
                    ld = (half * (FH // 2) + pr) * 2 + which
                    si, slot = wload("wup", l, ld, KC * 256)
                    for j in range(2):
                        hc = half * FH + pr * 2 + j
                        ch = hc + which * FCH
                        b = next_mm_bank()
                        mm_group(b, si, slot, KC, 256, j * 128, lambda k: xh_t[:, k, 0:w], w, ["xh"])
                        par = j
                        if grp:
                            Fg = fbuf_v[:, par, 0:2 * (2 + gn)].rearrange("p (s t) -> p s t", t=2 + gn)
                            STf = stf_S[:, :, ch, :]
                            BKf = bank(b, 128, 2 * gn, gc0).rearrange("p (s t) -> p s t", t=gn)
                            AVg = accf_v[:, par, gc0:gc0 + 2 * gn].rearrange("p (s t) -> p s t", t=gn)
                            sres = [("stf", "A"), ("stf", "B")]
                            S.op("act", lambda e, Fg=Fg, STf=STf: e.activation(out=Fg[:, :, 0:2], in_=STf, func=AF.Copy),
                                 R=sres, W=[("RA", "fbuf", par)])
                            S.op("act", lambda e, Fg=Fg, BKf=BKf: e.activation(out=Fg[:, :, 2:2 + gn], in_=BKf, func=AF.Copy),
                                 R=[("bank", b)], W=[("RA", "fbuf", par)])
                            S.op("dve", lambda e, Fg=Fg, AVg=AVg, ch=ch: e.tensor_scalar(
                                out=AVg, in0=Fg[:, :, 0:gn], scalar1=P_("wf", ch * FCW), scalar2=P_("bf", ch),
                                op0=ALU.mult, op1=ALU.add), R=[("RA", "fbuf", par), "prm"], W=[("RA", "accf", par)])
                            for tp in range(1, FCW):
                                S.op("dve", lambda e, Fg=Fg, AVg=AVg, ch=ch, tp=tp: e.scalar_tensor_tensor(
                                    out=AVg, in0=Fg[:, :, tp:tp + gn], scalar=P_("wf", ch * FCW + tp), in1=AVg,
                                    op0=ALU.mult, op1=ALU.add), R=[("RA", "fbuf", par), "prm"], W=[("RA", "accf", par)])
                            S.op("act", lambda e, Fg=Fg, STf=STf: e.activation(out=STf, in_=Fg[:, :, gn:gn + 2], func=AF.Copy),
                                 R=[("RA", "fbuf", par)], W=sres)
                        else:
                            off = 0
                            for seg in segs:
                                stv, stres = st_ffn(seg, l)
                                n = seg.n
                                fb = fbuf_v[:, par, off:off + 2 + n]
                                S.op("act", lambda e, fb=fb, stv=stv, ch=ch: e.activation(out=fb[:, 0:2], in_=stv[:, ch, :], func=AF.Copy),
                                     R=[stres], W=[("RA", "fbuf", par)])
                                S.op("act", lambda e, fb=fb, b=b, seg=seg, n=n: e.activation(out=fb[:, 2:2 + n], in_=bank(b, 128, n, seg.col0),
                                                                                         func=AF.Copy),
                                     R=[("bank", b)], W=[("RA", "fbuf", par)])
                                av = accf_v[:, par, seg.col0:seg.col0 + n]
                                S.op("dve", lambda e, fb=fb, av=av, ch=ch, n=n: e.tensor_scalar(
                                    out=av, in0=fb[:, 0:n], scalar1=P_("wf", ch * FCW), scalar2=P_("bf", ch),
                                    op0=ALU.mult, op1=ALU.add), R=[("RA", "fbuf", par), "prm"], W=[("RA", "accf", par)])
                                for tp in range(1, FCW):
                                    S.op("dve", lambda e, fb=fb, av=av, ch=ch, n=n, tp=tp: e.scalar_tensor_tensor(
                                        out=av, in0=fb[:, tp:tp + n], scalar=P_("wf", ch * FCW + tp), in1=av,
                                        op0=ALU.mult, op1=ALU.add), R=[("RA", "fbuf", par), "prm"], W=[("RA", "accf", par)])
                                S.op("act", lambda e, fb=fb, stv=stv, ch=ch, n=n: e.activation(out=stv[:, ch, :], in_=fb[:, n:n + 2], func=AF.Copy),
                                     R=[("RA", "fbuf", par)], W=[stres])
                                off += 2 + n
                        if which == 0:
                            S.op("act", lambda e, j=j, par=par: e.activation(out=sgate_v[:, j, 0:w], in_=accf_v[:, par, 0:w], func=AF.Silu),
                                 R=[("RA", "accf", par)], W=[("RA", "sgate", j)])
                        else:
                            S.op("dve", lambda e, j=j, par=par, ai=hc - half * FH: e.tensor_tensor(
                                out=a_v[:, ai, 0:w], in0=accf_v[:, par, 0:w], in1=sgate_v[:, j, 0:w], op=ALU.mult),
                                R=[("RA", "accf", par), ("RA", "sgate", j)], W=[("RA", "a")])
            for n in range(KC):
                si, slot = wload("wdn", l, half * KC + n, FH * 128)
                b = next_mm_bank()
                mm_group(b, si, slot, FH, 128, 0, lambda k: a_v[:, k, 0:w], w, [("RA", "a")])
                if half == 0:
                    S.op("act", lambda e, b=b, n=n: e.activation(out=y2_v[:, n, 0:w], in_=bank(b, 128, w), func=AF.Copy),
                         R=[("bank", b)], W=[("RB", "y2", n)])
                else:
                    S.op("dve", lambda e, b=b, n=n: e.tensor_tensor(out=y2_v[:, n, 0:w], in0=y2_v[:, n, 0:w], in1=bank(b, 128, w),
                                                                  op=ALU.add), R=[("bank", b)], W=[("RB", "y2", n)])
                    par = n % 2
                    S.op("act", lambda e, n=n, par=par: e.activation(out=sqn_t[:, par, 0:w], in_=y2_v[:, n, 0:w], func=AF.Square),
                         R=[("RB", "y2", n)], W=[("sqn", par)])
                    nxt = (lambda n=n, par=par: S.op("pe", lambda e: e.matmul(bank(0, 128, w), lhsT=onesD, rhs=sqn_t[:, par, 0:w],
                                                                              start=(n == 0), stop=(n == KC - 1)),
                                                     R=[("sqn", par), "cst"], W=[("bank", 0)]))
                    if pend2["f"] is not None:
                        pend2["f"]()
                    pend2["f"] = nxt
        pend2["f"]()
        pend2["f"] = None
        residual_epilogue(w, y2_v, ("RB", "y2"), "g4")
        for seg in segs:
            if seg.last:
                stv, stres = st_ffn(seg, l)
                S.op("sp", lambda e, stv=stv, seg=seg: e.dma_start(out=of_d[l, seg.sidx], in_=stv.rearrange("p c j -> p (c j)")),
                     R=[stres], dma="out_stf" + seg.seq)

    for ti, tile in enumerate(tiles):
        cur["ti"] = ti
        w = tile["w"]
        seg0 = tile["segs"][0]
        xres = [("x", k) for k in range(KC)]
        has_p = seg0.seq == "P"
        if has_p:
            S.op("sp", lambda e, seg0=seg0: e.dma_start(out=x_t[:, :, 0:seg0.n], in_=xp_d[:, :, seg0.tok0:seg0.tok0 + seg0.n]),
                 W=xres, dma="xin")
        else:
            seg0 = Seg("P", 0, 0, False, False, 0, 0)
        has_s = tile["segs"][-1].seq != "P"
        if has_s:
            S.op("sp", lambda e, seg0=seg0, w=w: e.dma_start(out=x_t[:, :, seg0.n:w], in_=xs_d[:, :, 0:w - seg0.n]),
                 W=xres, dma="xin")
        for l in range(L):
            layer(l, tile)
        if has_p:
            S.op("sp", lambda e, seg0=seg0: e.dma_start(out=yp_d[:, :, seg0.tok0:seg0.tok0 + seg0.n], in_=x_t[:, :, 0:seg0.n]),
                 R=xres, dma="out_x")
        if has_s:
            S.op("sp", lambda e, seg0=seg0, w=w: e.dma_start(out=ys_d[:, :, 0:w - seg0.n], in_=x_t[:, :, seg0.n:w]),
                 R=xres, dma="out_x")
    S.wait_all("sp", [k for k in S.cnt if k[0] == "dma" and k[1].startswith("out")])
    S.emit(nc)
    es.close()
    return nc, S


def fm(a):
    sh = a.shape
    nt, nf = sh[-2], sh[-1]
    b = a.reshape(sh[:-2] + (nt, nf // 128, 128))
    nd = b.ndim
    perm = tuple(range(nd - 3)) + (nd - 1, nd - 2, nd - 3)
    return np.ascontiguousarray(b.transpose(perm))


def unfm(a):
    nd = a.ndim
    perm = tuple(range(nd - 3)) + (nd - 1, nd - 2, nd - 3)
    b = a.transpose(perm)
    return np.ascontiguousarray(b.reshape(b.shape[:-2] + (b.shape[-2] * b.shape[-1],)))


def wblk(W, cols):
    K = W.shape[0]
    sub = W[:, cols].reshape(K // 128, 128, len(cols)).transpose(1, 0, 2)
    return np.ascontiguousarray(sub.reshape(128, -1))


def make_consts():
    cst = np.zeros((128, 708), np.float32)
    cst[:, 0:128] = np.eye(128, dtype=np.float32)
    cst[:, 128:256] = 1.0
    cst[:, 384:512] = 1.0 / 128.0
    k = np.arange(64)[:, None]
    i = np.arange(64)[None, :]
    cst[0:64, 512:576] = (k <= i)
    cst[0:64, 576:640] = np.where(i >= k, 0.0, -1e30)
    cst[0:64, 640:704] = (i > k)
    cst[:, 704] = EPS
    cst[:, 705] = 1.0
    return cst


def prep_shared(c, inp):
    L, H, CC, KC, FCH, FH = c.L, c.H, c.CC, c.KC, c.FCH, c.FH
    cst = make_consts()
    cst[:, 256:384] = 1.0 / c.D
    CW = CC * 128
    GW = H * 128
    c1 = 2 * CW
    c2 = c1 + 3 * GW
    c3 = c2 + GW
    win = np.empty((L, c.NLIN, 128, KC * 256), np.float32)
    wba = np.empty((L, 128, KC * 2 * H), np.float32)
    wout = np.empty((L, c.NLOUT, 128, c.KM * 256), np.float32)
    wup = np.empty((L, c.NLUP, 128, KC * 256), np.float32)
    wdn = np.empty((L, c.NLDN, 128, FH * 128), np.float32)
    prm = np.zeros((L, 128, c.NPL), np.float32)
    ar = np.arange
    for l in range(L):
        Wi = inp["w_in"][l]
        li = 0
        for pair in range(CC // 2):
            win[l, li] = wblk(Wi, CW + pair * 256 + ar(256)); li += 1
            win[l, li] = wblk(Wi, pair * 256 + ar(256)); li += 1
        for ld in range(3 * H // 2):
            win[l, li] = wblk(Wi, c1 + ld * 256 + ar(256)); li += 1
        for ld in range(H // 2):
            win[l, li] = wblk(Wi, c2 + ld * 256 + ar(256)); li += 1
        wba[l] = wblk(Wi, c3 + ar(2 * H))
        Wo = inp["w_out"][l]
        for ld in range(c.NLOUT):
            wout[l, ld] = wblk(Wo, ld * 256 + ar(256))
        Wu = inp["w_up"][l]
        for half in range(2):
            for pr in range(FH // 2):
                for which in range(2):
                    ld = (half * (FH // 2) + pr) * 2 + which
                    col0 = which * c.FFN + (half * FH + pr * 2) * 128
                    wup[l, ld] = wblk(Wu, col0 + ar(256))
        Wd = inp["w_down"][l]
        for half in range(2):
            for n in range(KC):
                wdn[l, half * KC + n] = wblk(Wd[half * FH * 128:(half + 1) * FH * 128], n * 128 + ar(128))

        def put(name, arr):
            o, n = c.P[name]
            prm[l, :, o:o + n] = arr.reshape(128, n)

        def colvec(v):
            return v.reshape(-1, 128).T

        put("g1", colvec(inp["g_pre_mix"][l]))
        put("g2", colvec(inp["g_post_mix"][l]))
        put("g3", colvec(inp["g_pre_ffn"][l]))
        put("g4", colvec(inp["g_post_ffn"][l]))
        put("wdw", inp["w_dw"][l].reshape(KW, CC, 128).transpose(2, 1, 0))
        put("bdw", colvec(inp["b_dw"][l]))
        put("gng", colvec(inp["gn_g"][l]))
        put("gnb", colvec(inp["gn_b"][l]))
        put("wsc", inp["w_sc"][l].reshape(SCW, 3 * H, 128).transpose(2, 1, 0))
        put("on", inp["onorm_g"][l].reshape(128, 1))
        put("wf", inp["w_ffn_dw"][l].reshape(FCW, 2 * FCH, 128).transpose(2, 1, 0))
        put("bf", colvec(inp["b_ffn_dw"][l]))
        put("alog", np.broadcast_to(inp["a_log"][l][None, :], (128, H)))
        put("dtb", np.broadcast_to(inp["dt_bias"][l][None, :], (128, H)))
    return dict(prm=prm, cst=cst, win=win, wba=wba, wout=wout, wup=wup, wdn=wdn)


def prep_core(c, inp, core):
    L = c.L
    d = {}
    d["xp"] = fm(inp["x_prompt"][core])
    xs = inp["x_sample"][2 * core:2 * core + 2]
    d["xs"] = fm(xs.reshape(2 * c.DSEQ, c.D))
    sl = slice(2 * core, 2 * core + 2)
    d["st_conv"] = fm(inp["state_conv"][:, sl]).reshape(L, 2, 128, -1)
    d["st_qkv"] = fm(inp["state_qkv_conv"][:, sl]).reshape(L, 2, 128, -1)
    d["st_ffn"] = fm(inp["state_ffn_conv"][:, sl]).reshape(L, 2, 128, -1)
    sg = inp["state_gdn"][:, sl]
    d["st_gdn"] = np.ascontiguousarray(sg.transpose(0, 1, 3, 2, 4)).reshape(L, 2, 128, -1)
    return d


def run(c, inp, ncores):
    nc, S = build(c)
    shared = prep_shared(c, inp)
    in_maps = []
    for core in range(ncores):
        d = dict(shared)
        d.update(prep_core(c, inp, core))
        in_maps.append(d)
    res = run_bass_kernel_spmd(nc, in_maps, core_ids=list(range(ncores)))
    L, H, CC, FCH = c.L, c.H, c.CC, c.FCH
    B = ncores
    yp = np.empty((B, c.SEQ, c.D), np.float32)
    ys = np.empty((2 * B, c.DSEQ, c.D), np.float32)
    conv = np.empty((L, 3 * B, KW - 1, CC * 128), np.float32)
    qkv = np.empty((L, 3 * B, SCW - 1, 3 * H * 128), np.float32)
    gdn = np.empty((L, 3 * B, H, 128, 128), np.float32)
    ffn = np.empty((L, 3 * B, FCW - 1, 2 * FCH * 128), np.float32)
    for core in range(ncores):
        r = res.results[core]
        yp[core] = unfm(r["yp"])
        ys[2 * core:2 * core + 2] = unfm(r["ys"]).reshape(2, c.DSEQ, c.D)
        for s, bi in ((0, core), (1, B + 2 * core), (2, B + 2 * core + 1)):
            conv[:, bi] = unfm(r["o_conv"][:, s].reshape(L, 128, CC, KW - 1))
            qkv[:, bi] = unfm(r["o_qkv"][:, s].reshape(L, 128, 3 * H, SCW - 1))
            ffn[:, bi] = unfm(r["o_ffn"][:, s].reshape(L, 128, 2 * FCH, FCW - 1))
            gdn[:, bi] = r["o_gdn"][:, s].reshape(L, 128, H, 128).transpose(0, 2, 1, 3)
    return (yp, ys, conv[:, :B], qkv[:, :B], gdn[:, :B], ffn[:, :B],
            conv[:, B:], qkv[:, B:], gdn[:, B:], ffn[:, B:])


def kernel(**inputs):
    inp = {k: np.asarray(v) for k, v in inputs.items()}
    c = Cfg()
    return run(c, inp, 8)
```
